# Optimizing a Trainium2 kernel written in Bass

```python
import jax, jax.numpy as jnp
from jax import lax
import numpy as np

D_MODEL = 1024
BATCH = 8
SEQ = 2048
DEPTH = 1

GRID_W = 64
CTX_LEN = 256
N_MOD = 6
HG_HEADS = 4
HG_DK = 128
HG_DV = 128
GLA_HEADS = 4
GLA_DK = 128
GLA_DV = 128
GLA_RANK = 16
GLA_GATE_NORM = 16.0
HG_W = HG_HEADS * HG_DK
HG_VW = HG_HEADS * HG_DV
GLA_KW = GLA_HEADS * GLA_DK
GLA_VW = GLA_HEADS * GLA_DV
D_FF = ((8 * D_MODEL // 3 + 255) // 256) * 256
EPS = 1e-6
IN_SPLITS = (HG_W, HG_VW, HG_W, HG_W, HG_VW, GLA_KW, GLA_KW, GLA_VW, GLA_VW, GLA_RANK, GLA_RANK, D_MODEL, D_MODEL)
IN_WIDTH = sum(IN_SPLITS)
IN_OFFSETS = tuple(int(v) for v in np.cumsum(IN_SPLITS)[:-1])

kernel_name = "hybrid_hgrn2_gla_prefix_dit_block"


def rms_norm(a, w):
    af = a.astype(jnp.float32)
    return (af * lax.rsqrt(jnp.mean(af * af, axis=-1, keepdims=True) + EPS)).astype(a.dtype) * w


def modulate(h, shift_c, scale_c, shift_x, scale_x):
    hc, hx = h[:, :CTX_LEN], h[:, CTX_LEN:]
    return jnp.concatenate([hc * (1 + scale_c) + shift_c,
                            hx * (1 + scale_x[:, None]) + shift_x[:, None]], axis=1)


def apply_gate(h, gate_c, gate_x):
    return jnp.concatenate([h[:, :CTX_LEN] * gate_c, h[:, CTX_LEN:] * gate_x[:, None]], axis=1)


def to_heads(a, n_heads):
    b, t, w = a.shape
    return a.reshape(b, t, n_heads, w // n_heads).transpose(0, 2, 1, 3)


def merge_heads(a):
    b, h, t, d = a.shape
    return a.transpose(0, 2, 1, 3).reshape(b, t, h * d)


def segment_reverse(a):
    return jnp.concatenate([jnp.flip(a[:, :, :CTX_LEN], axis=2), jnp.flip(a[:, :, CTX_LEN:], axis=2)], axis=2)


def chunk_scan(q, k, v, log_f, n_chunks):
    b, h, t, dk = q.shape
    dv = v.shape[-1]

    def chunks(a):
        return jnp.moveaxis(a.reshape(b, h, n_chunks, GRID_W, a.shape[-1]), 2, 0)

    causal = jnp.tril(jnp.ones((GRID_W, GRID_W), dtype=bool))[:, :, None]

    def step(s, blk):
        qc, kc, vc, gc = blk
        cum = jnp.cumsum(gc.astype(jnp.float32), axis=2)
        pair = jnp.exp(jnp.where(causal, cum[:, :, :, None, :] - cum[:, :, None, :, :], -jnp.inf))
        scores = jnp.einsum('bhtk,bhtsk,bhsk->bhts', qc, pair, kc)
        o = (jnp.einsum('bhts,bhsv->bhtv', scores, vc)
             + jnp.einsum('bhtk,bhkv->bhtv', qc * jnp.exp(cum), s))
        tail = jnp.exp(cum[:, :, -1:, :] - cum)
        s_new = (s * jnp.exp(cum[:, :, -1, :])[..., None]
                 + jnp.einsum('bhsk,bhsv->bhkv', kc * tail, vc))
        return s_new, o

    s0 = jnp.zeros((b, h, dk, dv), jnp.float32)
    _, o = lax.scan(step, s0, (chunks(q), chunks(k), chunks(v), chunks(log_f)))
    return jnp.moveaxis(o, 0, 2).reshape(b, h, t, dv).astype(v.dtype)


def bidirectional_scan(q, k_fw, k_bw, v, lf_fw, lf_bw, n_chunks):
    o_fw = chunk_scan(q, k_fw, v, lf_fw, n_chunks)
    o_bw = chunk_scan(segment_reverse(q), segment_reverse(k_bw), segment_reverse(v),
                      segment_reverse(lf_bw), n_chunks)
    return o_fw + segment_reverse(o_bw)


def hybrid_mixer(h, w_in, lb, hg_onorm, gla_w_gk, gla_b_gk, gla_onorm, w_br_hg, w_br_gla, w_out, n_chunks):
    (hq, hi, hf_fw, hf_bw, hg_gate, gq, gk, gv, g_gate,
     lr_fw, lr_bw, gate_hg, gate_gla) = jnp.split(h @ w_in, IN_OFFSETS, axis=-1)

    def hg_forget(raw, lb_dir):
        f = lb_dir + (1 - lb_dir) * jax.nn.sigmoid(raw.astype(jnp.float32))
        return to_heads(1 - f, HG_HEADS), to_heads(jnp.log(f), HG_HEADS)

    q = to_heads(jax.nn.silu(hq), HG_HEADS)
    i = to_heads(hi, HG_HEADS)
    k_fw, lf_fw = hg_forget(hf_fw, lb[0])
    k_bw, lf_bw = hg_forget(hf_bw, lb[1])
    o = bidirectional_scan(q, k_fw, k_bw, i, lf_fw, lf_bw, n_chunks)
    o_hg = merge_heads(rms_norm(o, hg_onorm)) * jax.nn.silu(hg_gate)

    def gla_gate_log(lr, w, bias):
        return to_heads(jax.nn.log_sigmoid((lr @ w + bias).astype(jnp.float32)) / GLA_GATE_NORM, GLA_HEADS)

    q = to_heads(gq, GLA_HEADS) * GLA_DK ** -0.5
    k = to_heads(gk, GLA_HEADS)
    v = to_heads(gv, GLA_HEADS)
    lf_fw = gla_gate_log(lr_fw, gla_w_gk[0], gla_b_gk[0])
    lf_bw = gla_gate_log(lr_bw, gla_w_gk[1], gla_b_gk[1])
    o = bidirectional_scan(q, k, k, v, lf_fw, lf_bw, n_chunks)
    o_gla = merge_heads(rms_norm(o, gla_onorm)) * jax.nn.silu(g_gate)

    merged = (jax.nn.sigmoid(gate_hg) * (o_hg @ w_br_hg)
              + jax.nn.sigmoid(gate_gla) * (o_gla @ w_br_gla))
    return merged @ w_out


def swiglu(h, w_gate, w_up, w_down):
    return (jax.nn.silu(h @ w_gate) * (h @ w_up)) @ w_down


def setup_inputs(seed: int = 0) -> dict:
    key = jax.random.key(seed)
    ks = jax.random.split(key, 24)

    def nrm(k, shape, scale):
        return jax.random.normal(k, shape, jnp.float32) * scale

    def gain(k, shape):
        return 1.0 + nrm(k, shape, 0.05)

    return {
        "x": nrm(ks[0], (BATCH, SEQ, D_MODEL), 1.0),
        "c": nrm(ks[1], (BATCH, D_MODEL), 1.0),
        "ctx": nrm(ks[2], (BATCH, CTX_LEN, D_MODEL), 1.0),
        "c_ctx": nrm(ks[3], (D_MODEL,), 1.0),
        "w_mod": nrm(ks[4], (DEPTH, D_MODEL, N_MOD * D_MODEL), 0.5 * D_MODEL ** -0.5),
        "b_mod": nrm(ks[5], (DEPTH, N_MOD * D_MODEL), 0.01),
        "norm_pre1": gain(ks[6], (DEPTH, D_MODEL)),
        "norm_post1": gain(ks[7], (DEPTH, D_MODEL)),
        "norm_pre2": gain(ks[8], (DEPTH, D_MODEL)),
        "norm_post2": gain(ks[9], (DEPTH, D_MODEL)),
        "w_in": nrm(ks[10], (DEPTH, D_MODEL, IN_WIDTH), D_MODEL ** -0.5),
        "hg_lb": nrm(ks[11], (DEPTH + 1, 2, HG_W), 1.0),
        "hg_onorm": gain(ks[12], (DEPTH, HG_DV)),
        "gla_w_gk": nrm(ks[13], (DEPTH, 2, GLA_RANK, GLA_KW), GLA_RANK ** -0.5),
        "gla_b_gk": nrm(ks[14], (DEPTH, 2, GLA_KW), 0.1),
        "gla_onorm": gain(ks[15], (DEPTH, GLA_DV)),
        "w_br_hg": nrm(ks[16], (DEPTH, HG_VW, D_MODEL), HG_VW ** -0.5),
        "w_br_gla": nrm(ks[17], (DEPTH, GLA_VW, D_MODEL), GLA_VW ** -0.5),
        "w_out": nrm(ks[18], (DEPTH, D_MODEL, D_MODEL), D_MODEL ** -0.5),
        "w_ff_gate": nrm(ks[19], (DEPTH, D_MODEL, D_FF), D_MODEL ** -0.5),
        "w_ff_up": nrm(ks[20], (DEPTH, D_MODEL, D_FF), D_MODEL ** -0.5),
        "w_ff_down": nrm(ks[21], (DEPTH, D_FF, D_MODEL), D_FF ** -0.5),
    }


def reference(x, c, ctx, c_ctx, w_mod, b_mod, norm_pre1, norm_post1, norm_pre2, norm_post2, w_in, hg_lb,
              hg_onorm, gla_w_gk, gla_b_gk, gla_onorm, w_br_hg, w_br_gla, w_out, w_ff_gate, w_ff_up, w_ff_down):
    rows = x.shape[1] // GRID_W
    n_chunks = CTX_LEN // GRID_W + rows
    z = jnp.concatenate([ctx, x], axis=1)
    lb_all = jnp.cumsum(jax.nn.softmax(hg_lb.astype(jnp.float32), axis=0), axis=0)
    for l in range(DEPTH):
        m_c = jnp.split(jax.nn.silu(c_ctx) @ w_mod[l] + b_mod[l], N_MOD, axis=-1)
        m_x = jnp.split(jax.nn.silu(c) @ w_mod[l] + b_mod[l], N_MOD, axis=-1)
        h = modulate(rms_norm(z, norm_pre1[l]), m_c[0], m_c[1], m_x[0], m_x[1])
        y = hybrid_mixer(h, w_in[l], lb_all[l], hg_onorm[l], gla_w_gk[l], gla_b_gk[l], gla_onorm[l],
                         w_br_hg[l], w_br_gla[l], w_out[l], n_chunks)
        z = z + apply_gate(rms_norm(y, norm_post1[l]), m_c[2], m_x[2])
        h = modulate(rms_norm(z, norm_pre2[l]), m_c[3], m_c[4], m_x[3], m_x[4])
        y = swiglu(h, w_ff_gate[l], w_ff_up[l], w_ff_down[l])
        z = z + apply_gate(rms_norm(y, norm_post2[l]), m_c[5], m_x[5])
    return z[:, CTX_LEN:]
```

```python
import numpy as np
from contextlib import ExitStack
import concourse.bass as bass
import concourse.mybir as mybir
from concourse.bass_utils import run_bass_kernel_spmd

F32 = mybir.dt.float32
BF16 = mybir.dt.bfloat16
AF = mybir.ActivationFunctionType
ALU = mybir.AluOpType

D = 1024
TC = 256
TL = 2048
T = TC + TL
NT = T // 128
NCH = T // 64
DFF = 2816
FC = DFF // 128
EPS = 1e-6
NCORES = 8
GROUPS = [(0, 256), (256, 512), (768, 512), (1280, 512), (1792, 512)]

ENGS = ["pe", "act", "dve", "pool", "sp"]
NRING = 40
SBUF_BASE = 16576
SBUF_CAP = 229376 - 128


class Prog:
    def __init__(self, nc, stack):
        self.nc = nc
        self.ops = {e: [] for e in ENGS}
        self.semobj = {}
        for e in ENGS:
            self.semobj["s_" + e] = stack.enter_context(nc.semaphore("s_" + e))
        for i in range(NRING):
            self.semobj["r%d" % i] = stack.enter_context(nc.semaphore("r%d" % i))
        self.cnt = {e: 0 for e in ENGS}
        self.seen = {e: {} for e in ENGS}
        self.ring_total = [0] * NRING
        self.ring_next = 0
        self.reg = {}
        self.fence_deps = []

    def _waits(self, eng, deps):
        best = {}
        for d in deps:
            if d is None:
                continue
            key, val = d
            if eng == "pe" and key == "s_pe":
                continue
            if val > best.get(key, 0):
                best[key] = val
        waits = []
        for key, val in best.items():
            if self.seen[eng].get(key, 0) >= val:
                continue
            self.seen[eng][key] = val
            waits.append((key, val))
        return waits

    def _deps(self, reads, writes):
        deps = list(self.fence_deps)
        for r in reads:
            st = self.reg.get(r)
            if st is not None:
                deps.append(st[0])
        for w in writes:
            st = self.reg.get(w)
            if st is not None:
                deps.append(st[0])
                deps += list(st[1].items())
        return deps

    def _record(self, h, reads, writes):
        for r in reads:
            st = self.reg.get(r)
            if st is None:
                st = self.reg[r] = [None, {}]
            if h[1] > st[1].get(h[0], 0):
                st[1][h[0]] = h[1]
        for w in writes:
            self.reg[w] = [h, {}]

    def op(self, eng, fn, reads=(), writes=(), deps=()):
        rec = _Recorder()
        fn(rec)
        specs = rec.specs
        assert specs
        fn = (lambda e, specs=specs: _play(e, specs))
        writes = list(writes) + [r for r in reads if isinstance(r, tuple) and r[0] == "ps"]
        d = self._deps(reads, writes) + list(deps)
        waits = self._waits(eng, d)
        self.cnt[eng] += 1
        h = ("s_" + eng, self.cnt[eng])
        self.ops[eng].append((waits, fn, ("s_" + eng, 1)))
        self._record(h, reads, writes)
        return h

    def dma(self, queue, pairs, reads=(), writes=(), deps=()):
        slot = self.ring_next
        self.ring_next = (self.ring_next + 1) % NRING
        key = "r%d" % slot
        d = self._deps(reads, writes) + list(deps)
        if self.ring_total[slot] > 0:
            d.append((key, self.ring_total[slot]))
        waits = self._waits(queue, d)
        first = True
        for (o, i) in pairs:
            self.ring_total[slot] += 16
            self.ops[queue].append((waits if first else [], (lambda e, o=o, i=i: e.dma_start(out=o, in_=i)), (key, 16)))
            first = False
        h = (key, self.ring_total[slot])
        self._record(h, reads, writes)
        return h

    def fence(self):
        self.fence_deps = [("s_" + e, self.cnt[e]) for e in ENGS if self.cnt[e] > 0] + \
                          [("r%d" % i, self.ring_total[i]) for i in range(NRING) if self.ring_total[i] > 0]
        self.reg = {}

    def finish(self):
        self.fence()
        self.ops["sp"].append((self._waits("sp", self.fence_deps), None, None))
        nc = self.nc

        def replay(e, name):
            for waits, fn, inc in self.ops[name]:
                for key, val in waits:
                    e.wait_ge(self.semobj[key], val)
                if fn is None:
                    continue
                ins = fn(e)
                if inc is not None:
                    ins.then_inc(self.semobj[inc[0]], inc[1])

        with nc.Block() as block:
            block.tensor(lambda e: replay(e, "pe"))
            block.scalar(lambda e: replay(e, "act"))
            block.vector(lambda e: replay(e, "dve"))
            block.gpsimd(lambda e: replay(e, "pool"))
            block.sync(lambda e: replay(e, "sp"))


class _Recorder:
    def __init__(self):
        self.specs = []

    def __getattr__(self, name):
        def f(*a, **k):
            self.specs.append((name, a, k))
            return None
        return f


def _play(e, specs):
    ins = None
    for (name, a, k) in specs:
        ins = getattr(e, name)(*a, **k)
    return ins


class Arena:
    def __init__(self, nc, cap):
        self.nc = nc
        self.lo = SBUF_BASE
        self.hi = cap
        self.n = 0

    @staticmethod
    def _size(shape, dt):
        n = 1
        for s in shape[1:]:
            n *= s
        return n * (4 if dt == F32 else 2)

    def alloc(self, name, shape, dt):
        off = (self.lo + 31) // 32 * 32
        sz = self._size(shape, dt)
        assert off + sz <= self.hi, "SBUF overflow at %s: %d + %d > %d" % (name, off, sz, self.hi)
        self.lo = off + sz
        self.n += 1
        return self.nc.alloc_sbuf_tensor_at("%s_%d" % (name, self.n), shape, dt, offset=off)

    def alloc_top(self, name, shape, dt):
        sz = self._size(shape, dt)
        off = (self.hi - sz) // 32 * 32
        assert off >= self.lo, "SBUF overflow (top) at %s" % name
        self.hi = off
        self.n += 1
        return self.nc.alloc_sbuf_tensor_at("%s_%d" % (name, self.n), shape, dt, offset=off)


def build_program(debug=None):
    nc = bass.Bass("TRN2", target_bir_lowering=False)

    def din(name, shape, dt=F32):
        return nc.dram_tensor(name, list(shape), dt, kind="ExternalInput").ap()

    x_d = din("x", [TL, D])
    ctx_d = din("ctxx", [TC, D])
    cx_d = din("cx", [128, 8])
    cctx_d = din("cctx", [128, 8])
    wmod_d = din("w_mod", [D, 6 * D])
    bmodP_d = din("bmodP", [128, 48])
    bmodR_d = din("bmodR", [1, 6 * D])
    npre1_d = din("npre1P", [128, 8])
    npre2_d = din("npre2P", [128, 8])
    npost1_d = din("npost1R", [1, D])
    npost2_d = din("npost2R", [1, D])
    win_d = din("w_in", [D, 6688])
    lbP_d = din("lbP", [128, 16])
    onorm_d = din("onormP", [128, 2])
    bgk_d = din("bgkP", [128, 8])
    wgk_d = din("wgk", [2, 16, 512])
    wbrh_d = din("w_br_hg", [512, D])
    wbrg_d = din("w_br_gla", [512, D])
    wout_d = din("w_out", [D, D])
    wfg_d = din("w_ff_gate", [D, DFF])
    wfu_d = din("w_ff_up", [D, DFF])
    wfd_d = din("w_ff_down", [DFF, D])
    ident_d = din("ident", [128, 128])
    ones_d = din("ones", [128, 128])
    maskF_d = din("maskF4", [128, 512])
    maskB_d = din("maskB4", [128, 512])
    cm_d = din("cm", [128, T])
    out_d = nc.dram_tensor("out", [TL, D], F32, kind="ExternalOutput").ap()
    z1_d = nc.dram_tensor("z1_scratch", [TL, D], F32, kind="Internal").ap()
    ob_d = nc.dram_tensor("ob_scratch", [8, 128, TL], BF16, kind="Internal").ap()
    dbg_d = {}
    if debug:
        for name, shape in debug.items():
            if name.startswith("_"):
                continue
            dbg_d[name] = nc.dram_tensor("dbg_" + name, list(shape), F32, kind="ExternalOutput").ap()

    with ExitStack() as st:
        p = Prog(nc, st)
        ar = Arena(nc, SBUF_CAP)
        PSB = [st.enter_context(nc.psum_tensor("psb%d" % i, [128, 1024], F32)) for i in range(4)]

        def bank(i):
            return PSB[i // 2][:, (i % 2) * 512:(i % 2 + 1) * 512]

        def pk(i):
            return ("ps", i)

        def dbg(name, src_ap, reads, shape):
            if not debug or name not in debug:
                return
            m = ar.lo
            tmp = ar.alloc("dbgtmp", list(shape), F32)
            p.op("pool", lambda e: e.tensor_copy(out=tmp[:], in_=src_ap), reads=reads, writes=[("dbgtmp", name)])
            p.dma("sp", [(dbg_d[name], tmp[:])], reads=[("dbgtmp", name)])
            p.fence()
            ar.lo = m

        identb = ar.alloc("identb", [128, 128], BF16)
        onesb = ar.alloc("onesb", [128, 128], BF16)
        maskF = ar.alloc("maskF", [128, 512], BF16)
        maskB = ar.alloc("maskB", [128, 512], BF16)
        cm = ar.alloc("cm", [128, T], BF16)
        p.dma("pool", [(identb[:], ident_d), (onesb[:], ones_d), (maskF[:], maskF_d), (maskB[:], maskB_d),
                       (cm[:, 0:1152], cm_d[:, 0:1152]), (cm[:, 1152:T], cm_d[:, 1152:T])],
              writes=["consts"])
        modP = ar.alloc("modP", [128, 6, 8, 2], F32)
        bmodP = ar.alloc("bmodP", [128, 48], F32)
        npre1 = ar.alloc("npre1", [128, 8], F32)
        npre2 = ar.alloc("npre2", [128, 8], F32)
        lbP = ar.alloc("lbP", [128, 16], F32)
        onormP = ar.alloc("onormP", [128, 2], F32)
        bgkP = ar.alloc("bgkP", [128, 8], F32)
        cx = ar.alloc("cx", [128, 8], F32)
        cc = ar.alloc("cc", [128, 8], F32)
        p.dma("sp", [(bmodP[:], bmodP_d), (npre1[:], npre1_d), (npre2[:], npre2_d), (lbP[:], lbP_d),
                     (onormP[:], onorm_d), (bgkP[:], bgk_d), (cx[:], cx_d), (cc[:], cctx_d)], writes=["smallin"])
        vec = {}
        for nm in ["sc1x", "sh1x", "sc1c", "sh1c", "sc2x", "sh2x", "lb", "oml", "noml", "nbgk"]:
            vec[nm] = ar.alloc(nm, [128, 8], F32)
        cs = ar.alloc("cs", [128, 8, 2], BF16)
        csrep = ar.alloc("csrep", [128, 8, 128], BF16)
        sg = ar.alloc("sgc", [128, 8, 2], F32)
        p.op("act", lambda e: e.activation(out=sg[:, :, 0], in_=cx[:], func=AF.Sigmoid), reads=["smallin"], writes=["sg0"])
        p.op("act", lambda e: e.activation(out=sg[:, :, 1], in_=cc[:], func=AF.Sigmoid), reads=["smallin"], writes=["sg1"])
        p.op("dve", lambda e: e.tensor_tensor(out=cs[:, :, 0], in0=sg[:, :, 0], in1=cx[:], op=ALU.mult), reads=["sg0"], writes=["cs0"])
        p.op("dve", lambda e: e.tensor_tensor(out=cs[:, :, 1], in0=sg[:, :, 1], in1=cc[:], op=ALU.mult), reads=["sg1"], writes=["cs1"])
        p.op("dve", lambda e: e.tensor_copy(out=csrep[:], in_=cs[:, :, 0:1].broadcast_to([128, 8, 128])), reads=["cs0"], writes=["csrep"])
        p.op("dve", lambda e: e.tensor_tensor(out=vec["lb"][:], in0=lbP[:, 0:8], in1=lbP[:, 8:16], op=ALU.subtract), reads=["smallin"], writes=["lbd"])
        p.op("act", lambda e: e.activation(out=vec["lb"][:], in_=vec["lb"][:], func=AF.Sigmoid), reads=["lbd"], writes=["lb"])
        p.op("dve", lambda e: e.tensor_scalar(out=vec["oml"][:], in0=vec["lb"][:], scalar1=-1.0, scalar2=1.0, op0=ALU.mult, op1=ALU.add), reads=["lb"], writes=["oml"])
        p.op("dve", lambda e: e.tensor_scalar(out=vec["noml"][:], in0=vec["lb"][:], scalar1=-1.0, scalar2=None, op0=ALU.add), reads=["lb"], writes=["noml"])
        p.op("dve", lambda e: e.tensor_scalar(out=vec["nbgk"][:], in0=bgkP[:], scalar1=-1.0, scalar2=None, op0=ALU.mult), reads=["smallin"], writes=["nbgk"])

        if debug and debug.get("_stop") == "s0a":
            p.finish()
            return nc
        def mod_pp(j, wm, wkey):
            psv = bank(0)[:, 0:16].rearrange("p (n t) -> p n t", t=2)

            def g(e):
                last = None
                for nchk in range(8):
                    for kc in range(8):
                        last = e.matmul(psv[:, nchk, :], lhsT=wm[:, kc, nchk * 128:(nchk + 1) * 128], rhs=cs[:, kc, :],
                                        start=(kc == 0), stop=(kc == 7))
                return last
            p.op("pe", g, reads=[wkey, "cs0", "cs1"], writes=[pk(0)])
            p.op("dve", lambda e: e.tensor_tensor(out=modP[:, j], in0=psv,
                                                 in1=bmodP[:, j * 8:(j + 1) * 8].unsqueeze(2).broadcast_to([128, 8, 2]), op=ALU.add),
                 reads=[pk(0), "smallin"], writes=[("modP", j)])

        def load_wm(j, wm, wkey):
            p.dma("pool", [(wm[:], wmod_d[:, j * 1024:(j + 1) * 1024].rearrange("(c p) n -> p c n", p=128))], writes=[wkey])

        m_small = ar.lo
        hT = ar.alloc("hT", [128, 8, T], BF16)
        m_persist = ar.lo

        wmb = [ar.alloc("wm%d" % i, [128, 8, 1024], BF16) for i in range(2)]
        xb = [ar.alloc("xb%d" % i, [128, D], F32) for i in range(3)]
        NXN = 8
        xnb = [ar.alloc("xn%d" % i, [128, D], BF16) for i in range(NXN)]
        junk = ar.alloc("junk", [128, D], BF16)
        ssq = ar.alloc("ssq", [128, NT], F32)
        rstd = ar.alloc("rstd", [128, NT], F32)

        load_wm(0, wmb[0], ("wm", 0))
        load_wm(1, wmb[1], ("wm", 1))
        mod_pp(0, wmb[0], ("wm", 0))
        mod_pp(1, wmb[1], ("wm", 1))
        if debug and debug.get("_stop") == "s0b":
            p.finish()
            return nc
        for which, scn, shn in ((0, "sc1x", "sh1x"), (1, "sc1c", "sh1c")):
            p.op("dve", lambda e, which=which, scn=scn: e.scalar_tensor_tensor(out=vec[scn][:], in0=modP[:, 1, :, which], scalar=1.0, in1=npre1[:],
                                                                              op0=ALU.add, op1=ALU.mult),
                 reads=[("modP", 1), "smallin"], writes=[scn])
            p.op("dve", lambda e, which=which, shn=shn: e.tensor_copy(out=vec[shn][:], in_=modP[:, 0, :, which]), reads=[("modP", 0)], writes=[shn])

        def norm_transpose(i, src_ap, srckey, xt, xtkey, xn, xnkey, sc, sh, sckeys, dst, dstkey, col0, ssq_t, rstd_t, load=True, pair=3, phase="all"):
            if phase in ("all", "norm"):
                norm_part(i, src_ap, srckey, xt, xtkey, xn, xnkey, ssq_t, rstd_t, load)
            if phase in ("all", "tr"):
                tr_part(i, xn, xnkey, sc, sh, sckeys, dst, dstkey, col0, pair)

        def norm_part(i, src_ap, srckey, xt, xtkey, xn, xnkey, ssq_t, rstd_t, load):
            if load:
                p.dma("sp", [(xt[:], src_ap)], reads=[srckey] if srckey else [], writes=[xtkey])
            p.op("act", lambda e: e.activation(out=junk[:], in_=xt[:], func=AF.Square, accum_out=ssq_t[:, i:i + 1]),
                 reads=[xtkey], writes=["junk", ("ssq", i)])
            p.op("act", lambda e: e.activation(out=rstd_t[:, i:i + 1], in_=ssq_t[:, i:i + 1], func=AF.Ln, scale=1.0 / D, bias=EPS),
                 reads=[("ssq", i)], writes=[("rstd", i)])
            p.op("act", lambda e: e.activation(out=rstd_t[:, i:i + 1], in_=rstd_t[:, i:i + 1], func=AF.Exp, scale=-0.5),
                 reads=[("rstd", i)], writes=[("rstd", i)])
            p.op("dve", lambda e: e.tensor_scalar(out=xn[:], in0=xt[:], scalar1=rstd_t[:, i:i + 1], scalar2=None, op0=ALU.mult),
                 reads=[xtkey, ("rstd", i)], writes=[xnkey])

        def tr_part(i, xn, xnkey, sc, sh, sckeys, dst, dstkey, col0, pair):
            ptb = PSB[pair][:, :].rearrange("p (k t) -> p k t", t=128)
            pkeys = [pk(2 * pair), pk(2 * pair + 1)]

            def g(e):
                for kc in range(8):
                    e.matmul(ptb[:, kc, :], lhsT=xn[:, kc * 128:(kc + 1) * 128], rhs=identb[:], start=True, stop=True)
            p.op("pe", g, reads=[xnkey, "consts"], writes=pkeys)
            for kk in range(4):
                for kc, eng in ((kk, "act"), (kk + 4, "dve")):
                    o = dst[:, kc, col0:col0 + 128]
                    bkey = [pkeys[0] if kc < 4 else pkeys[1]]
                    if eng == "act":
                        p.op("act", lambda e: e.activation(out=o, in_=ptb[:, kc, :], func=AF.Identity, scale=sc[:, kc:kc + 1], bias=sh[:, kc:kc + 1]),
                             reads=bkey + sckeys, writes=[(dstkey, i, kc)])
                    else:
                        p.op("dve", lambda e: e.tensor_scalar(out=o, in0=ptb[:, kc, :], scalar1=sc[:, kc:kc + 1], scalar2=sh[:, kc:kc + 1],
                                                             op0=ALU.mult, op1=ALU.add),
                             reads=bkey + sckeys, writes=[(dstkey, i, kc)])

        s1_groups = [(0, 2), (2, 4), (6, 4), (10, 4), (14, 4)]

        def s1_norm(i):
            src = ctx_d[i * 128:(i + 1) * 128, :] if i < 2 else x_d[(i - 2) * 128:(i - 1) * 128, :]
            norm_part(i, src, None, xb[i % 3], ("xb", i % 3), xnb[i % NXN], ("xn", i % NXN), ssq, rstd, True)

        def s1_tr(t0, nt):
            sc, sh, keys = (vec["sc1c"], vec["sh1c"], ["sc1c", "sh1c"]) if t0 < 2 else (vec["sc1x"], vec["sh1x"], ["sc1x", "sh1x"])

            def g(e):
                for kc in range(8):
                    for t in range(nt):
                        i = t0 + t
                        e.matmul(bank(kc)[:, t * 128:(t + 1) * 128], lhsT=xnb[i % NXN][:, kc * 128:(kc + 1) * 128], rhs=identb[:],
                                 start=True, stop=True)
            p.op("pe", g, reads=[("xn", (t0 + t) % NXN) for t in range(nt)] + ["consts"], writes=[pk(k_) for k_ in range(8)])
            for kk in range(4):
                for kc, eng in ((kk, "act"), (kk + 4, "dve")):
                    o = hT[:, kc, t0 * 128:(t0 + nt) * 128]
                    srcp = bank(kc)[:, 0:nt * 128]
                    wk_ = [("hT", t0 + t, kc) for t in range(nt)]
                    if eng == "act":
                        p.op("act", lambda e: e.activation(out=o, in_=srcp, func=AF.Identity, scale=sc[:, kc:kc + 1], bias=sh[:, kc:kc + 1]),
                             reads=[pk(kc)] + keys, writes=wk_)
                    else:
                        p.op("dve", lambda e: e.tensor_scalar(out=o, in0=srcp, scalar1=sc[:, kc:kc + 1], scalar2=sh[:, kc:kc + 1],
                                                             op0=ALU.mult, op1=ALU.add),
                             reads=[pk(kc)] + keys, writes=wk_)

        for gi_, (t0_, nt_) in enumerate(s1_groups):
            for i in range(t0_, t0_ + nt_):
                s1_norm(i)
            if gi_ >= 1:
                s1_tr(*s1_groups[gi_ - 1])
        s1_tr(*s1_groups[-1])
        dbg("hT", hT[:, 0, :], [], [128, T])
        if debug and debug.get("_stop") == "s1":
            p.finish()
            return nc
        p.fence()
        ar.lo = m_persist

        def hT_keys(s, n):
            return [("hT", i, kc) for i in range(s // 128, (s + n) // 128) for kc in range(8)]

        wq = ar.alloc("wq", [128, 8, 128], BF16)
        wv = ar.alloc("wv", [128, 8, 128], BF16)
        wa = ar.alloc("wa", [128, 8, 128], BF16)
        wb = ar.alloc("wb", [128, 8, 128], BF16)
        wg = ar.alloc("wg", [128, 8, 128], BF16)
        wlr = ar.alloc("wlr", [128, 8, 32], BF16)
        wgk = ar.alloc("wgk", [16, 2, 512], BF16)
        X1 = ar.alloc("X1", [128, T], F32)
        CUM = ar.alloc("CUM", [128, T], F32)
        Kb = ar.alloc("Kb", [128, T], BF16)
        QT = ar.alloc("QT", [128, T], BF16)
        GT2 = [ar.alloc("GT%d" % i, [128, TL], BF16) for i in range(2)]
        VTM2 = [ar.alloc("VTM%d" % i, [128, NT, 128], BF16) for i in range(2)]
        Qt2 = [[ar.alloc("Qt%d%d" % (i, d), [128, T], BF16) for d in range(2)] for i in range(2)]
        Kt2 = [[ar.alloc("Kt%d%d" % (i, d), [128, T], BF16) for d in range(2)] for i in range(2)]
        smd2 = [[{nm: ar.alloc("%s%d%d" % (nm, i, d), [128, NCH], F32) for nm in ("MID", "DL", "E")} for d in range(2)] for i in range(2)]
        KTM = ar.alloc("KTM", [128, NT, 128], BF16)
        DS = ar.alloc("DS", [128, 128, NCH + 1], F32)
        DBC = ar.alloc("DBC", [128, 16, NCH + 1], F32)
        SPv = [ar.alloc("SPv%d" % d, [128, 128, NCH + 1], BF16) for d in range(2)]
        MS = [ar.alloc("MS%d" % d, [128, 512], BF16) for d in range(2)]
        for d_ in range(2):
            p.op("pool", lambda e: e.memset(MS[d_][:], 0.0), writes=[("MS", d_)])
        SQ = ar.alloc("SQ", [128, 512], BF16)
        Rr = ar.alloc("Rr", [128, 512], F32)
        ON = ar.alloc("ON", [128, 512], F32)
        OBt = [ar.alloc("OBt%d" % i, [128, 512], BF16) for i in range(2)]
        Dsc = ar.alloc("Dsc", [128, NCH + 1], F32)
        tmpA = ar.alloc("tmpA", [128, NCH], F32)
        tmpB = ar.alloc("tmpB", [128, NCH], F32)
        p.op("pool", lambda e: e.memset(Dsc[:], 0.0), writes=["Dsc"])
        p.op("pool", lambda e: e.memset(DS[:, :, 0:1], 0.0), writes=[("DS0",)])
        lrT = ar.alloc("lrT", [16, 2, T], BF16)
        p.dma("pool", [(wlr[:], win_d[:, 4608:4640].rearrange("(c p) n -> p c n", p=128)),
                       (wgk[:], wgk_d.rearrange("d r n -> r d n"))], writes=["wlr", "wgk"])
        GS = [s_ for (s_, n_) in GROUPS]
        X1b = X1[:].bitcast(BF16)
        nheads = 8 if not debug else debug.get("_nheads", [8])[0]

        def wcol(off):
            return win_d[:, off:off + 128].rearrange("(c p) n -> p c n", p=128)

        def head_p1(hh):
            br, hd, hp = hh // 4, hh % 4, hh % 2
            gs = 1.0 if br == 0 else -1.0 / 16.0
            GT, VTM, Qt, Kt = GT2[hp], VTM2[hp], Qt2[hp], Kt2[hp]
            def load_w(h2, which):
                if h2 >= nheads:
                    return
                b2, d2 = h2 // 4, h2 % 4
                if b2 == 0:
                    o2 = dict(q=d2 * 128, v=512 + d2 * 128, a=1024 + d2 * 128, b=1536 + d2 * 128, g=2048 + d2 * 128)
                else:
                    o2 = dict(q=2560 + d2 * 128, a=3072 + d2 * 128, v=3584 + d2 * 128, g=4096 + d2 * 128)
                bufs = dict(q=wq, v=wv, a=wa, b=wb, g=wg)
                for w_ in which:
                    if w_ in o2:
                        p.dma("pool", [(bufs[w_][:], wcol(o2[w_]))], writes=["w" + w_])
            if hh == 0:
                load_w(0, "qvgab")
            yield

            def proj_group(wt, wkey, gi, s, n, evac):
                b = gi % 2
                ps = bank(b)[:, 0:n]

                def g(e):
                    for kc in range(8):
                        e.matmul(ps, lhsT=wt[:, kc, :], rhs=hT[:, kc, s:s + n], start=(kc == 0), stop=(kc == 7))
                p.op("pe", g, reads=[wkey] + hT_keys(s, n), writes=[pk(b)])
                evac(s, n, ps, pk(b))

            if hh == 4:
                for d in range(2):
                    for gi, (s, n) in enumerate(GROUPS):
                        b = gi % 2
                        ps = bank(b)[0:16, 0:n]

                        def g(e):
                            for kc in range(8):
                                e.matmul(ps, lhsT=wlr[:, kc, d * 16:(d + 1) * 16], rhs=hT[:, kc, s:s + n], start=(kc == 0), stop=(kc == 7))
                        p.op("pe", g, reads=["wlr"] + hT_keys(s, n), writes=[pk(b)])
                        p.op("act", lambda e: e.activation(out=lrT[:, d, s:s + n], in_=ps, func=AF.Copy), reads=[pk(b)], writes=[("lrT", d, s)])
                        yield

            def sig_exp(s, n, ps, pkey):
                xs = X1[:, s:s + n]
                p.op("act", lambda e: e.activation(out=xs, in_=ps, func=AF.Exp, scale=-1.0), reads=[pkey], writes=[("X1", s), ("XA", s), ("XB", s)])
                p.op("act", lambda e: e.activation(out=xs, in_=xs, func=AF.Ln, bias=1.0), reads=[("X1", s)], writes=[("X1", s)])
                p.op("act", lambda e: e.activation(out=xs, in_=xs, func=AF.Exp, scale=-1.0), reads=[("X1", s)], writes=[("X1", s)])

            if br == 0:
                def evq(s, n, ps, pkey):
                    sig_exp(s, n, ps, pkey)
                    p.op("dve", lambda e: e.tensor_tensor(out=QT[:, s:s + n], in0=ps, in1=X1[:, s:s + n], op=ALU.mult),
                         reads=[pkey, ("X1", s)], writes=[("QT", s)])
            else:
                def evq(s, n, ps, pkey):
                    p.op("act", lambda e: e.activation(out=QT[:, s:s + n], in_=ps, func=AF.Copy, scale=128.0 ** -0.5), reads=[pkey], writes=[("QT", s)])

            def evg(s, n, ps, pkey):
                sig_exp(s, n, ps, pkey)
                p.op("dve", lambda e: e.tensor_tensor(out=GT[:, s - TC:s - TC + n], in0=ps, in1=X1[:, s:s + n], op=ALU.mult),
                     reads=[pkey, ("X1", s)], writes=[("GT", hp, s)])

            def v_group(i0):
                nt4 = min(4, NT - i0)
                psv = bank(2)

                def g(e):
                    for tl in range(nt4):
                        i = i0 + tl
                        for kc in range(8):
                            e.matmul(psv[:, tl * 128:(tl + 1) * 128], lhsT=hT[:, kc, i * 128:(i + 1) * 128], rhs=wv[:, kc, :],
                                     start=(kc == 0), stop=(kc == 7))
                p.op("pe", g, reads=["wv"] + hT_keys(i0 * 128, nt4 * 128), writes=[pk(2)])
                p.op("act", lambda e: e.activation(out=VTM[:, i0:i0 + nt4, :], in_=psv[:, 0:nt4 * 128].rearrange("p (t v) -> p t v", v=128), func=AF.Copy),
                     reads=[pk(2)], writes=[("VTM", hp, i0)])

            for gi, (s, n) in enumerate(GROUPS):
                if gi >= 1:
                    proj_group(wq, "wq", gi, s, n, evq)
                    proj_group(wg, "wg", gi + 1, s, n, evg)
                v_group(gi * 4)
                yield
            if br == 1:
                def evk(s, n, ps, pkey):
                    p.op("act", lambda e: e.activation(out=Kb[:, s:s + n], in_=ps, func=AF.Copy), reads=[pkey], writes=[("Kb", s)])
                for gi, (s, n) in enumerate(GROUPS):
                    proj_group(wa, "wa", gi, s, n, evk)
                    yield
            load_w(hh + 1, "qvg" if br == 0 else "qvga")

            for d in range(2):
                if d == 1:
                    yield "SPLIT"
                dh = d * 4 + hd
                smd = smd2[hp][d]
                pmid, pend = (31, 63) if d == 0 else (32, 0)
                chains = []
                for gi, (s, n) in enumerate(GROUPS):
                    c0, ncg = s // 64, n // 64
                    b = gi % 2
                    ps = bank(b)[:, 0:n]
                    xs = X1[:, s:s + n]
                    cu = CUM[:, s:s + n]
                    c3 = cu.rearrange("p (c l) -> p c l", l=64)
                    x3 = xs.rearrange("p (c l) -> p c l", l=64)
                    kX, kC, kK = ("X1", s), ("CUM", s), ("Kb", s)
                    if br == 0:
                        wt, wkey = (wa, "wa") if d == 0 else (wb, "wb")

                        def e0(s=s, n=n, ps=ps, b=b, wt=wt, wkey=wkey, xs=xs, kX=kX):
                            def g(e):
                                for kc in range(8):
                                    e.matmul(ps, lhsT=wt[:, kc, :], rhs=hT[:, kc, s:s + n], start=(kc == 0), stop=(kc == 7))
                            p.op("pe", g, reads=[wkey] + hT_keys(s, n), writes=[pk(b)])
                            p.op("act", lambda e: e.activation(out=xs, in_=ps, func=AF.Exp, scale=-1.0), reads=[pk(b)], writes=[kX, ("XA", s), ("XB", s)])

                        def e1(s=s, n=n, xs=xs, cu=cu, kX=kX, kK=kK, kC=kC):
                            p.op("act", lambda e: e.activation(out=cu, in_=xs, func=AF.Ln, bias=1.0), reads=[kX], writes=[kC])
                            p.op("act", lambda e: e.activation(out=xs, in_=cu, func=AF.Exp, scale=-1.0), reads=[kC], writes=[kX])
                            p.op("pool", lambda e: e.tensor_scalar(out=Kb[:, s:s + n], in0=xs, scalar1=vec["noml"][:, dh:dh + 1],
                                                                  scalar2=vec["oml"][:, dh:dh + 1], op0=ALU.mult, op1=ALU.add),
                                 reads=[kX, "noml", "oml"], writes=[kK])
                            p.op("act", lambda e: e.activation(out=xs, in_=xs, func=AF.Ln, scale=vec["oml"][:, dh:dh + 1], bias=vec["lb"][:, dh:dh + 1]),
                                 reads=[kX, "oml", "lb"], writes=[kX])
                    else:
                        def e0(s=s, n=n, ps=ps, b=b, xs=xs, kX=kX):
                            p.op("pe", lambda e: e.matmul(ps, lhsT=wgk[:, d, hd * 128:(hd + 1) * 128], rhs=lrT[:, d, s:s + n], start=True, stop=True),
                                 reads=["wgk", ("lrT", d, s)], writes=[pk(b)])
                            p.op("act", lambda e: e.activation(out=xs, in_=ps, func=AF.Exp, scale=-1.0, bias=vec["nbgk"][:, dh:dh + 1]),
                                 reads=[pk(b), "nbgk"], writes=[kX, ("XA", s), ("XB", s)])

                        def e1(xs=xs, kX=kX):
                            p.op("act", lambda e: e.activation(out=xs, in_=xs, func=AF.Ln, bias=1.0), reads=[kX], writes=[kX])

                    def e2(s=s, n=n, xs=xs, cu=cu, kX=kX, kC=kC):
                        if d == 0:
                            p.op("dve", lambda e: e.tensor_tensor_scan(out=cu, data0=cm[:, 0:n], data1=xs, initial=0.0, op0=ALU.mult, op1=ALU.add),
                                 reads=[kX, "consts"], writes=[kC])
                        else:
                            p.op("dve", lambda e: e.tensor_tensor_scan(out=cu[:, ::-1], data0=cm[:, 0:n], data1=xs[:, ::-1], initial=0.0,
                                                                      op0=ALU.mult, op1=ALU.add),
                                 reads=[kX, "consts"], writes=[kC])

                    def e3(s=s, c0=c0, ncg=ncg, c3=c3, kC=kC):
                        p.op("pool", lambda e: e.tensor_copy(out=smd["MID"][:, c0:c0 + ncg], in_=c3[:, :, pmid]), reads=[kC], writes=[("MID", hp, d, s)])
                        p.op("pool", lambda e: e.tensor_copy(out=smd["DL"][:, c0:c0 + ncg], in_=c3[:, :, pend]), reads=[kC], writes=[("DL", hp, d, s)])

                    def e4(s=s, c0=c0, ncg=ncg, c3=c3, kC=kC):
                        p.op("dve", lambda e: e.tensor_tensor(out=c3, in0=c3, in1=smd["MID"][:, c0:c0 + ncg].unsqueeze(2).broadcast_to([128, ncg, 64]),
                                                             op=ALU.subtract),
                             reads=[kC, ("MID", hp, d, s)], writes=[kC])

                    xa = X1b[:, 2 * s:2 * s + n]
                    xb_ = X1b[:, 2 * s + n:2 * s + 2 * n]

                    def e5(gi=gi, s=s, cu=cu, xa=xa, kX=kX, kC=kC):
                        if gi >= 1:
                            p.op("act", lambda e: e.activation(out=xa, in_=cu, func=AF.Exp, scale=gs), reads=[kC, kX], writes=[("XA", s)])

                    def e6(gi=gi, s=s, n=n, xa=xa):
                        if gi >= 1:
                            p.op("dve", lambda e: e.tensor_tensor(out=Qt[d][:, s:s + n], in0=QT[:, s:s + n], in1=xa, op=ALU.mult),
                                 reads=[("XA", s), ("QT", s)], writes=[("Qt", hp, d, s)])

                    def e7(s=s, cu=cu, xb_=xb_, kX=kX, kC=kC):
                        p.op("act", lambda e: e.activation(out=xb_, in_=cu, func=AF.Exp, scale=-gs), reads=[kC, kX], writes=[("XB", s)])

                    def e8(s=s, n=n, xb_=xb_, kK=kK):
                        p.op("dve", lambda e: e.tensor_tensor(out=Kt[d][:, s:s + n], in0=Kb[:, s:s + n], in1=xb_, op=ALU.mult),
                             reads=[("XB", s), kK], writes=[("Kt", hp, d, s)])
                    chains.append([e0, e1, e2, e3, e4, e5, e6, e7, e8])
                nel = len(chains[0])
                for step in range(nel + len(chains) - 1):
                    for gi in range(len(chains)):
                        k = step - gi
                        if 0 <= k < nel:
                            chains[gi][k]()
                    yield
            if br == 0:
                load_w(hh + 1, "ab")

        def head_p2(hh):
            br, hd, hp = hh // 4, hh % 4, hh % 2
            gs = 1.0 if br == 0 else -1.0 / 16.0
            GT, VTM, Qt, Kt = GT2[hp], VTM2[hp], Qt2[hp], Kt2[hp]
            allk = lambda nm, d: [(nm, hp, d, s_) for s_ in GS]
            for d in range(2):
                if d == 1:
                    yield "SPLIT"
                smd = smd2[hp][d]
                MIDt, DLt = smd["MID"], smd["DL"]
                p.op("pool", lambda e: e.tensor_tensor(out=tmpA[:], in0=DLt[:], in1=MIDt[:], op=ALU.subtract),
                     reads=allk("DL", d) + allk("MID", d), writes=["tmpA"])
                if d == 0:
                    p.op("pool", lambda e: e.tensor_tensor(out=tmpB[:, 0:35], in0=MIDt[:, 1:36], in1=tmpA[:, 0:35], op=ALU.add),
                         reads=["tmpA"] + allk("MID", d), writes=["tmpB"])
                    p.op("act", lambda e: e.activation(out=Dsc[:, 1:36], in_=tmpB[:, 0:35], func=AF.Exp, scale=gs), reads=["tmpB"], writes=["Dsc"])
                else:
                    p.op("pool", lambda e: e.tensor_tensor(out=tmpB[:, 1:36], in0=MIDt[:, 0:35], in1=tmpA[:, 1:36], op=ALU.add),
                         reads=["tmpA"] + allk("MID", d), writes=["tmpB"])
                    p.op("pool", lambda e: e.tensor_tensor(out=tmpB[:, 0:1], in0=MIDt[:, 35:36], in1=tmpA[:, 0:1], op=ALU.add),
                         reads=["tmpA", "tmpB"] + allk("MID", d), writes=["tmpB"])
                    p.op("act", lambda e: e.activation(out=Dsc[:, 1:5], in_=tmpB[:, 3::-1], func=AF.Exp, scale=gs), reads=["tmpB"], writes=["Dsc"])
                    p.op("act", lambda e: e.activation(out=Dsc[:, 5:36], in_=tmpB[:, 35:4:-1], func=AF.Exp, scale=gs), reads=["tmpB", "Dsc"], writes=["Dsc"])
                p.op("pool", lambda e: e.tensor_copy(out=DBC[:], in_=Dsc[:].unsqueeze(1).broadcast_to([128, 16, NCH + 1])), reads=["Dsc"], writes=["DBC"])
                yield
                for i0 in range(0, NT, 4):
                    nt4 = min(4, NT - i0)
                    b = 3 + (i0 // 4) % 2
                    psk = bank(b)

                    def g(e):
                        for tl in range(nt4):
                            i = i0 + tl
                            e.matmul(psk[:, tl * 128:(tl + 1) * 128], lhsT=Kt[d][:, i * 128:(i + 1) * 128], rhs=identb[:], start=True, stop=True)
                    p.op("pe", g, reads=allk("Kt", d) + ["consts"], writes=[pk(b)])
                    p.op("act", lambda e: e.activation(out=KTM[:, i0:i0 + nt4, :], in_=psk[:, 0:nt4 * 128].rearrange("p (t v) -> p t v", v=128), func=AF.Copy),
                         reads=[pk(b)], writes=[("KTM", i0)])
                    yield
                for si, (ti0, ntl) in enumerate(((0, 2), (2, 4), (6, 4), (10, 4), (14, 4))):
                    for half in range(2):
                        b = 3 + half
                        psd = bank(b)

                        def g(e):
                            for m in range(ntl):
                                i = ti0 + m
                                e.matmul(psd[:, m * 128:(m + 1) * 128], lhsT=KTM[half * 64:(half + 1) * 64, i, :],
                                         rhs=VTM[half * 64:(half + 1) * 64, i, :], start=True, stop=True)
                        p.op("pe", g, reads=[("KTM", i0_) for i0_ in range(0, NT, 4)] + [("VTM", hp, i0_) for i0_ in range(0, NT, 4)], writes=[pk(b)])
                        cfirst = 2 * ti0 + half
                        if d == 0:
                            dsv = DS[:, :, 1 + cfirst:1 + cfirst + 2 * ntl - 1:2]
                        elif ti0 == 0:
                            dsv = DS[:, :, 4 - half:4 - half - 2 * ntl + 1:-2]
                        else:
                            jst = 40 - cfirst
                            dsv = DS[:, :, jst:jst - 2 * ntl + 1:-2]
                        p.op("act", lambda e: e.activation(out=dsv.rearrange("p v j -> p j v"),
                                                          in_=psd[:, 0:ntl * 128].rearrange("p (c v) -> p c v", v=128), func=AF.Copy),
                             reads=[pk(b)], writes=[("DS", si, half)])
                    yield
                dsk = [("DS", si, half) for si in range(5) for half in range(2)]
                for qv in range(4):
                    for hv in range(2):
                        v0 = qv * 32 + hv * 16
                        v2 = DS[:, v0:v0 + 16, :].rearrange("p v j -> p (v j)")
                        o2 = SPv[d][:, v0:v0 + 16, :].rearrange("p v j -> p (v j)")
                        p.op("dve", lambda e: e.tensor_tensor_scan(out=o2, data0=v2, data1=DBC[:].rearrange("p v j -> p (v j)"), initial=0.0,
                                                                  op0=ALU.add, op1=ALU.mult),
                             reads=dsk + ["DBC", ("DS0",)], writes=[("SP", d, qv, hv)])
                    yield
            spk = [("SP", d_, q_, h_) for d_ in range(2) for q_ in range(4) for h_ in range(2)]
            chains = []
            for gl in range(4):
                s0 = 256 + gl * 512
                otb = 5 + gl % 2
                pot = bank(otb)
                obt = OBt[gl % 2]

                def f0(gl=gl, s0=s0):
                    for d in range(2):
                        psc = bank(3 + d)

                        def g(e):
                            for tl in range(4):
                                i = 2 + 4 * gl + tl
                                e.matmul(psc[:, tl * 128:(tl + 1) * 128], lhsT=Kt[d][:, i * 128:(i + 1) * 128], rhs=Qt[d][:, i * 128:(i + 1) * 128],
                                         start=True, stop=True)
                        p.op("pe", g, reads=[("Kt", hp, d, s0), ("Qt", hp, d, s0)], writes=[pk(3 + d)])
                        p.op("dve", lambda e: e.copy_predicated(out=MS[d][:], mask=(maskF if d == 0 else maskB)[:].bitcast(mybir.dt.uint16), data=psc),
                             reads=[pk(3 + d), "consts"], writes=[("MS", d)])

                def f1(gl=gl, s0=s0, otb=otb, pot=pot):
                    def g(e):
                        for tl in range(4):
                            i = 2 + 4 * gl + tl
                            cols = slice(tl * 128, (tl + 1) * 128)
                            e.matmul(pot[:, cols], lhsT=VTM[:, i, :], rhs=MS[0][:, cols], start=True, stop=False)
                            e.matmul(pot[:, cols], lhsT=VTM[:, i, :], rhs=MS[1][:, cols], start=False, stop=False)
                            for half in range(2):
                                c = 2 * i + half
                                cc_ = slice(tl * 128 + half * 64, tl * 128 + half * 64 + 64)
                                e.matmul(pot[:, cc_], lhsT=SPv[0][:, :, c], rhs=Qt[0][:, c * 64:(c + 1) * 64], start=False, stop=False)
                                e.matmul(pot[:, cc_], lhsT=SPv[1][:, :, 39 - c], rhs=Qt[1][:, c * 64:(c + 1) * 64], start=False, stop=True)
                    p.op("pe", g, reads=[("MS", 0), ("MS", 1), ("Qt", hp, 0, s0), ("Qt", hp, 1, s0)] + spk + [("VTM", hp, i0_) for i0_ in range(0, NT, 4)],
                         writes=[pk(otb)])

                def f2(otb=otb, pot=pot):
                    p.op("act", lambda e: e.activation(out=SQ[:], in_=pot, func=AF.Square), reads=[pk(otb)], writes=["SQ"])
                    p.op("pe", lambda e: e.matmul(bank(7), lhsT=onesb[:], rhs=SQ[:], start=True, stop=True), reads=["SQ", "consts"], writes=[pk(7)])
                    p.op("act", lambda e: e.activation(out=Rr[:], in_=bank(7), func=AF.Ln, scale=1.0 / 128, bias=EPS), reads=[pk(7)], writes=["Rr"])
                    p.op("act", lambda e: e.activation(out=Rr[:], in_=Rr[:], func=AF.Exp, scale=-0.5), reads=["Rr"], writes=["Rr"])

                def f3(gl=gl, s0=s0, otb=otb, pot=pot, obt=obt):
                    p.op("dve", lambda e: e.scalar_tensor_tensor(out=ON[:], in0=pot, scalar=onormP[:, br:br + 1], in1=Rr[:], op0=ALU.mult, op1=ALU.mult),
                         reads=[pk(otb), "Rr", "smallin"], writes=["ON"])
                    p.op("pool", lambda e: e.tensor_tensor(out=obt[:], in0=ON[:], in1=GT[:, gl * 512:(gl + 1) * 512], op=ALU.mult),
                         reads=["ON", ("GT", hp, s0)], writes=[("OBt", gl % 2)])
                    p.dma("sp", [(ob_d[hh, :, gl * 512:(gl + 1) * 512], obt[:])], reads=[("OBt", gl % 2)], writes=[("obd", hh, gl)])
                chains.append([f0, f1, f2, f3])
            nel = 4
            for step in range(nel + len(chains) - 1):
                for gl in range(len(chains)):
                    k = step - gl
                    if 0 <= k < nel:
                        chains[gl][k]()
                yield

        def co_run(ga, gb, stop_a, stop_b):
            act = [ga is not None, gb is not None]
            gens = [ga, gb]
            stops = [stop_a, stop_b]
            while act[0] or act[1]:
                for k in range(2):
                    if not act[k]:
                        continue
                    try:
                        v = next(gens[k])
                    except StopIteration:
                        act[k] = False
                        continue
                    if v == "SPLIT" and stops[k]:
                        act[k] = False

        _lo_s2 = ar.lo
        ar.lo = m_persist
        OB = ar.alloc("OB", [128, 8, TL], BF16)
        w3 = []
        for i in range(2):
            w3.append(dict(gh=ar.alloc("wgh%d" % i, [128, 8, 128], BF16), gg=ar.alloc("wgg%d" % i, [128, 8, 128], BF16),
                           bh=ar.alloc("wbh%d" % i, [128, 4, 128], BF16), bg=ar.alloc("wbg%d" % i, [128, 4, 128], BF16)))
            if i == 0:
                assert ar.lo <= m_persist + 5 * 2048 + 512 + 2048 + 2 * 9216 + 2 * 4608 + 4096
        lo_after_w3 = ar.lo
        ar.lo = _lo_s2

        def load_w3(nn, deps=()):
            if nn >= 8:
                return
            w = w3[nn % 2]
            p.dma("pool", [(w["gh"][:], wcol(4640 + nn * 128)), (w["gg"][:], wcol(5664 + nn * 128)),
                           (w["bh"][:], wbrh_d[:, nn * 128:(nn + 1) * 128].rearrange("(h p) n -> p h n", p=128)),
                           (w["bg"][:], wbrg_d[:, nn * 128:(nn + 1) * 128].rearrange("(h p) n -> p h n", p=128))], writes=[("w3", nn % 2)], deps=deps)

        def load_ob(h_, deps=()):
            for gl_ in range(4):
                p.dma("sp", [(OB[:, h_, gl_ * 512:(gl_ + 1) * 512], ob_d[h_, :, gl_ * 512:(gl_ + 1) * 512])],
                      reads=[("obd", h_, gl_)], writes=[("OBl", h_, gl_)], deps=deps)

        g2 = None
        for hh in range(nheads):
            g1 = head_p1(hh)
            co_run(g1, g2, True, False)
            g2 = head_p2(hh)
            co_run(g1, g2, False, True)
        snap = [("s_" + e_, p.cnt[e_]) for e_ in ENGS if p.cnt[e_] > 0] + \
               [("r%d" % i_, p.ring_total[i_]) for i_ in range(NRING) if p.ring_total[i_] > 0]
        for h_ in range(nheads - 1):
            load_ob(h_, deps=snap)
        load_w3(0, deps=snap)
        co_run(None, g2, False, False)
        if debug and debug.get("_stop") == "s2":
            p.finish()
            return nc
        p.fence()
        ar.lo = m_persist

        load_ob(nheads - 1)
        ar.lo = lo_after_w3
        grow = [ar.alloc_top("grow%d" % i, [128, D], F32) for i in range(2)]
        hi_grow = ar.hi
        MT = ar.alloc_top("MT", [128, 8, TL], BF16)
        wout = ar.alloc_top("wout", [128, 8, D], BF16)
        S12 = [ar.alloc("S12_%d" % i, [128, 512], F32) for i in range(2)]
        M12 = [ar.alloc("M12_%d" % i, [128, 512], F32) for i in range(2)]
        wmr = ar.alloc("wmr", [128, 8, 1024], BF16)
        brow = ar.alloc("brow", [128, D], F32)
        nrow = ar.alloc("nrow", [128, D], F32)
        woutk = [("wout", kc) for kc in range(8)]
        growk = [[("grow", gi_, 0), ("grow", gi_, 1)] for gi_ in range(2)]

        def load_wmr(j):
            p.dma("pool", [(wmr[:], wmod_d[:, j * 1024:(j + 1) * 1024].rearrange("(c p) n -> p c n", p=128))], writes=["wmr"])

        def gate_row(gi_, j, npost_d):
            p.dma("sp", [(brow[:], bmodR_d[:, j * 1024:(j + 1) * 1024].partition_broadcast(128)),
                         (nrow[:], npost_d.partition_broadcast(128))], writes=["brow"])
            for half in range(2):
                psr = bank(half)

                def g(e, psr=psr, half=half):
                    for kc in range(8):
                        e.matmul(psr, lhsT=csrep[:, kc, :], rhs=wmr[:, kc, half * 512:(half + 1) * 512], start=(kc == 0), stop=(kc == 7))
                p.op("pe", g, reads=["wmr"], writes=[pk(half)])
                p.op("dve", lambda e, psr=psr, half=half: e.tensor_tensor(out=grow[gi_][:, half * 512:(half + 1) * 512], in0=psr,
                                                                         in1=brow[:, half * 512:(half + 1) * 512], op=ALU.add),
                     reads=[pk(half), "brow"], writes=[("grow", gi_, half)])
            p.op("pool", lambda e: e.tensor_tensor(out=grow[gi_][:], in0=grow[gi_][:], in1=nrow[:], op=ALU.mult),
                 reads=[("grow", gi_, 0), ("grow", gi_, 1), "brow"], writes=[("grow", gi_, 0), ("grow", gi_, 1)])

        def side_work(nn):
            if nn == 0:
                for kc in range(8):
                    p.dma("pool", [(wout[:, kc, :], wout_d[kc * 128:(kc + 1) * 128, :])], writes=[("wout", kc)])
                mod_pp(3, wmr, "wmr")
                load_wmr(4)
            elif nn == 1:
                mod_pp(4, wmr, "wmr")
                load_wmr(2)
                p.op("dve", lambda e: e.scalar_tensor_tensor(out=vec["sc2x"][:], in0=modP[:, 4, :, 0], scalar=1.0, in1=npre2[:], op0=ALU.add, op1=ALU.mult),
                     reads=[("modP", 4), "smallin"], writes=["sc2x"])
                p.op("dve", lambda e: e.tensor_copy(out=vec["sh2x"][:], in_=modP[:, 3, :, 0]), reads=[("modP", 3)], writes=["sh2x"])
            elif nn == 2:
                gate_row(0, 2, npost1_d)
                load_wmr(5)
            elif nn == 3:
                gate_row(1, 5, npost2_d)

        load_w3(1)
        load_wmr(3)
        itn = 0
        for nn in range(8):
            w = w3[nn % 2]
            wk = ("w3", nn % 2)
            for gl in range(4):
                s = gl * 512
                for half in range(2):
                    bset = (itn % 3) * 2
                    itn += 1
                    pg_, pb_ = bank(bset), bank(bset + 1)
                    wgt, wbr = (w["gh"], w["bh"]) if half == 0 else (w["gg"], w["bg"])

                    def g(e, pg_=pg_, pb_=pb_, wgt=wgt, wbr=wbr, s=s, half=half):
                        for kc in range(8):
                            e.matmul(pg_, lhsT=wgt[:, kc, :], rhs=hT[:, kc, TC + s:TC + s + 512], start=(kc == 0), stop=(kc == 7))
                        for h in range(4):
                            e.matmul(pb_, lhsT=wbr[:, h, :], rhs=OB[:, half * 4 + h, s:s + 512], start=(h == 0), stop=(h == 3))
                    p.op("pe", g, reads=[wk] + [("OBl", half * 4 + h_, gl) for h_ in range(4)], writes=[pk(bset), pk(bset + 1)])
                    p.op("act", lambda e, pg_=pg_, half=half: e.activation(out=S12[half][:], in_=pg_, func=AF.Sigmoid),
                         reads=[pk(bset)], writes=[("S12", half)])
                    p.op("dve", lambda e, pb_=pb_, half=half: e.tensor_tensor(out=M12[half][:], in0=pb_, in1=S12[half][:], op=ALU.mult),
                         reads=[pk(bset + 1), ("S12", half)], writes=[("M12", half)])
                p.op("pool", lambda e, nn=nn, s=s: e.tensor_tensor(out=MT[:, nn, s:s + 512], in0=M12[0][:], in1=M12[1][:], op=ALU.add),
                     reads=[("M12", 0), ("M12", 1)], writes=[("MT", nn, gl)])
            load_w3(nn + 2)
            side_work(nn)
        dbg("MT", MT[:, 0, :], [], [128, TL])
        if debug and debug.get("_stop") == "3a":
            p.finish()
            return nc
        p.fence()
        ar.lo = m_small

        h2T = ar.alloc("h2T", [128, 8, TL], BF16)
        wdn = ar.alloc("wdn", [128, FC, D], BF16)
        m3 = ar.lo
        for fc in range(FC):
            p.dma("pool", [(wdn[:, fc, :], wfd_d[fc * 128:(fc + 1) * 128, :])], writes=[("wdn", fc)])
        NB3 = 3
        xb = [ar.alloc("xb3_%d" % i, [128, D], F32) for i in range(NB3)]
        T1 = [ar.alloc("T1_%d" % i, [128, D], F32) for i in range(NB3)]
        Z1 = [ar.alloc("Z1_%d" % i, [128, D], F32) for i in range(NB3)]
        XN = [ar.alloc("XN_%d" % i, [128, D], BF16) for i in range(NB3)]
        junk = ar.alloc("junk3", [128, D], BF16)
        ss1 = ar.alloc("ss1", [128, 16], F32)
        r1 = ar.alloc("r1", [128, 16], F32)
        ss2 = ar.alloc("ss2", [128, 16], F32)
        r2 = ar.alloc("r2", [128, 16], F32)
        _lo3b = ar.lo
        ar.lo = max(_lo3b, m3 + FC * 1024 * 2)
        wfg = [ar.alloc("wfg%d" % i, [128, 8, 128], BF16) for i in range(3)]
        wfu = [ar.alloc("wfu%d" % i, [128, 8, 128], BF16) for i in range(3)]
        lo_after_wf = ar.lo
        ar.lo = _lo3b

        def load_wf(fc):
            wi = fc % 3
            p.dma("pool", [(wfg[wi][:], wfg_d[:, fc * 128:(fc + 1) * 128].rearrange("(c p) n -> p c n", p=128)),
                           (wfu[wi][:], wfu_d[:, fc * 128:(fc + 1) * 128].rearrange("(c p) n -> p c n", p=128))], writes=[("wf", wi)])
        def tile_a(i):
            yp = PSB[i % 3]
            ypk = [pk(2 * (i % 3)), pk(2 * (i % 3) + 1)]

            def g(e, i=i, yp=yp):
                last = None
                for half in range(2):
                    for kc in range(8):
                        last = e.matmul(yp[:, half * 512:(half + 1) * 512], lhsT=MT[:, kc, i * 128:(i + 1) * 128],
                                        rhs=wout[:, kc, half * 512:(half + 1) * 512], start=(kc == 0), stop=(kc == 7))
                return last
            p.op("pe", g, reads=woutk, writes=ypk)
            p.op("act", lambda e, i=i, yp=yp: e.activation(out=junk[:], in_=yp[:, :], func=AF.Square, accum_out=ss1[:, i:i + 1]),
                 reads=ypk, writes=["junk", ("ss1", i)])
            p.op("act", lambda e, i=i: e.activation(out=r1[:, i:i + 1], in_=ss1[:, i:i + 1], func=AF.Ln, scale=1.0 / D, bias=EPS),
                 reads=[("ss1", i)], writes=[("r1", i)])
            p.op("act", lambda e, i=i: e.activation(out=r1[:, i:i + 1], in_=r1[:, i:i + 1], func=AF.Exp, scale=-0.5),
                 reads=[("r1", i)], writes=[("r1", i)])
            p.op("dve", lambda e, i=i, yp=yp: e.scalar_tensor_tensor(out=T1[i % NB3][:], in0=yp[:, :], scalar=r1[:, i:i + 1], in1=grow[0][:],
                                                                   op0=ALU.mult, op1=ALU.mult),
                 reads=ypk + [("r1", i)] + growk[0], writes=[("T1", i % NB3)])

        def tile_a2(i):
            p.op("pool", lambda e, i=i: e.tensor_tensor(out=Z1[i % NB3][:], in0=T1[i % NB3][:], in1=xb[i % NB3][:], op=ALU.add),
                 reads=[("T1", i % NB3), ("xb", i % NB3)], writes=[("Z1", i % NB3)])
            p.dma("pool", [(z1_d[i * 128:(i + 1) * 128, :], Z1[i % NB3][:])], reads=[("Z1", i % NB3)], writes=[("z1d", i)])
            norm_transpose(i, None, None, Z1[i % NB3], ("Z1", i % NB3), XN[i % NB3], ("XN", i % NB3), vec["sc2x"], vec["sh2x"], ["sc2x", "sh2x"],
                           h2T, "h2T", i * 128, ss2, r2, load=False, pair=3, phase="norm")

        def tile_b(i):
            norm_transpose(i, None, None, Z1[i % NB3], ("Z1", i % NB3), XN[i % NB3], ("XN", i % NB3), vec["sc2x"], vec["sh2x"], ["sc2x", "sh2x"],
                           h2T, "h2T", i * 128, ss2, r2, load=False, pair=3, phase="tr")

        for i_ in range(2):
            p.dma("sp", [(xb[i_][:], x_d[i_ * 128:(i_ + 1) * 128, :])], writes=[("xb", i_)])
        for i in range(18):
            if i < 16:
                tile_a(i)
            if 1 <= i < 17:
                tile_a2(i - 1)
            if i == 13:
                for fc_ in range(3):
                    load_wf(fc_)
            if i + 2 < 16:
                p.dma("sp", [(xb[(i + 2) % NB3][:], x_d[(i + 2) * 128:(i + 3) * 128, :])], writes=[("xb", (i + 2) % NB3)])
            if i >= 2:
                tile_b(i - 2)
        dbg("h2T", h2T[:, 0, :], [], [128, TL])
        if debug and debug.get("_stop") == "3b":
            p.finish()
            return nc
        p.fence()
        ar.lo = m3
        ar.hi = hi_grow

        HID = ar.alloc("HID", [128, FC, 1024], BF16)
        assert ar.lo <= lo_after_wf - 6 * 2048
        ar.lo = lo_after_wf
        SG = [ar.alloc("SG%d" % i, [128, 512], F32) for i in range(2)]
        T2 = [ar.alloc("T2_%d" % i, [128, D], F32) for i in range(2)]
        zt = [ar.alloc("zt%d" % i, [128, D], F32) for i in range(3)]
        OT_ = [ar.alloc("OT%d" % i, [128, D], F32) for i in range(2)]
        junk = ar.alloc("junk4", [128, D], BF16)
        ss3 = ar.alloc("ss3", [128, 16], F32)
        r3 = ar.alloc("r3", [128, 16], F32)
        wdnk = [("wdn", fc) for fc in range(FC)]
        itn = 0
        for grp in range(2):
            for fc in range(FC):
                wi = fc % 3
                if not (grp == 0 and fc < 3):
                    load_wf(fc)
                for sub in range(2):
                    t0 = grp * 1024 + sub * 512
                    bset = (itn % 3) * 2
                    itn += 1
                    pg_, pu_ = bank(bset), bank(bset + 1)

                    def g(e, pg_=pg_, pu_=pu_, wi=wi, t0=t0):
                        last = None
                        for kc in range(8):
                            e.matmul(pg_, lhsT=wfg[wi][:, kc, :], rhs=h2T[:, kc, t0:t0 + 512], start=(kc == 0), stop=(kc == 7))
                        for kc in range(8):
                            last = e.matmul(pu_, lhsT=wfu[wi][:, kc, :], rhs=h2T[:, kc, t0:t0 + 512], start=(kc == 0), stop=(kc == 7))
                        return last
                    p.op("pe", g, reads=[("wf", wi)], writes=[pk(bset), pk(bset + 1)])
                    p.op("act", lambda e, pg_=pg_, sub=sub: e.activation(out=SG[sub][:], in_=pg_, func=AF.Silu), reads=[pk(bset)], writes=[("SG", sub)])
                    p.op("dve", lambda e, pu_=pu_, sub=sub, fc=fc: e.tensor_tensor(out=HID[:, fc, sub * 512:(sub + 1) * 512], in0=pu_, in1=SG[sub][:], op=ALU.mult),
                         reads=[pk(bset + 1), ("SG", sub)], writes=[("HID", fc, sub)])
            for i_ in range(grp * 8, grp * 8 + 2):
                p.dma("sp", [(zt[i_ % 3][:], z1_d[i_ * 128:(i_ + 1) * 128, :])], writes=[("zt", i_ % 3)])
            for tt in range(8):
                i = grp * 8 + tt
                yp = PSB[i % 2]
                ypk = [pk(2 * (i % 2)), pk(2 * (i % 2) + 1)]

                def g(e, tt=tt, yp=yp):
                    last = None
                    for half in range(2):
                        for fc in range(FC):
                            last = e.matmul(yp[:, half * 512:(half + 1) * 512], lhsT=HID[:, fc, tt * 128:(tt + 1) * 128],
                                            rhs=wdn[:, fc, half * 512:(half + 1) * 512], start=(fc == 0), stop=(fc == FC - 1))
                    return last
                p.op("pe", g, reads=wdnk + [("HID", fc, tt // 4) for fc in range(FC)], writes=ypk)
                p.op("act", lambda e, i=i, yp=yp: e.activation(out=junk[:], in_=yp[:, :], func=AF.Square, accum_out=ss3[:, i:i + 1]),
                     reads=ypk, writes=["junk", ("ss3", i)])
                p.op("act", lambda e, i=i: e.activation(out=r3[:, i:i + 1], in_=ss3[:, i:i + 1], func=AF.Ln, scale=1.0 / D, bias=EPS),
                     reads=[("ss3", i)], writes=[("r3", i)])
                p.op("act", lambda e, i=i: e.activation(out=r3[:, i:i + 1], in_=r3[:, i:i + 1], func=AF.Exp, scale=-0.5),
                     reads=[("r3", i)], writes=[("r3", i)])
                p.op("dve", lambda e, i=i, yp=yp: e.scalar_tensor_tensor(out=T2[i % 2][:], in0=yp[:, :], scalar=r3[:, i:i + 1], in1=grow[1][:],
                                                                       op0=ALU.mult, op1=ALU.mult),
                     reads=ypk + [("r3", i)], writes=[("T2", i % 2)])
                if tt + 2 < 8:
                    p.dma("sp", [(zt[(i + 2) % 3][:], z1_d[(i + 2) * 128:(i + 3) * 128, :])], writes=[("zt", (i + 2) % 3)])
                p.op("pool", lambda e, i=i: e.tensor_tensor(out=OT_[i % 2][:], in0=T2[i % 2][:], in1=zt[i % 3][:], op=ALU.add),
                     reads=[("T2", i % 2), ("zt", i % 3)], writes=[("OTo", i % 2)])
                p.dma("pool", [(out_d[i * 128:(i + 1) * 128, :], OT_[i % 2][:])], reads=[("OTo", i % 2)], writes=[("outd", i)])
        p.finish()
    return nc


_NC_CACHE = {}


def _host_inputs(inputs):
    f32 = np.float32
    g = lambda k: np.asarray(inputs[k], dtype=f32)
    x, c, ctx, c_ctx = g("x"), g("c"), g("ctx"), g("c_ctx")
    col = lambda v: np.ascontiguousarray(v.reshape(-1, 128).T)
    shared = {
        "cctx": col(c_ctx),
        "w_mod": np.ascontiguousarray(g("w_mod")[0]),
        "bmodP": col(g("b_mod")[0]),
        "bmodR": np.ascontiguousarray(g("b_mod")[0].reshape(1, -1)),
        "npre1P": col(g("norm_pre1")[0]),
        "npre2P": col(g("norm_pre2")[0]),
        "npost1R": np.ascontiguousarray(g("norm_post1")[0].reshape(1, -1)),
        "npost2R": np.ascontiguousarray(g("norm_post2")[0].reshape(1, -1)),
        "w_in": np.ascontiguousarray(g("w_in")[0]),
        "lbP": np.ascontiguousarray(g("hg_lb").reshape(2, 2, 4, 128).transpose(3, 0, 1, 2).reshape(128, 16)),
        "onormP": np.ascontiguousarray(np.stack([g("hg_onorm")[0], g("gla_onorm")[0]], axis=1)),
        "bgkP": np.ascontiguousarray(g("gla_b_gk")[0].reshape(2, 4, 128).transpose(2, 0, 1).reshape(128, 8)),
        "wgk": np.ascontiguousarray(g("gla_w_gk")[0]),
        "w_br_hg": np.ascontiguousarray(g("w_br_hg")[0]),
        "w_br_gla": np.ascontiguousarray(g("w_br_gla")[0]),
        "w_out": np.ascontiguousarray(g("w_out")[0]),
        "w_ff_gate": np.ascontiguousarray(g("w_ff_gate")[0]),
        "w_ff_up": np.ascontiguousarray(g("w_ff_up")[0]),
        "w_ff_down": np.ascontiguousarray(g("w_ff_down")[0]),
    }
    s = np.arange(128)[:, None]
    t = np.arange(128)[None, :]
    same = (s // 64) == (t // 64)
    shared["ident"] = np.eye(128, dtype=f32)
    shared["ones"] = np.ones((128, 128), f32)
    shared["maskF4"] = np.tile((same & (s <= t)).astype(f32), (1, 4))
    shared["maskB4"] = np.tile((same & (s >= t)).astype(f32), (1, 4))
    cmr = np.ones((128, T), f32)
    cmr[:, ::64] = 0.0
    shared["cm"] = cmr
    maps = []
    for b in range(NCORES):
        m = dict(shared)
        m["x"] = np.ascontiguousarray(x[b])
        m["ctxx"] = np.ascontiguousarray(ctx[b])
        m["cx"] = col(c[b])
        maps.append(m)
    return maps


def kernel(**inputs):
    if "nc" not in _NC_CACHE:
        _NC_CACHE["nc"] = build_program()
    nc = _NC_CACHE["nc"]
    maps = _host_inputs(inputs)
    res = run_bass_kernel_spmd(nc, maps, core_ids=list(range(NCORES)))
    out = np.stack([np.asarray(res.results[b]["out"], dtype=np.float32) for b in range(NCORES)], axis=0)
    return out
```

```python
import numpy as np
from contextlib import ExitStack
import concourse.bass as bass
import concourse.mybir as mybir
from concourse.bass_utils import run_bass_kernel_spmd

F32 = mybir.dt.float32
BF16 = mybir.dt.bfloat16
AF = mybir.ActivationFunctionType
ALU = mybir.AluOpType

D = 1024
TC = 256
TL = 2048
T = TC + TL
NT = T // 128
NCH = T // 64
DFF = 2816
FC = DFF // 128
EPS = 1e-6
NCORES = 8
GROUPS = [(0, 256), (256, 512), (768, 512), (1280, 512), (1792, 512)]

ENGS = ["pe", "act", "dve", "pool", "sp"]
NRING = 40
SBUF_BASE = 16576
SBUF_CAP = 229376 - 128


class Prog:
    def __init__(self, nc, stack):
        self.nc = nc
        self.ops = {e: [] for e in ENGS}
        self.semobj = {}
        for e in ENGS:
            self.semobj["s_" + e] = stack.enter_context(nc.semaphore("s_" + e))
        for i in range(NRING):
            self.semobj["r%d" % i] = stack.enter_context(nc.semaphore("r%d" % i))
        self.cnt = {e: 0 for e in ENGS}
        self.seen = {e: {} for e in ENGS}
        self.ring_total = [0] * NRING
        self.ring_next = 0
        self.reg = {}
        self.fence_deps = []

    def _waits(self, eng, deps):
        best = {}
        for d in deps:
            if d is None:
                continue
            key, val = d
            if eng == "pe" and key == "s_pe":
                continue
            if val > best.get(key, 0):
                best[key] = val
        waits = []
        for key, val in best.items():
            if self.seen[eng].get(key, 0) >= val:
                continue
            self.seen[eng][key] = val
            waits.append((key, val))
        return waits

    def _deps(self, reads, writes):
        deps = list(self.fence_deps)
        for r in reads:
            st = self.reg.get(r)
            if st is not None:
                deps.append(st[0])
        for w in writes:
            st = self.reg.get(w)
            if st is not None:
                deps.append(st[0])
                deps += list(st[1].items())
        return deps

    def _record(self, h, reads, writes):
        for r in reads:
            st = self.reg.get(r)
            if st is None:
                st = self.reg[r] = [None, {}]
            if h[1] > st[1].get(h[0], 0):
                st[1][h[0]] = h[1]
        for w in writes:
            self.reg[w] = [h, {}]

    def op(self, eng, fn, reads=(), writes=(), deps=()):
        rec = _Recorder()
        fn(rec)
        specs = rec.specs
        assert specs
        fn = (lambda e, specs=specs: _play(e, specs))
        writes = list(writes) + [r for r in reads if isinstance(r, tuple) and r[0] == "ps"]
        d = self._deps(reads, writes) + list(deps)
        waits = self._waits(eng, d)
        self.cnt[eng] += 1
        h = ("s_" + eng, self.cnt[eng])
        self.ops[eng].append((waits, fn, ("s_" + eng, 1)))
        self._record(h, reads, writes)
        return h

    def dma(self, queue, pairs, reads=(), writes=(), deps=()):
        slot = self.ring_next
        self.ring_next = (self.ring_next + 1) % NRING
        key = "r%d" % slot
        d = self._deps(reads, writes) + list(deps)
        if self.ring_total[slot] > 0:
            d.append((key, self.ring_total[slot]))
        waits = self._waits(queue, d)
        first = True
        for (o, i) in pairs:
            self.ring_total[slot] += 16
            self.ops[queue].append((waits if first else [], (lambda e, o=o, i=i: e.dma_start(out=o, in_=i)), (key, 16)))
            first = False
        h = (key, self.ring_total[slot])
        self._record(h, reads, writes)
        return h

    def fence(self):
        self.fence_deps = [("s_" + e, self.cnt[e]) for e in ENGS if self.cnt[e] > 0] + \
                          [("r%d" % i, self.ring_total[i]) for i in range(NRING) if self.ring_total[i] > 0]
        self.reg = {}

    def finish(self):
        self.fence()
        self.ops["sp"].append((self._waits("sp", self.fence_deps), None, None))
        nc = self.nc

        def replay(e, name):
            for waits, fn, inc in self.ops[name]:
                for key, val in waits:
                    e.wait_ge(self.semobj[key], val)
                if fn is None:
                    continue
                ins = fn(e)
                if inc is not None:
                    ins.then_inc(self.semobj[inc[0]], inc[1])

        with nc.Block() as block:
            block.tensor(lambda e: replay(e, "pe"))
            block.scalar(lambda e: replay(e, "act"))
            block.vector(lambda e: replay(e, "dve"))
            block.gpsimd(lambda e: replay(e, "pool"))
            block.sync(lambda e: replay(e, "sp"))


class _Recorder:
    def __init__(self):
        self.specs = []

    def __getattr__(self, name):
        def f(*a, **k):
            self.specs.append((name, a, k))
            return None
        return f


def _play(e, specs):
    ins = None
    for (name, a, k) in specs:
        ins = getattr(e, name)(*a, **k)
    return ins


class Arena:
    def __init__(self, nc, cap):
        self.nc = nc
        self.lo = SBUF_BASE
        self.hi = cap
        self.n = 0

    @staticmethod
    def _size(shape, dt):
        n = 1
        for s in shape[1:]:
            n *= s
        return n * (4 if dt == F32 else 2)

    def alloc(self, name, shape, dt):
        off = (self.lo + 31) // 32 * 32
        sz = self._size(shape, dt)
        assert off + sz <= self.hi, "SBUF overflow at %s: %d + %d > %d" % (name, off, sz, self.hi)
        self.lo = off + sz
        self.n += 1
        return self.nc.alloc_sbuf_tensor_at("%s_%d" % (name, self.n), shape, dt, offset=off)

    def alloc_top(self, name, shape, dt):
        sz = self._size(shape, dt)
        off = (self.hi - sz) // 32 * 32
        assert off >= self.lo, "SBUF overflow (top) at %s" % name
        self.hi = off
        self.n += 1
        return self.nc.alloc_sbuf_tensor_at("%s_%d" % (name, self.n), shape, dt, offset=off)


def build_program(debug=None):
    nc = bass.Bass("TRN2", target_bir_lowering=False)

    def din(name, shape, dt=F32):
        return nc.dram_tensor(name, list(shape), dt, kind="ExternalInput").ap()

    x_d = din("x", [TL, D])
    ctx_d = din("ctxx", [TC, D])
    cx_d = din("cx", [128, 8])
    cctx_d = din("cctx", [128, 8])
    wmod_d = din("w_mod", [D, 6 * D])
    bmodP_d = din("bmodP", [128, 48])
    bmodR_d = din("bmodR", [1, 6 * D])
    npre1_d = din("npre1P", [128, 8])
    npre2_d = din("npre2P", [128, 8])
    npost1_d = din("npost1R", [1, D])
    npost2_d = din("npost2R", [1, D])
    win_d = din("w_in", [D, 6688])
    lbP_d = din("lbP", [128, 16])
    onorm_d = din("onormP", [128, 2])
    bgk_d = din("bgkP", [128, 8])
    wgk_d = din("wgk", [2, 16, 512])
    wbrh_d = din("w_br_hg", [512, D])
    wbrg_d = din("w_br_gla", [512, D])
    wout_d = din("w_out", [D, D])
    wfg_d = din("w_ff_gate", [D, DFF])
    wfu_d = din("w_ff_up", [D, DFF])
    wfd_d = din("w_ff_down", [DFF, D])
    ident_d = din("ident", [128, 128])
    ones_d = din("ones", [128, 128])
    maskF_d = din("maskF4", [128, 512])
    maskB_d = din("maskB4", [128, 512])
    cm_d = din("cm", [128, T])
    out_d = nc.dram_tensor("out", [TL, D], F32, kind="ExternalOutput").ap()
    z1_d = nc.dram_tensor("z1_scratch", [TL, D], F32, kind="Internal").ap()
    ob_d = nc.dram_tensor("ob_scratch", [8, 128, TL], BF16, kind="Internal").ap()
    dbg_d = {}
    if debug:
        for name, shape in debug.items():
            if name.startswith("_"):
                continue
            dbg_d[name] = nc.dram_tensor("dbg_" + name, list(shape), F32, kind="ExternalOutput").ap()

    with ExitStack() as st:
        p = Prog(nc, st)
        ar = Arena(nc, SBUF_CAP)
        PSB = [st.enter_context(nc.psum_tensor("psb%d" % i, [128, 1024], F32)) for i in range(4)]

        def bank(i):
            return PSB[i // 2][:, (i % 2) * 512:(i % 2 + 1) * 512]

        def pk(i):
            return ("ps", i)

        def dbg(name, src_ap, reads, shape):
            if not debug or name not in debug:
                return
            m = ar.lo
            tmp = ar.alloc("dbgtmp", list(shape), F32)
            p.op("pool", lambda e: e.tensor_copy(out=tmp[:], in_=src_ap), reads=reads, writes=[("dbgtmp", name)])
            p.dma("sp", [(dbg_d[name], tmp[:])], reads=[("dbgtmp", name)])
            p.fence()
            ar.lo = m

        identb = ar.alloc("identb", [128, 128], BF16)
        onesb = ar.alloc("onesb", [128, 128], BF16)
        maskF = ar.alloc("maskF", [128, 512], BF16)
        maskB = ar.alloc("maskB", [128, 512], BF16)
        cm = ar.alloc("cm", [128, T], BF16)
        p.dma("pool", [(identb[:], ident_d), (onesb[:], ones_d), (maskF[:], maskF_d), (maskB[:], maskB_d),
                       (cm[:, 0:1152], cm_d[:, 0:1152]), (cm[:, 1152:T], cm_d[:, 1152:T])],
              writes=["consts"])
        modP = ar.alloc("modP", [128, 6, 8, 2], F32)
        bmodP = ar.alloc("bmodP", [128, 48], F32)
        npre1 = ar.alloc("npre1", [128, 8], F32)
        npre2 = ar.alloc("npre2", [128, 8], F32)
        lbP = ar.alloc("lbP", [128, 16], F32)
        onormP = ar.alloc("onormP", [128, 2], F32)
        bgkP = ar.alloc("bgkP", [128, 8], F32)
        cx = ar.alloc("cx", [128, 8], F32)
        cc = ar.alloc("cc", [128, 8], F32)
        p.dma("sp", [(bmodP[:], bmodP_d), (npre1[:], npre1_d), (npre2[:], npre2_d), (lbP[:], lbP_d),
                     (onormP[:], onorm_d), (bgkP[:], bgk_d), (cx[:], cx_d), (cc[:], cctx_d)], writes=["smallin"])
        vec = {}
        for nm in ["sc1x", "sh1x", "sc1c", "sh1c", "sc2x", "sh2x", "lb", "oml", "noml", "nbgk"]:
            vec[nm] = ar.alloc(nm, [128, 8], F32)
        cs = ar.alloc("cs", [128, 8, 2], BF16)
        csrep = ar.alloc("csrep", [128, 8, 128], BF16)
        sg = ar.alloc("sgc", [128, 8, 2], F32)
        p.op("act", lambda e: e.activation(out=sg[:, :, 0], in_=cx[:], func=AF.Sigmoid), reads=["smallin"], writes=["sg0"])
        p.op("act", lambda e: e.activation(out=sg[:, :, 1], in_=cc[:], func=AF.Sigmoid), reads=["smallin"], writes=["sg1"])
        p.op("dve", lambda e: e.tensor_tensor(out=cs[:, :, 0], in0=sg[:, :, 0], in1=cx[:], op=ALU.mult), reads=["sg0"], writes=["cs0"])
        p.op("dve", lambda e: e.tensor_tensor(out=cs[:, :, 1], in0=sg[:, :, 1], in1=cc[:], op=ALU.mult), reads=["sg1"], writes=["cs1"])
        p.op("dve", lambda e: e.tensor_copy(out=csrep[:], in_=cs[:, :, 0:1].broadcast_to([128, 8, 128])), reads=["cs0"], writes=["csrep"])
        p.op("dve", lambda e: e.tensor_tensor(out=vec["lb"][:], in0=lbP[:, 0:8], in1=lbP[:, 8:16], op=ALU.subtract), reads=["smallin"], writes=["lbd"])
        p.op("act", lambda e: e.activation(out=vec["lb"][:], in_=vec["lb"][:], func=AF.Sigmoid), reads=["lbd"], writes=["lb"])
        p.op("dve", lambda e: e.tensor_scalar(out=vec["oml"][:], in0=vec["lb"][:], scalar1=-1.0, scalar2=1.0, op0=ALU.mult, op1=ALU.add), reads=["lb"], writes=["oml"])
        p.op("dve", lambda e: e.tensor_scalar(out=vec["noml"][:], in0=vec["lb"][:], scalar1=-1.0, scalar2=None, op0=ALU.add), reads=["lb"], writes=["noml"])
        p.op("dve", lambda e: e.tensor_scalar(out=vec["nbgk"][:], in0=bgkP[:], scalar1=-1.0, scalar2=None, op0=ALU.mult), reads=["smallin"], writes=["nbgk"])

        if debug and debug.get("_stop") == "s0a":
            p.finish()
            return nc
        def mod_pp(j, wm, wkey):
            psv = bank(0)[:, 0:16].rearrange("p (n t) -> p n t", t=2)

            def g(e):
                last = None
                for nchk in range(8):
                    for kc in range(8):
                        last = e.matmul(psv[:, nchk, :], lhsT=wm[:, kc, nchk * 128:(nchk + 1) * 128], rhs=cs[:, kc, :],
                                        start=(kc == 0), stop=(kc == 7))
                return last
            p.op("pe", g, reads=[wkey, "cs0", "cs1"], writes=[pk(0)])
            p.op("dve", lambda e: e.tensor_tensor(out=modP[:, j], in0=psv,
                                                 in1=bmodP[:, j * 8:(j + 1) * 8].unsqueeze(2).broadcast_to([128, 8, 2]), op=ALU.add),
                 reads=[pk(0), "smallin"], writes=[("modP", j)])

        def load_wm(j, wm, wkey):
            p.dma("pool", [(wm[:], wmod_d[:, j * 1024:(j + 1) * 1024].rearrange("(c p) n -> p c n", p=128))], writes=[wkey])

        m_small = ar.lo
        hT = ar.alloc("hT", [128, 8, T], BF16)
        m_persist = ar.lo

        wmb = [ar.alloc("wm%d" % i, [128, 8, 1024], BF16) for i in range(2)]
        xb = [ar.alloc("xb%d" % i, [128, D], F32) for i in range(3)]
        NXN = 8
        xnb = [ar.alloc("xn%d" % i, [128, D], BF16) for i in range(NXN)]
        junk = ar.alloc("junk", [128, D], BF16)
        ssq = ar.alloc("ssq", [128, NT], F32)
        rstd = ar.alloc("rstd", [128, NT], F32)

        load_wm(0, wmb[0], ("wm", 0))
        load_wm(1, wmb[1], ("wm", 1))
        mod_pp(0, wmb[0], ("wm", 0))
        mod_pp(1, wmb[1], ("wm", 1))
        if debug and debug.get("_stop") == "s0b":
            p.finish()
            return nc
        for which, scn, shn in ((0, "sc1x", "sh1x"), (1, "sc1c", "sh1c")):
            p.op("dve", lambda e, which=which, scn=scn: e.scalar_tensor_tensor(out=vec[scn][:], in0=modP[:, 1, :, which], scalar=1.0, in1=npre1[:],
                                                                              op0=ALU.add, op1=ALU.mult),
                 reads=[("modP", 1), "smallin"], writes=[scn])
            p.op("dve", lambda e, which=which, shn=shn: e.tensor_copy(out=vec[shn][:], in_=modP[:, 0, :, which]), reads=[("modP", 0)], writes=[shn])

        def norm_transpose(i, src_ap, srckey, xt, xtkey, xn, xnkey, sc, sh, sckeys, dst, dstkey, col0, ssq_t, rstd_t, load=True, pair=3, phase="all"):
            if phase in ("all", "norm"):
                norm_part(i, src_ap, srckey, xt, xtkey, xn, xnkey, ssq_t, rstd_t, load)
            if phase in ("all", "tr"):
                tr_part(i, xn, xnkey, sc, sh, sckeys, dst, dstkey, col0, pair)

        def norm_part(i, src_ap, srckey, xt, xtkey, xn, xnkey, ssq_t, rstd_t, load):
            if load:
                p.dma("sp", [(xt[:], src_ap)], reads=[srckey] if srckey else [], writes=[xtkey])
            p.op("act", lambda e: e.activation(out=junk[:], in_=xt[:], func=AF.Square, accum_out=ssq_t[:, i:i + 1]),
                 reads=[xtkey], writes=["junk", ("ssq", i)])
            p.op("act", lambda e: e.activation(out=rstd_t[:, i:i + 1], in_=ssq_t[:, i:i + 1], func=AF.Ln, scale=1.0 / D, bias=EPS),
                 reads=[("ssq", i)], writes=[("rstd", i)])
            p.op("act", lambda e: e.activation(out=rstd_t[:, i:i + 1], in_=rstd_t[:, i:i + 1], func=AF.Exp, scale=-0.5),
                 reads=[("rstd", i)], writes=[("rstd", i)])
            p.op("dve", lambda e: e.tensor_scalar(out=xn[:], in0=xt[:], scalar1=rstd_t[:, i:i + 1], scalar2=None, op0=ALU.mult),
                 reads=[xtkey, ("rstd", i)], writes=[xnkey])

        def tr_part(i, xn, xnkey, sc, sh, sckeys, dst, dstkey, col0, pair):
            ptb = PSB[pair][:, :].rearrange("p (k t) -> p k t", t=128)
            pkeys = [pk(2 * pair), pk(2 * pair + 1)]

            def g(e):
                for kc in range(8):
                    e.matmul(ptb[:, kc, :], lhsT=xn[:, kc * 128:(kc + 1) * 128], rhs=identb[:], start=True, stop=True)
            p.op("pe", g, reads=[xnkey, "consts"], writes=pkeys)
            for kk in range(4):
                for kc, eng in ((kk, "act"), (kk + 4, "dve")):
                    o = dst[:, kc, col0:col0 + 128]
                    bkey = [pkeys[0] if kc < 4 else pkeys[1]]
                    if eng == "act":
                        p.op("act", lambda e: e.activation(out=o, in_=ptb[:, kc, :], func=AF.Identity, scale=sc[:, kc:kc + 1], bias=sh[:, kc:kc + 1]),
                             reads=bkey + sckeys, writes=[(dstkey, i, kc)])
                    else:
                        p.op("dve", lambda e: e.tensor_scalar(out=o, in0=ptb[:, kc, :], scalar1=sc[:, kc:kc + 1], scalar2=sh[:, kc:kc + 1],
                                                             op0=ALU.mult, op1=ALU.add),
                             reads=bkey + sckeys, writes=[(dstkey, i, kc)])

        s1_groups = [(0, 2), (2, 4), (6, 4), (10, 4), (14, 4)]

        def s1_norm(i):
            src = ctx_d[i * 128:(i + 1) * 128, :] if i < 2 else x_d[(i - 2) * 128:(i - 1) * 128, :]
            norm_part(i, src, None, xb[i % 3], ("xb", i % 3), xnb[i % NXN], ("xn", i % NXN), ssq, rstd, True)

        def s1_tr(t0, nt):
            sc, sh, keys = (vec["sc1c"], vec["sh1c"], ["sc1c", "sh1c"]) if t0 < 2 else (vec["sc1x"], vec["sh1x"], ["sc1x", "sh1x"])

            def g(e):
                for kc in range(8):
                    for t in range(nt):
                        i = t0 + t
                        e.matmul(bank(kc)[:, t * 128:(t + 1) * 128], lhsT=xnb[i % NXN][:, kc * 128:(kc + 1) * 128], rhs=identb[:],
                                 start=True, stop=True)
            p.op("pe", g, reads=[("xn", (t0 + t) % NXN) for t in range(nt)] + ["consts"], writes=[pk(k_) for k_ in range(8)])
            for kk in range(4):
                for kc, eng in ((kk, "act"), (kk + 4, "dve")):
                    o = hT[:, kc, t0 * 128:(t0 + nt) * 128]
                    srcp = bank(kc)[:, 0:nt * 128]
                    wk_ = [("hT", t0 + t, kc) for t in range(nt)]
                    if eng == "act":
                        p.op("act", lambda e: e.activation(out=o, in_=srcp, func=AF.Identity, scale=sc[:, kc:kc + 1], bias=sh[:, kc:kc + 1]),
                             reads=[pk(kc)] + keys, writes=wk_)
                    else:
                        p.op("dve", lambda e: e.tensor_scalar(out=o, in0=srcp, scalar1=sc[:, kc:kc + 1], scalar2=sh[:, kc:kc + 1],
                                                             op0=ALU.mult, op1=ALU.add),
                             reads=[pk(kc)] + keys, writes=wk_)

        for gi_, (t0_, nt_) in enumerate(s1_groups):
            for i in range(t0_, t0_ + nt_):
                s1_norm(i)
            if gi_ >= 1:
                s1_tr(*s1_groups[gi_ - 1])
        s1_tr(*s1_groups[-1])
        dbg("hT", hT[:, 0, :], [], [128, T])
        if debug and debug.get("_stop") == "s1":
            p.finish()
            return nc
        p.fence()
        ar.lo = m_persist

        def hT_keys(s, n):
            return [("hT", i, kc) for i in range(s // 128, (s + n) // 128) for kc in range(8)]

        wq = ar.alloc("wq", [128, 8, 128], BF16)
        wv = ar.alloc("wv", [128, 8, 128], BF16)
        wa = ar.alloc("wa", [128, 8, 128], BF16)
        wb = ar.alloc("wb", [128, 8, 128], BF16)
        wg = ar.alloc("wg", [128, 8, 128], BF16)
        wlr = ar.alloc("wlr", [128, 8, 32], BF16)
        wgk = ar.alloc("wgk", [16, 2, 512], BF16)
        X1 = ar.alloc("X1", [128, T], F32)
        CUM = ar.alloc("CUM", [128, T], F32)
        Kb = ar.alloc("Kb", [128, T], BF16)
        QT = ar.alloc("QT", [128, T], BF16)
        GT2 = [ar.alloc("GT%d" % i, [128, TL], BF16) for i in range(2)]
        VTM2 = [ar.alloc("VTM%d" % i, [128, NT, 128], BF16) for i in range(2)]
        Qt2 = [[ar.alloc("Qt%d%d" % (i, d), [128, T], BF16) for d in range(2)] for i in range(2)]
        Kt2 = [[ar.alloc("Kt%d%d" % (i, d), [128, T], BF16) for d in range(2)] for i in range(2)]
        smd2 = [[{nm: ar.alloc("%s%d%d" % (nm, i, d), [128, NCH], F32) for nm in ("MID", "DL", "E")} for d in range(2)] for i in range(2)]
        KTM = ar.alloc("KTM", [128, NT, 128], BF16)
        DS = ar.alloc("DS", [128, 128, NCH + 1], F32)
        DBC = ar.alloc("DBC", [128, 16, NCH + 1], F32)
        SPv = [ar.alloc("SPv%d" % d, [128, 128, NCH + 1], BF16) for d in range(2)]
        MS = [ar.alloc("MS%d" % d, [128, 512], BF16) for d in range(2)]
        for d_ in range(2):
            p.op("pool", lambda e: e.memset(MS[d_][:], 0.0), writes=[("MS", d_)])
        SQ = ar.alloc("SQ", [128, 512], BF16)
        Rr = ar.alloc("Rr", [128, 512], F32)
        ON = ar.alloc("ON", [128, 512], F32)
        OBt = [ar.alloc("OBt%d" % i, [128, 512], BF16) for i in range(2)]
        Dsc = ar.alloc("Dsc", [128, NCH + 1], F32)
        tmpA = ar.alloc("tmpA", [128, NCH], F32)
        tmpB = ar.alloc("tmpB", [128, NCH], F32)
        p.op("pool", lambda e: e.memset(Dsc[:], 0.0), writes=["Dsc"])
        p.op("pool", lambda e: e.memset(DS[:, :, 0:1], 0.0), writes=[("DS0",)])
        lrT = ar.alloc("lrT", [16, 2, T], BF16)
        p.dma("pool", [(wlr[:], win_d[:, 4608:4640].rearrange("(c p) n -> p c n", p=128)),
                       (wgk[:], wgk_d.rearrange("d r n -> r d n"))], writes=["wlr", "wgk"])
        GS = [s_ for (s_, n_) in GROUPS]
        X1b = X1[:].bitcast(BF16)
        nheads = 8 if not debug else debug.get("_nheads", [8])[0]

        def wcol(off):
            return win_d[:, off:off + 128].rearrange("(c p) n -> p c n", p=128)

        def head_p1(hh):
            br, hd, hp = hh // 4, hh % 4, hh % 2
            gs = 1.0 if br == 0 else -1.0 / 16.0
            GT, VTM, Qt, Kt = GT2[hp], VTM2[hp], Qt2[hp], Kt2[hp]
            def load_w(h2, which):
                if h2 >= nheads:
                    return
                b2, d2 = h2 // 4, h2 % 4
                if b2 == 0:
                    o2 = dict(q=d2 * 128, v=512 + d2 * 128, a=1024 + d2 * 128, b=1536 + d2 * 128, g=2048 + d2 * 128)
                else:
                    o2 = dict(q=2560 + d2 * 128, a=3072 + d2 * 128, v=3584 + d2 * 128, g=4096 + d2 * 128)
                bufs = dict(q=wq, v=wv, a=wa, b=wb, g=wg)
                for w_ in which:
                    if w_ in o2:
                        p.dma("pool", [(bufs[w_][:], wcol(o2[w_]))], writes=["w" + w_])
            if hh == 0:
                load_w(0, "qvgab")
            yield

            def proj_group(wt, wkey, gi, s, n, evac):
                b = gi % 2
                ps = bank(b)[:, 0:n]

                def g(e):
                    for kc in range(8):
                        e.matmul(ps, lhsT=wt[:, kc, :], rhs=hT[:, kc, s:s + n], start=(kc == 0), stop=(kc == 7))
                p.op("pe", g, reads=[wkey] + hT_keys(s, n), writes=[pk(b)])
                evac(s, n, ps, pk(b))

            if hh == 4:
                for d in range(2):
                    for gi, (s, n) in enumerate(GROUPS):
                        b = gi % 2
                        ps = bank(b)[0:16, 0:n]

                        def g(e):
                            for kc in range(8):
                                e.matmul(ps, lhsT=wlr[:, kc, d * 16:(d + 1) * 16], rhs=hT[:, kc, s:s + n], start=(kc == 0), stop=(kc == 7))
                        p.op("pe", g, reads=["wlr"] + hT_keys(s, n), writes=[pk(b)])
                        p.op("act", lambda e: e.activation(out=lrT[:, d, s:s + n], in_=ps, func=AF.Copy), reads=[pk(b)], writes=[("lrT", d, s)])
                        yield

            def sig_exp(s, n, ps, pkey):
                xs = X1[:, s:s + n]
                p.op("act", lambda e: e.activation(out=xs, in_=ps, func=AF.Exp, scale=-1.0), reads=[pkey], writes=[("X1", s), ("XA", s), ("XB", s)])
                p.op("act", lambda e: e.activation(out=xs, in_=xs, func=AF.Ln, bias=1.0), reads=[("X1", s)], writes=[("X1", s)])
                p.op("act", lambda e: e.activation(out=xs, in_=xs, func=AF.Exp, scale=-1.0), reads=[("X1", s)], writes=[("X1", s)])

            if br == 0:
                def evq(s, n, ps, pkey):
                    sig_exp(s, n, ps, pkey)
                    p.op("dve", lambda e: e.tensor_tensor(out=QT[:, s:s + n], in0=ps, in1=X1[:, s:s + n], op=ALU.mult),
                         reads=[pkey, ("X1", s)], writes=[("QT", s)])
            else:
                def evq(s, n, ps, pkey):
                    p.op("act", lambda e: e.activation(out=QT[:, s:s + n], in_=ps, func=AF.Copy, scale=128.0 ** -0.5), reads=[pkey], writes=[("QT", s)])

            def evg(s, n, ps, pkey):
                sig_exp(s, n, ps, pkey)
                p.op("dve", lambda e: e.tensor_tensor(out=GT[:, s - TC:s - TC + n], in0=ps, in1=X1[:, s:s + n], op=ALU.mult),
                     reads=[pkey, ("X1", s)], writes=[("GT", hp, s)])

            def v_group(i0):
                nt4 = min(4, NT - i0)
                psv = bank(2)

                def g(e):
                    for tl in range(nt4):
                        i = i0 + tl
                        for kc in range(8):
                            e.matmul(psv[:, tl * 128:(tl + 1) * 128], lhsT=hT[:, kc, i * 128:(i + 1) * 128], rhs=wv[:, kc, :],
                                     start=(kc == 0), stop=(kc == 7))
                p.op("pe", g, reads=["wv"] + hT_keys(i0 * 128, nt4 * 128), writes=[pk(2)])
                p.op("act", lambda e: e.activation(out=VTM[:, i0:i0 + nt4, :], in_=psv[:, 0:nt4 * 128].rearrange("p (t v) -> p t v", v=128), func=AF.Copy),
                     reads=[pk(2)], writes=[("VTM", hp, i0)])

            for gi, (s, n) in enumerate(GROUPS):
                if gi >= 1:
                    proj_group(wq, "wq", gi, s, n, evq)
                    proj_group(wg, "wg", gi + 1, s, n, evg)
                v_group(gi * 4)
                yield
            if br == 1:
                def evk(s, n, ps, pkey):
                    p.op("act", lambda e: e.activation(out=Kb[:, s:s + n], in_=ps, func=AF.Copy), reads=[pkey], writes=[("Kb", s)])
                for gi, (s, n) in enumerate(GROUPS):
                    proj_group(wa, "wa", gi, s, n, evk)
                    yield
            load_w(hh + 1, "qvg" if br == 0 else "qvga")

            for d in range(2):
                if d == 1:
                    yield "SPLIT"
                dh = d * 4 + hd
                smd = smd2[hp][d]
                pmid, pend = (31, 63) if d == 0 else (32, 0)
                chains = []
                for gi, (s, n) in enumerate(GROUPS):
                    c0, ncg = s // 64, n // 64
                    b = gi % 2
                    ps = bank(b)[:, 0:n]
                    xs = X1[:, s:s + n]
                    cu = CUM[:, s:s + n]
                    c3 = cu.rearrange("p (c l) -> p c l", l=64)
                    x3 = xs.rearrange("p (c l) -> p c l", l=64)
                    kX, kC, kK = ("X1", s), ("CUM", s), ("Kb", s)
                    if br == 0:
                        wt, wkey = (wa, "wa") if d == 0 else (wb, "wb")

                        def e0(s=s, n=n, ps=ps, b=b, wt=wt, wkey=wkey, xs=xs, kX=kX):
                            def g(e):
                                for kc in range(8):
                                    e.matmul(ps, lhsT=wt[:, kc, :], rhs=hT[:, kc, s:s + n], start=(kc == 0), stop=(kc == 7))
                            p.op("pe", g, reads=[wkey] + hT_keys(s, n), writes=[pk(b)])
                            p.op("act", lambda e: e.activation(out=xs, in_=ps, func=AF.Exp, scale=-1.0), reads=[pk(b)], writes=[kX, ("XA", s), ("XB", s)])

                        def e1(s=s, n=n, xs=xs, cu=cu, kX=kX, kK=kK, kC=kC):
                            p.op("act", lambda e: e.activation(out=cu, in_=xs, func=AF.Ln, bias=1.0), reads=[kX], writes=[kC])
                            p.op("act", lambda e: e.activation(out=xs, in_=cu, func=AF.Exp, scale=-1.0), reads=[kC], writes=[kX])
                            p.op("pool", lambda e: e.tensor_scalar(out=Kb[:, s:s + n], in0=xs, scalar1=vec["noml"][:, dh:dh + 1],
                                                                  scalar2=vec["oml"][:, dh:dh + 1], op0=ALU.mult, op1=ALU.add),
                                 reads=[kX, "noml", "oml"], writes=[kK])
                            p.op("act", lambda e: e.activation(out=xs, in_=xs, func=AF.Ln, scale=vec["oml"][:, dh:dh + 1], bias=vec["lb"][:, dh:dh + 1]),
                                 reads=[kX, "oml", "lb"], writes=[kX])
                    else:
                        def e0(s=s, n=n, ps=ps, b=b, xs=xs, kX=kX):
                            p.op("pe", lambda e: e.matmul(ps, lhsT=wgk[:, d, hd * 128:(hd + 1) * 128], rhs=lrT[:, d, s:s + n], start=True, stop=True),
                                 reads=["wgk", ("lrT", d, s)], writes=[pk(b)])
                            p.op("act", lambda e: e.activation(out=xs, in_=ps, func=AF.Exp, scale=-1.0, bias=vec["nbgk"][:, dh:dh + 1]),
                                 reads=[pk(b), "nbgk"], writes=[kX, ("XA", s), ("XB", s)])

                        def e1(xs=xs, kX=kX):
                            p.op("act", lambda e: e.activation(out=xs, in_=xs, func=AF.Ln, bias=1.0), reads=[kX], writes=[kX])

                    def e2(s=s, n=n, xs=xs, cu=cu, kX=kX, kC=kC):
                        if d == 0:
                            p.op("dve", lambda e: e.tensor_tensor_scan(out=cu, data0=cm[:, 0:n], data1=xs, initial=0.0, op0=ALU.mult, op1=ALU.add),
                                 reads=[kX, "consts"], writes=[kC])
                        else:
                            p.op("dve", lambda e: e.tensor_tensor_scan(out=cu[:, ::-1], data0=cm[:, 0:n], data1=xs[:, ::-1], initial=0.0,
                                                                      op0=ALU.mult, op1=ALU.add),
                                 reads=[kX, "consts"], writes=[kC])

                    def e3(s=s, c0=c0, ncg=ncg, c3=c3, kC=kC):
                        p.op("pool", lambda e: e.tensor_copy(out=smd["MID"][:, c0:c0 + ncg], in_=c3[:, :, pmid]), reads=[kC], writes=[("MID", hp, d, s)])
                        p.op("pool", lambda e: e.tensor_copy(out=smd["DL"][:, c0:c0 + ncg], in_=c3[:, :, pend]), reads=[kC], writes=[("DL", hp, d, s)])

                    def e4(s=s, c0=c0, ncg=ncg, c3=c3, kC=kC):
                        p.op("dve", lambda e: e.tensor_tensor(out=c3, in0=c3, in1=smd["MID"][:, c0:c0 + ncg].unsqueeze(2).broadcast_to([128, ncg, 64]),
                                                             op=ALU.subtract),
                             reads=[kC, ("MID", hp, d, s)], writes=[kC])

                    xa = X1b[:, 2 * s:2 * s + n]
                    xb_ = X1b[:, 2 * s + n:2 * s + 2 * n]

                    def e5(gi=gi, s=s, cu=cu, xa=xa, kX=kX, kC=kC):
                        if gi >= 1:
                            p.op("act", lambda e: e.activation(out=xa, in_=cu, func=AF.Exp, scale=gs), reads=[kC, kX], writes=[("XA", s)])

                    def e6(gi=gi, s=s, n=n, xa=xa):
                        if gi >= 1:
                            p.op("dve", lambda e: e.tensor_tensor(out=Qt[d][:, s:s + n], in0=QT[:, s:s + n], in1=xa, op=ALU.mult),
                                 reads=[("XA", s), ("QT", s)], writes=[("Qt", hp, d, s)])

                    def e7(s=s, cu=cu, xb_=xb_, kX=kX, kC=kC):
                        p.op("act", lambda e: e.activation(out=xb_, in_=cu, func=AF.Exp, scale=-gs), reads=[kC, kX], writes=[("XB", s)])

                    def e8(s=s, n=n, xb_=xb_, kK=kK):
                        p.op("dve", lambda e: e.tensor_tensor(out=Kt[d][:, s:s + n], in0=Kb[:, s:s + n], in1=xb_, op=ALU.mult),
                             reads=[("XB", s), kK], writes=[("Kt", hp, d, s)])
                    chains.append([e0, e1, e2, e3, e4, e5, e6, e7, e8])
                nel = len(chains[0])
                for step in range(nel + len(chains) - 1):
                    for gi in range(len(chains)):
                        k = step - gi
                        if 0 <= k < nel:
                            chains[gi][k]()
                    yield
            if br == 0:
                load_w(hh + 1, "ab")

        def head_p2(hh):
            br, hd, hp = hh // 4, hh % 4, hh % 2
            gs = 1.0 if br == 0 else -1.0 / 16.0
            GT, VTM, Qt, Kt = GT2[hp], VTM2[hp], Qt2[hp], Kt2[hp]
            allk = lambda nm, d: [(nm, hp, d, s_) for s_ in GS]
            for d in range(2):
                if d == 1:
                    yield "SPLIT"
                smd = smd2[hp][d]
                MIDt, DLt = smd["MID"], smd["DL"]
                p.op("pool", lambda e: e.tensor_tensor(out=tmpA[:], in0=DLt[:], in1=MIDt[:], op=ALU.subtract),
                     reads=allk("DL", d) + allk("MID", d), writes=["tmpA"])
                if d == 0:
                    p.op("pool", lambda e: e.tensor_tensor(out=tmpB[:, 0:35], in0=MIDt[:, 1:36], in1=tmpA[:, 0:35], op=ALU.add),
                         reads=["tmpA"] + allk("MID", d), writes=["tmpB"])
                    p.op("act", lambda e: e.activation(out=Dsc[:, 1:36], in_=tmpB[:, 0:35], func=AF.Exp, scale=gs), reads=["tmpB"], writes=["Dsc"])
                else:
                    p.op("pool", lambda e: e.tensor_tensor(out=tmpB[:, 1:36], in0=MIDt[:, 0:35], in1=tmpA[:, 1:36], op=ALU.add),
                         reads=["tmpA"] + allk("MID", d), writes=["tmpB"])
                    p.op("pool", lambda e: e.tensor_tensor(out=tmpB[:, 0:1], in0=MIDt[:, 35:36], in1=tmpA[:, 0:1], op=ALU.add),
                         reads=["tmpA", "tmpB"] + allk("MID", d), writes=["tmpB"])
                    p.op("act", lambda e: e.activation(out=Dsc[:, 1:5], in_=tmpB[:, 3::-1], func=AF.Exp, scale=gs), reads=["tmpB"], writes=["Dsc"])
                    p.op("act", lambda e: e.activation(out=Dsc[:, 5:36], in_=tmpB[:, 35:4:-1], func=AF.Exp, scale=gs), reads=["tmpB", "Dsc"], writes=["Dsc"])
                p.op("pool", lambda e: e.tensor_copy(out=DBC[:], in_=Dsc[:].unsqueeze(1).broadcast_to([128, 16, NCH + 1])), reads=["Dsc"], writes=["DBC"])
                yield
                for i0 in range(0, NT, 4):
                    nt4 = min(4, NT - i0)
                    b = 3 + (i0 // 4) % 2
                    psk = bank(b)

                    def g(e):
                        for tl in range(nt4):
                            i = i0 + tl
                            e.matmul(psk[:, tl * 128:(tl + 1) * 128], lhsT=Kt[d][:, i * 128:(i + 1) * 128], rhs=identb[:], start=True, stop=True)
                    p.op("pe", g, reads=allk("Kt", d) + ["consts"], writes=[pk(b)])
                    p.op("act", lambda e: e.activation(out=KTM[:, i0:i0 + nt4, :], in_=psk[:, 0:nt4 * 128].rearrange("p (t v) -> p t v", v=128), func=AF.Copy),
                         reads=[pk(b)], writes=[("KTM", i0)])
                    yield
                for si, (ti0, ntl) in enumerate(((0, 2), (2, 4), (6, 4), (10, 4), (14, 4))):
                    pp = PSB[2]
                    for half in range(2):
                        psd = bank(4 + half)

                        def g(e):
                            for m in range(ntl):
                                i = ti0 + m
                                e.matmul(psd[:, m * 128:(m + 1) * 128], lhsT=KTM[half * 64:(half + 1) * 64, i, :],
                                         rhs=VTM[half * 64:(half + 1) * 64, i, :], start=True, stop=True)
                        p.op("pe", g, reads=[("KTM", i0_) for i0_ in range(0, NT, 4)] + [("VTM", hp, i0_) for i0_ in range(0, NT, 4)],
                             writes=[pk(4 + half)])
                    c_lo = 2 * ti0
                    if d == 0:
                        dsv = DS[:, :, 1 + c_lo:1 + c_lo + 2 * ntl]
                    else:
                        jst = (4 if ti0 == 0 else 40) - c_lo
                        dsv = DS[:, :, jst:jst - 2 * ntl:-1]
                    src4 = pp[:, :].rearrange("p (h c) -> p h c", h=2)[:, :, 0:ntl * 128].rearrange("p h (m v) -> p h m v", v=128)
                    p.op("act", lambda e: e.activation(out=dsv.rearrange("p v (m h) -> p h m v", h=2), in_=src4, func=AF.Copy),
                         reads=[pk(4), pk(5)], writes=[("DS", si, 0), ("DS", si, 1)])
                    yield
                dsk = [("DS", si, half) for si in range(5) for half in range(2)]
                for qv in range(4):
                    for hv in range(2):
                        v0 = qv * 32 + hv * 16
                        v2 = DS[:, v0:v0 + 16, :].rearrange("p v j -> p (v j)")
                        o2 = SPv[d][:, v0:v0 + 16, :].rearrange("p v j -> p (v j)")
                        p.op("dve", lambda e: e.tensor_tensor_scan(out=o2, data0=v2, data1=DBC[:].rearrange("p v j -> p (v j)"), initial=0.0,
                                                                  op0=ALU.add, op1=ALU.mult),
                             reads=dsk + ["DBC", ("DS0",)], writes=[("SP", d, qv, hv)])
                    yield
            spk = [("SP", d_, q_, h_) for d_ in range(2) for q_ in range(4) for h_ in range(2)]
            chains = []
            for gl in range(4):
                s0 = 256 + gl * 512
                otb = 5 + gl % 2
                pot = bank(otb)
                obt = OBt[gl % 2]

                def f0(gl=gl, s0=s0):
                    for d in range(2):
                        psc = bank(3 + d)

                        def g(e):
                            for tl in range(4):
                                i = 2 + 4 * gl + tl
                                e.matmul(psc[:, tl * 128:(tl + 1) * 128], lhsT=Kt[d][:, i * 128:(i + 1) * 128], rhs=Qt[d][:, i * 128:(i + 1) * 128],
                                         start=True, stop=True)
                        p.op("pe", g, reads=[("Kt", hp, d, s0), ("Qt", hp, d, s0)], writes=[pk(3 + d)])
                        p.op("dve", lambda e: e.copy_predicated(out=MS[d][:], mask=(maskF if d == 0 else maskB)[:].bitcast(mybir.dt.uint16), data=psc),
                             reads=[pk(3 + d), "consts"], writes=[("MS", d)])

                def f1(gl=gl, s0=s0, otb=otb, pot=pot):
                    def g(e):
                        for tl in range(4):
                            i = 2 + 4 * gl + tl
                            cols = slice(tl * 128, (tl + 1) * 128)
                            e.matmul(pot[:, cols], lhsT=VTM[:, i, :], rhs=MS[0][:, cols], start=True, stop=False)
                            e.matmul(pot[:, cols], lhsT=VTM[:, i, :], rhs=MS[1][:, cols], start=False, stop=False)
                            for half in range(2):
                                c = 2 * i + half
                                cc_ = slice(tl * 128 + half * 64, tl * 128 + half * 64 + 64)
                                e.matmul(pot[:, cc_], lhsT=SPv[0][:, :, c], rhs=Qt[0][:, c * 64:(c + 1) * 64], start=False, stop=False)
                                e.matmul(pot[:, cc_], lhsT=SPv[1][:, :, 39 - c], rhs=Qt[1][:, c * 64:(c + 1) * 64], start=False, stop=True)
                    p.op("pe", g, reads=[("MS", 0), ("MS", 1), ("Qt", hp, 0, s0), ("Qt", hp, 1, s0)] + spk + [("VTM", hp, i0_) for i0_ in range(0, NT, 4)],
                         writes=[pk(otb)])

                def f2(otb=otb, pot=pot):
                    p.op("act", lambda e: e.activation(out=SQ[:], in_=pot, func=AF.Square), reads=[pk(otb)], writes=["SQ"])
                    p.op("pe", lambda e: e.matmul(bank(7), lhsT=onesb[:], rhs=SQ[:], start=True, stop=True), reads=["SQ", "consts"], writes=[pk(7)])
                    p.op("act", lambda e: e.activation(out=Rr[:], in_=bank(7), func=AF.Ln, scale=1.0 / 128, bias=EPS), reads=[pk(7)], writes=["Rr"])
                    p.op("act", lambda e: e.activation(out=Rr[:], in_=Rr[:], func=AF.Exp, scale=-0.5), reads=["Rr"], writes=["Rr"])

                def f3(gl=gl, s0=s0, otb=otb, pot=pot, obt=obt):
                    p.op("dve", lambda e: e.scalar_tensor_tensor(out=ON[:], in0=pot, scalar=onormP[:, br:br + 1], in1=Rr[:], op0=ALU.mult, op1=ALU.mult),
                         reads=[pk(otb), "Rr", "smallin"], writes=["ON"])
                    p.op("pool", lambda e: e.tensor_tensor(out=obt[:], in0=ON[:], in1=GT[:, gl * 512:(gl + 1) * 512], op=ALU.mult),
                         reads=["ON", ("GT", hp, s0)], writes=[("OBt", gl % 2)])
                    p.dma("sp", [(ob_d[hh, :, gl * 512:(gl + 1) * 512], obt[:])], reads=[("OBt", gl % 2)], writes=[("obd", hh, gl)])
                chains.append([f0, f1, f2, f3])
            nel = 4
            for step in range(nel + len(chains) - 1):
                for gl in range(len(chains)):
                    k = step - gl
                    if 0 <= k < nel:
                        chains[gl][k]()
                yield

        def co_run(ga, gb, stop_a, stop_b):
            act = [ga is not None, gb is not None]
            gens = [ga, gb]
            stops = [stop_a, stop_b]
            while act[0] or act[1]:
                for k in range(2):
                    if not act[k]:
                        continue
                    try:
                        v = next(gens[k])
                    except StopIteration:
                        act[k] = False
                        continue
                    if v == "SPLIT" and stops[k]:
                        act[k] = False

        _lo_s2 = ar.lo
        ar.lo = m_persist
        OB = ar.alloc("OB", [128, 8, TL], BF16)
        w3 = []
        for i in range(2):
            w3.append(dict(gh=ar.alloc("wgh%d" % i, [128, 8, 128], BF16), gg=ar.alloc("wgg%d" % i, [128, 8, 128], BF16),
                           bh=ar.alloc("wbh%d" % i, [128, 4, 128], BF16), bg=ar.alloc("wbg%d" % i, [128, 4, 128], BF16)))
            if i == 0:
                assert ar.lo <= m_persist + 5 * 2048 + 512 + 2048 + 2 * 9216 + 2 * 4608 + 4096
        lo_after_w3 = ar.lo
        ar.lo = _lo_s2

        def load_w3(nn, deps=()):
            if nn >= 8:
                return
            w = w3[nn % 2]
            p.dma("pool", [(w["gh"][:], wcol(4640 + nn * 128)), (w["gg"][:], wcol(5664 + nn * 128)),
                           (w["bh"][:], wbrh_d[:, nn * 128:(nn + 1) * 128].rearrange("(h p) n -> p h n", p=128)),
                           (w["bg"][:], wbrg_d[:, nn * 128:(nn + 1) * 128].rearrange("(h p) n -> p h n", p=128))], writes=[("w3", nn % 2)], deps=deps)

        def load_ob(h_, deps=()):
            for gl_ in range(4):
                p.dma("sp", [(OB[:, h_, gl_ * 512:(gl_ + 1) * 512], ob_d[h_, :, gl_ * 512:(gl_ + 1) * 512])],
                      reads=[("obd", h_, gl_)], writes=[("OBl", h_, gl_)], deps=deps)

        g2 = None
        for hh in range(nheads):
            g1 = head_p1(hh)
            co_run(g1, g2, True, False)
            g2 = head_p2(hh)
            co_run(g1, g2, False, True)
        snap = [("s_" + e_, p.cnt[e_]) for e_ in ENGS if p.cnt[e_] > 0] + \
               [("r%d" % i_, p.ring_total[i_]) for i_ in range(NRING) if p.ring_total[i_] > 0]
        for h_ in range(nheads - 1):
            load_ob(h_, deps=snap)
        load_w3(0, deps=snap)
        co_run(None, g2, False, False)
        if debug and debug.get("_stop") == "s2":
            p.finish()
            return nc
        p.fence()
        ar.lo = m_persist

        load_ob(nheads - 1)
        ar.lo = lo_after_w3
        grow = [ar.alloc_top("grow%d" % i, [128, D], F32) for i in range(2)]
        hi_grow = ar.hi
        MT = ar.alloc_top("MT", [128, 8, TL], BF16)
        wout = ar.alloc_top("wout", [128, 8, D], BF16)
        S12 = [ar.alloc("S12_%d" % i, [128, 512], F32) for i in range(2)]
        M12 = [ar.alloc("M12_%d" % i, [128, 512], F32) for i in range(2)]
        wmr = ar.alloc("wmr", [128, 8, 1024], BF16)
        brow = ar.alloc("brow", [128, D], F32)
        nrow = ar.alloc("nrow", [128, D], F32)
        woutk = [("wout", kc) for kc in range(8)]
        growk = [[("grow", gi_, 0), ("grow", gi_, 1)] for gi_ in range(2)]

        def load_wmr(j):
            p.dma("pool", [(wmr[:], wmod_d[:, j * 1024:(j + 1) * 1024].rearrange("(c p) n -> p c n", p=128))], writes=["wmr"])

        def gate_row(gi_, j, npost_d):
            p.dma("sp", [(brow[:], bmodR_d[:, j * 1024:(j + 1) * 1024].partition_broadcast(128)),
                         (nrow[:], npost_d.partition_broadcast(128))], writes=["brow"])
            for half in range(2):
                psr = bank(half)

                def g(e, psr=psr, half=half):
                    for kc in range(8):
                        e.matmul(psr, lhsT=csrep[:, kc, :], rhs=wmr[:, kc, half * 512:(half + 1) * 512], start=(kc == 0), stop=(kc == 7))
                p.op("pe", g, reads=["wmr"], writes=[pk(half)])
                p.op("dve", lambda e, psr=psr, half=half: e.tensor_tensor(out=grow[gi_][:, half * 512:(half + 1) * 512], in0=psr,
                                                                         in1=brow[:, half * 512:(half + 1) * 512], op=ALU.add),
                     reads=[pk(half), "brow"], writes=[("grow", gi_, half)])
            p.op("pool", lambda e: e.tensor_tensor(out=grow[gi_][:], in0=grow[gi_][:], in1=nrow[:], op=ALU.mult),
                 reads=[("grow", gi_, 0), ("grow", gi_, 1), "brow"], writes=[("grow", gi_, 0), ("grow", gi_, 1)])

        def side_work(nn):
            if nn == 0:
                for kc in range(8):
                    p.dma("pool", [(wout[:, kc, :], wout_d[kc * 128:(kc + 1) * 128, :])], writes=[("wout", kc)])
                mod_pp(3, wmr, "wmr")
                load_wmr(4)
            elif nn == 1:
                mod_pp(4, wmr, "wmr")
                load_wmr(2)
                p.op("dve", lambda e: e.scalar_tensor_tensor(out=vec["sc2x"][:], in0=modP[:, 4, :, 0], scalar=1.0, in1=npre2[:], op0=ALU.add, op1=ALU.mult),
                     reads=[("modP", 4), "smallin"], writes=["sc2x"])
                p.op("dve", lambda e: e.tensor_copy(out=vec["sh2x"][:], in_=modP[:, 3, :, 0]), reads=[("modP", 3)], writes=["sh2x"])
            elif nn == 2:
                gate_row(0, 2, npost1_d)
                load_wmr(5)
            elif nn == 3:
                gate_row(1, 5, npost2_d)

        load_w3(1)
        load_wmr(3)
        itn = 0
        for nn in range(8):
            w = w3[nn % 2]
            wk = ("w3", nn % 2)
            for gl in range(4):
                s = gl * 512
                for half in range(2):
                    bset = (itn % 3) * 2
                    itn += 1
                    pg_, pb_ = bank(bset), bank(bset + 1)
                    wgt, wbr = (w["gh"], w["bh"]) if half == 0 else (w["gg"], w["bg"])

                    def g(e, pg_=pg_, pb_=pb_, wgt=wgt, wbr=wbr, s=s, half=half):
                        for kc in range(8):
                            e.matmul(pg_, lhsT=wgt[:, kc, :], rhs=hT[:, kc, TC + s:TC + s + 512], start=(kc == 0), stop=(kc == 7))
                        for h in range(4):
                            e.matmul(pb_, lhsT=wbr[:, h, :], rhs=OB[:, half * 4 + h, s:s + 512], start=(h == 0), stop=(h == 3))
                    p.op("pe", g, reads=[wk] + [("OBl", half * 4 + h_, gl) for h_ in range(4)], writes=[pk(bset), pk(bset + 1)])
                    p.op("act", lambda e, pg_=pg_, half=half: e.activation(out=S12[half][:], in_=pg_, func=AF.Sigmoid),
                         reads=[pk(bset)], writes=[("S12", half)])
                    p.op("dve", lambda e, pb_=pb_, half=half: e.tensor_tensor(out=M12[half][:], in0=pb_, in1=S12[half][:], op=ALU.mult),
                         reads=[pk(bset + 1), ("S12", half)], writes=[("M12", half)])
                p.op("pool", lambda e, nn=nn, s=s: e.tensor_tensor(out=MT[:, nn, s:s + 512], in0=M12[0][:], in1=M12[1][:], op=ALU.add),
                     reads=[("M12", 0), ("M12", 1)], writes=[("MT", nn, gl)])
            load_w3(nn + 2)
            side_work(nn)
        dbg("MT", MT[:, 0, :], [], [128, TL])
        if debug and debug.get("_stop") == "3a":
            p.finish()
            return nc
        p.fence()
        ar.lo = m_small

        h2T = ar.alloc("h2T", [128, 8, TL], BF16)
        wdn = ar.alloc("wdn", [128, FC, D], BF16)
        m3 = ar.lo
        for fc in range(FC):
            p.dma("pool", [(wdn[:, fc, :], wfd_d[fc * 128:(fc + 1) * 128, :])], writes=[("wdn", fc)])
        NB3 = 3
        xb = [ar.alloc("xb3_%d" % i, [128, D], F32) for i in range(NB3)]
        T1 = [ar.alloc("T1_%d" % i, [128, D], F32) for i in range(NB3)]
        Z1 = [ar.alloc("Z1_%d" % i, [128, D], F32) for i in range(NB3)]
        XN = [ar.alloc("XN_%d" % i, [128, D], BF16) for i in range(NB3)]
        junk = ar.alloc("junk3", [128, D], BF16)
        ss1 = ar.alloc("ss1", [128, 16], F32)
        r1 = ar.alloc("r1", [128, 16], F32)
        ss2 = ar.alloc("ss2", [128, 16], F32)
        r2 = ar.alloc("r2", [128, 16], F32)
        def tile_a(i):
            yp = PSB[i % 3]
            ypk = [pk(2 * (i % 3)), pk(2 * (i % 3) + 1)]

            def g(e, i=i, yp=yp):
                last = None
                for half in range(2):
                    for kc in range(8):
                        last = e.matmul(yp[:, half * 512:(half + 1) * 512], lhsT=MT[:, kc, i * 128:(i + 1) * 128],
                                        rhs=wout[:, kc, half * 512:(half + 1) * 512], start=(kc == 0), stop=(kc == 7))
                return last
            p.op("pe", g, reads=woutk, writes=ypk)
            p.op("act", lambda e, i=i, yp=yp: e.activation(out=junk[:], in_=yp[:, :], func=AF.Square, accum_out=ss1[:, i:i + 1]),
                 reads=ypk, writes=["junk", ("ss1", i)])
            p.op("act", lambda e, i=i: e.activation(out=r1[:, i:i + 1], in_=ss1[:, i:i + 1], func=AF.Ln, scale=1.0 / D, bias=EPS),
                 reads=[("ss1", i)], writes=[("r1", i)])
            p.op("act", lambda e, i=i: e.activation(out=r1[:, i:i + 1], in_=r1[:, i:i + 1], func=AF.Exp, scale=-0.5),
                 reads=[("r1", i)], writes=[("r1", i)])
            p.op("dve", lambda e, i=i, yp=yp: e.scalar_tensor_tensor(out=T1[i % NB3][:], in0=yp[:, :], scalar=r1[:, i:i + 1], in1=grow[0][:],
                                                                   op0=ALU.mult, op1=ALU.mult),
                 reads=ypk + [("r1", i)] + growk[0], writes=[("T1", i % NB3)])

        def tile_a2(i):
            p.op("pool", lambda e, i=i: e.tensor_tensor(out=Z1[i % NB3][:], in0=T1[i % NB3][:], in1=xb[i % NB3][:], op=ALU.add),
                 reads=[("T1", i % NB3), ("xb", i % NB3)], writes=[("Z1", i % NB3)])
            p.dma("pool", [(z1_d[i * 128:(i + 1) * 128, :], Z1[i % NB3][:])], reads=[("Z1", i % NB3)], writes=[("z1d", i)])
            norm_transpose(i, None, None, Z1[i % NB3], ("Z1", i % NB3), XN[i % NB3], ("XN", i % NB3), vec["sc2x"], vec["sh2x"], ["sc2x", "sh2x"],
                           h2T, "h2T", i * 128, ss2, r2, load=False, pair=3, phase="norm")

        def tile_b(i):
            norm_transpose(i, None, None, Z1[i % NB3], ("Z1", i % NB3), XN[i % NB3], ("XN", i % NB3), vec["sc2x"], vec["sh2x"], ["sc2x", "sh2x"],
                           h2T, "h2T", i * 128, ss2, r2, load=False, pair=3, phase="tr")

        for i_ in range(2):
            p.dma("sp", [(xb[i_][:], x_d[i_ * 128:(i_ + 1) * 128, :])], writes=[("xb", i_)])
        for i in range(18):
            if i < 16:
                tile_a(i)
            if 1 <= i < 17:
                tile_a2(i - 1)
            if i + 2 < 16:
                p.dma("sp", [(xb[(i + 2) % NB3][:], x_d[(i + 2) * 128:(i + 3) * 128, :])], writes=[("xb", (i + 2) % NB3)])
            if i >= 2:
                tile_b(i - 2)
        dbg("h2T", h2T[:, 0, :], [], [128, TL])
        if debug and debug.get("_stop") == "3b":
            p.finish()
            return nc
        p.fence()
        ar.lo = m3
        ar.hi = hi_grow

        HID = ar.alloc("HID", [128, FC, 1024], BF16)
        wfg = [ar.alloc("wfg%d" % i, [128, 8, 128], BF16) for i in range(3)]
        wfu = [ar.alloc("wfu%d" % i, [128, 8, 128], BF16) for i in range(3)]
        SG = [ar.alloc("SG%d" % i, [128, 512], F32) for i in range(2)]
        T2 = [ar.alloc("T2_%d" % i, [128, D], F32) for i in range(2)]
        zt = [ar.alloc("zt%d" % i, [128, D], F32) for i in range(3)]
        OT_ = [ar.alloc("OT%d" % i, [128, D], F32) for i in range(2)]
        junk = ar.alloc("junk4", [128, D], BF16)
        ss3 = ar.alloc("ss3", [128, 16], F32)
        r3 = ar.alloc("r3", [128, 16], F32)
        wdnk = [("wdn", fc) for fc in range(FC)]
        itn = 0
        for grp in range(2):
            for fc in range(FC):
                wi = fc % 3
                p.dma("pool", [(wfg[wi][:], wfg_d[:, fc * 128:(fc + 1) * 128].rearrange("(c p) n -> p c n", p=128)),
                               (wfu[wi][:], wfu_d[:, fc * 128:(fc + 1) * 128].rearrange("(c p) n -> p c n", p=128))], writes=[("wf", wi)])
                for sub in range(2):
                    t0 = grp * 1024 + sub * 512
                    bset = (itn % 3) * 2
                    itn += 1
                    pg_, pu_ = bank(bset), bank(bset + 1)

                    def g(e, pg_=pg_, pu_=pu_, wi=wi, t0=t0):
                        last = None
                        for kc in range(8):
                            e.matmul(pg_, lhsT=wfg[wi][:, kc, :], rhs=h2T[:, kc, t0:t0 + 512], start=(kc == 0), stop=(kc == 7))
                        for kc in range(8):
                            last = e.matmul(pu_, lhsT=wfu[wi][:, kc, :], rhs=h2T[:, kc, t0:t0 + 512], start=(kc == 0), stop=(kc == 7))
                        return last
                    p.op("pe", g, reads=[("wf", wi)], writes=[pk(bset), pk(bset + 1)])
                    p.op("act", lambda e, pg_=pg_, sub=sub: e.activation(out=SG[sub][:], in_=pg_, func=AF.Silu), reads=[pk(bset)], writes=[("SG", sub)])
                    p.op("dve", lambda e, pu_=pu_, sub=sub, fc=fc: e.tensor_tensor(out=HID[:, fc, sub * 512:(sub + 1) * 512], in0=pu_, in1=SG[sub][:], op=ALU.mult),
                         reads=[pk(bset + 1), ("SG", sub)], writes=[("HID", fc, sub)])
            for i_ in range(grp * 8, grp * 8 + 2):
                p.dma("sp", [(zt[i_ % 3][:], z1_d[i_ * 128:(i_ + 1) * 128, :])], writes=[("zt", i_ % 3)])
            for tt in range(8):
                i = grp * 8 + tt
                yp = PSB[i % 2]
                ypk = [pk(2 * (i % 2)), pk(2 * (i % 2) + 1)]

                def g(e, tt=tt, yp=yp):
                    last = None
                    for half in range(2):
                        for fc in range(FC):
                            last = e.matmul(yp[:, half * 512:(half + 1) * 512], lhsT=HID[:, fc, tt * 128:(tt + 1) * 128],
                                            rhs=wdn[:, fc, half * 512:(half + 1) * 512], start=(fc == 0), stop=(fc == FC - 1))
                    return last
                p.op("pe", g, reads=wdnk + [("HID", fc, tt // 4) for fc in range(FC)], writes=ypk)
                p.op("act", lambda e, i=i, yp=yp: e.activation(out=junk[:], in_=yp[:, :], func=AF.Square, accum_out=ss3[:, i:i + 1]),
                     reads=ypk, writes=["junk", ("ss3", i)])
                p.op("act", lambda e, i=i: e.activation(out=r3[:, i:i + 1], in_=ss3[:, i:i + 1], func=AF.Ln, scale=1.0 / D, bias=EPS),
                     reads=[("ss3", i)], writes=[("r3", i)])
                p.op("act", lambda e, i=i: e.activation(out=r3[:, i:i + 1], in_=r3[:, i:i + 1], func=AF.Exp, scale=-0.5),
                     reads=[("r3", i)], writes=[("r3", i)])
                p.op("dve", lambda e, i=i, yp=yp: e.scalar_tensor_tensor(out=T2[i % 2][:], in0=yp[:, :], scalar=r3[:, i:i + 1], in1=grow[1][:],
                                                                       op0=ALU.mult, op1=ALU.mult),
                     reads=ypk + [("r3", i)], writes=[("T2", i % 2)])
                if tt + 2 < 8:
                    p.dma("sp", [(zt[(i + 2) % 3][:], z1_d[(i + 2) * 128:(i + 3) * 128, :])], writes=[("zt", (i + 2) % 3)])
                p.op("pool", lambda e, i=i: e.tensor_tensor(out=OT_[i % 2][:], in0=T2[i % 2][:], in1=zt[i % 3][:], op=ALU.add),
                     reads=[("T2", i % 2), ("zt", i % 3)], writes=[("OTo", i % 2)])
                p.dma("pool", [(out_d[i * 128:(i + 1) * 128, :], OT_[i % 2][:])], reads=[("OTo", i % 2)], writes=[("outd", i)])
        p.finish()
    return nc


_NC_CACHE = {}


def _host_inputs(inputs):
    f32 = np.float32
    g = lambda k: np.asarray(inputs[k], dtype=f32)
    x, c, ctx, c_ctx = g("x"), g("c"), g("ctx"), g("c_ctx")
    col = lambda v: np.ascontiguousarray(v.reshape(-1, 128).T)
    shared = {
        "cctx": col(c_ctx),
        "w_mod": np.ascontiguousarray(g("w_mod")[0]),
        "bmodP": col(g("b_mod")[0]),
        "bmodR": np.ascontiguousarray(g("b_mod")[0].reshape(1, -1)),
        "npre1P": col(g("norm_pre1")[0]),
        "npre2P": col(g("norm_pre2")[0]),
        "npost1R": np.ascontiguousarray(g("norm_post1")[0].reshape(1, -1)),
        "npost2R": np.ascontiguousarray(g("norm_post2")[0].reshape(1, -1)),
        "w_in": np.ascontiguousarray(g("w_in")[0]),
        "lbP": np.ascontiguousarray(g("hg_lb").reshape(2, 2, 4, 128).transpose(3, 0, 1, 2).reshape(128, 16)),
        "onormP": np.ascontiguousarray(np.stack([g("hg_onorm")[0], g("gla_onorm")[0]], axis=1)),
        "bgkP": np.ascontiguousarray(g("gla_b_gk")[0].reshape(2, 4, 128).transpose(2, 0, 1).reshape(128, 8)),
        "wgk": np.ascontiguousarray(g("gla_w_gk")[0]),
        "w_br_hg": np.ascontiguousarray(g("w_br_hg")[0]),
        "w_br_gla": np.ascontiguousarray(g("w_br_gla")[0]),
        "w_out": np.ascontiguousarray(g("w_out")[0]),
        "w_ff_gate": np.ascontiguousarray(g("w_ff_gate")[0]),
        "w_ff_up": np.ascontiguousarray(g("w_ff_up")[0]),
        "w_ff_down": np.ascontiguousarray(g("w_ff_down")[0]),
    }
    s = np.arange(128)[:, None]
    t = np.arange(128)[None, :]
    same = (s // 64) == (t // 64)
    shared["ident"] = np.eye(128, dtype=f32)
    shared["ones"] = np.ones((128, 128), f32)
    shared["maskF4"] = np.tile((same & (s <= t)).astype(f32), (1, 4))
    shared["maskB4"] = np.tile((same & (s >= t)).astype(f32), (1, 4))
    cmr = np.ones((128, T), f32)
    cmr[:, ::64] = 0.0
    shared["cm"] = cmr
    maps = []
    for b in range(NCORES):
        m = dict(shared)
        m["x"] = np.ascontiguousarray(x[b])
        m["ctxx"] = np.ascontiguousarray(ctx[b])
        m["cx"] = col(c[b])
        maps.append(m)
    return maps


def kernel(**inputs):
    if "nc" not in _NC_CACHE:
        _NC_CACHE["nc"] = build_program()
    nc = _NC_CACHE["nc"]
    maps = _host_inputs(inputs)
    res = run_bass_kernel_spmd(nc, maps, core_ids=list(range(NCORES)))
    out = np.stack([np.asarray(res.results[b]["out"], dtype=np.float32) for b in range(NCORES)], axis=0)
    return out
```

```python
import numpy as np
from contextlib import ExitStack
import concourse.bass as bass
import concourse.mybir as mybir
from concourse.bass_utils import run_bass_kernel_spmd

F32 = mybir.dt.float32
BF16 = mybir.dt.bfloat16
AF = mybir.ActivationFunctionType
ALU = mybir.AluOpType

D = 1024
TC = 256
TL = 2048
T = TC + TL
NT = T // 128
NCH = T // 64
DFF = 2816
FC = DFF // 128
EPS = 1e-6
NCORES = 8
GROUPS = [(0, 256), (256, 512), (768, 512), (1280, 512), (1792, 512)]

ENGS = ["pe", "act", "dve", "pool", "sp"]
NRING = 40
SBUF_BASE = 16576
SBUF_CAP = 229376 - 128


class Prog:
    def __init__(self, nc, stack):
        self.nc = nc
        self.ops = {e: [] for e in ENGS}
        self.semobj = {}
        for e in ENGS:
            self.semobj["s_" + e] = stack.enter_context(nc.semaphore("s_" + e))
        for i in range(NRING):
            self.semobj["r%d" % i] = stack.enter_context(nc.semaphore("r%d" % i))
        self.cnt = {e: 0 for e in ENGS}
        self.seen = {e: {} for e in ENGS}
        self.ring_total = [0] * NRING
        self.ring_next = 0
        self.reg = {}
        self.fence_deps = []

    def _waits(self, eng, deps):
        best = {}
        for d in deps:
            if d is None:
                continue
            key, val = d
            if eng == "pe" and key == "s_pe":
                continue
            if val > best.get(key, 0):
                best[key] = val
        waits = []
        for key, val in best.items():
            if self.seen[eng].get(key, 0) >= val:
                continue
            self.seen[eng][key] = val
            waits.append((key, val))
        return waits

    def _deps(self, reads, writes):
        deps = list(self.fence_deps)
        for r in reads:
            st = self.reg.get(r)
            if st is not None:
                deps.append(st[0])
        for w in writes:
            st = self.reg.get(w)
            if st is not None:
                deps.append(st[0])
                deps += list(st[1].items())
        return deps

    def _record(self, h, reads, writes):
        for r in reads:
            st = self.reg.get(r)
            if st is None:
                st = self.reg[r] = [None, {}]
            if h[1] > st[1].get(h[0], 0):
                st[1][h[0]] = h[1]
        for w in writes:
            self.reg[w] = [h, {}]

    def op(self, eng, fn, reads=(), writes=(), deps=()):
        rec = _Recorder()
        fn(rec)
        specs = rec.specs
        assert specs
        fn = (lambda e, specs=specs: _play(e, specs))
        writes = list(writes) + [r for r in reads if isinstance(r, tuple) and r[0] == "ps"]
        d = self._deps(reads, writes) + list(deps)
        waits = self._waits(eng, d)
        self.cnt[eng] += 1
        h = ("s_" + eng, self.cnt[eng])
        self.ops[eng].append((waits, fn, ("s_" + eng, 1)))
        self._record(h, reads, writes)
        return h

    def dma(self, queue, pairs, reads=(), writes=(), deps=()):
        slot = self.ring_next
        self.ring_next = (self.ring_next + 1) % NRING
        key = "r%d" % slot
        d = self._deps(reads, writes) + list(deps)
        if self.ring_total[slot] > 0:
            d.append((key, self.ring_total[slot]))
        waits = self._waits(queue, d)
        first = True
        for (o, i) in pairs:
            self.ring_total[slot] += 16
            self.ops[queue].append((waits if first else [], (lambda e, o=o, i=i: e.dma_start(out=o, in_=i)), (key, 16)))
            first = False
        h = (key, self.ring_total[slot])
        self._record(h, reads, writes)
        return h

    def fence(self):
        self.fence_deps = [("s_" + e, self.cnt[e]) for e in ENGS if self.cnt[e] > 0] + \
                          [("r%d" % i, self.ring_total[i]) for i in range(NRING) if self.ring_total[i] > 0]
        self.reg = {}

    def finish(self):
        self.fence()
        self.ops["sp"].append((self._waits("sp", self.fence_deps), None, None))
        nc = self.nc

        def replay(e, name):
            for waits, fn, inc in self.ops[name]:
                for key, val in waits:
                    e.wait_ge(self.semobj[key], val)
                if fn is None:
                    continue
                ins = fn(e)
                if inc is not None:
                    ins.then_inc(self.semobj[inc[0]], inc[1])

        with nc.Block() as block:
            block.tensor(lambda e: replay(e, "pe"))
            block.scalar(lambda e: replay(e, "act"))
            block.vector(lambda e: replay(e, "dve"))
            block.gpsimd(lambda e: replay(e, "pool"))
            block.sync(lambda e: replay(e, "sp"))


class _Recorder:
    def __init__(self):
        self.specs = []

    def __getattr__(self, name):
        def f(*a, **k):
            self.specs.append((name, a, k))
            return None
        return f


def _play(e, specs):
    ins = None
    for (name, a, k) in specs:
        ins = getattr(e, name)(*a, **k)
    return ins


class Arena:
    def __init__(self, nc, cap):
        self.nc = nc
        self.lo = SBUF_BASE
        self.hi = cap
        self.n = 0

    @staticmethod
    def _size(shape, dt):
        n = 1
        for s in shape[1:]:
            n *= s
        return n * (4 if dt == F32 else 2)

    def alloc(self, name, shape, dt):
        off = (self.lo + 31) // 32 * 32
        sz = self._size(shape, dt)
        assert off + sz <= self.hi, "SBUF overflow at %s: %d + %d > %d" % (name, off, sz, self.hi)
        self.lo = off + sz
        self.n += 1
        return self.nc.alloc_sbuf_tensor_at("%s_%d" % (name, self.n), shape, dt, offset=off)

    def alloc_top(self, name, shape, dt):
        sz = self._size(shape, dt)
        off = (self.hi - sz) // 32 * 32
        assert off >= self.lo, "SBUF overflow (top) at %s" % name
        self.hi = off
        self.n += 1
        return self.nc.alloc_sbuf_tensor_at("%s_%d" % (name, self.n), shape, dt, offset=off)


def build_program(debug=None):
    nc = bass.Bass("TRN2", target_bir_lowering=False)

    def din(name, shape, dt=F32):
        return nc.dram_tensor(name, list(shape), dt, kind="ExternalInput").ap()

    x_d = din("x", [TL, D])
    ctx_d = din("ctxx", [TC, D])
    cx_d = din("cx", [128, 8])
    cctx_d = din("cctx", [128, 8])
    wmod_d = din("w_mod", [D, 6 * D])
    bmodP_d = din("bmodP", [128, 48])
    bmodR_d = din("bmodR", [1, 6 * D])
    npre1_d = din("npre1P", [128, 8])
    npre2_d = din("npre2P", [128, 8])
    npost1_d = din("npost1R", [1, D])
    npost2_d = din("npost2R", [1, D])
    win_d = din("w_in", [D, 6688])
    lbP_d = din("lbP", [128, 16])
    onorm_d = din("onormP", [128, 2])
    bgk_d = din("bgkP", [128, 8])
    wgk_d = din("wgk", [2, 16, 512])
    wbrh_d = din("w_br_hg", [512, D])
    wbrg_d = din("w_br_gla", [512, D])
    wout_d = din("w_out", [D, D])
    wfg_d = din("w_ff_gate", [D, DFF])
    wfu_d = din("w_ff_up", [D, DFF])
    wfd_d = din("w_ff_down", [DFF, D])
    ident_d = din("ident", [128, 128])
    ones_d = din("ones", [128, 128])
    maskF_d = din("maskF4", [128, 512])
    maskB_d = din("maskB4", [128, 512])
    cm_d = din("cm", [128, T])
    out_d = nc.dram_tensor("out", [TL, D], F32, kind="ExternalOutput").ap()
    z1_d = nc.dram_tensor("z1_scratch", [TL, D], F32, kind="Internal").ap()
    ob_d = nc.dram_tensor("ob_scratch", [8, 128, TL], BF16, kind="Internal").ap()
    dbg_d = {}
    if debug:
        for name, shape in debug.items():
            if name.startswith("_"):
                continue
            dbg_d[name] = nc.dram_tensor("dbg_" + name, list(shape), F32, kind="ExternalOutput").ap()

    with ExitStack() as st:
        p = Prog(nc, st)
        ar = Arena(nc, SBUF_CAP)
        PSB = [st.enter_context(nc.psum_tensor("psb%d" % i, [128, 1024], F32)) for i in range(4)]

        def bank(i):
            return PSB[i // 2][:, (i % 2) * 512:(i % 2 + 1) * 512]

        def pk(i):
            return ("ps", i)

        def dbg(name, src_ap, reads, shape):
            if not debug or name not in debug:
                return
            m = ar.lo
            tmp = ar.alloc("dbgtmp", list(shape), F32)
            p.op("pool", lambda e: e.tensor_copy(out=tmp[:], in_=src_ap), reads=reads, writes=[("dbgtmp", name)])
            p.dma("sp", [(dbg_d[name], tmp[:])], reads=[("dbgtmp", name)])
            p.fence()
            ar.lo = m

        identb = ar.alloc("identb", [128, 128], BF16)
        onesb = ar.alloc("onesb", [128, 128], BF16)
        maskF = ar.alloc("maskF", [128, 512], BF16)
        maskB = ar.alloc("maskB", [128, 512], BF16)
        cm = ar.alloc("cm", [128, T], BF16)
        p.dma("pool", [(identb[:], ident_d), (onesb[:], ones_d), (maskF[:], maskF_d), (maskB[:], maskB_d),
                       (cm[:, 0:1152], cm_d[:, 0:1152]), (cm[:, 1152:T], cm_d[:, 1152:T])],
              writes=["consts"])
        modP = ar.alloc("modP", [128, 6, 8, 2], F32)
        bmodP = ar.alloc("bmodP", [128, 48], F32)
        npre1 = ar.alloc("npre1", [128, 8], F32)
        npre2 = ar.alloc("npre2", [128, 8], F32)
        lbP = ar.alloc("lbP", [128, 16], F32)
        onormP = ar.alloc("onormP", [128, 2], F32)
        bgkP = ar.alloc("bgkP", [128, 8], F32)
        cx = ar.alloc("cx", [128, 8], F32)
        cc = ar.alloc("cc", [128, 8], F32)
        p.dma("sp", [(bmodP[:], bmodP_d), (npre1[:], npre1_d), (npre2[:], npre2_d), (lbP[:], lbP_d),
                     (onormP[:], onorm_d), (bgkP[:], bgk_d), (cx[:], cx_d), (cc[:], cctx_d)], writes=["smallin"])
        vec = {}
        for nm in ["sc1x", "sh1x", "sc1c", "sh1c", "sc2x", "sh2x", "lb", "oml", "noml", "nbgk"]:
            vec[nm] = ar.alloc(nm, [128, 8], F32)
        cs = ar.alloc("cs", [128, 8, 2], BF16)
        csrep = ar.alloc("csrep", [128, 8, 128], BF16)
        sg = ar.alloc("sgc", [128, 8, 2], F32)
        p.op("act", lambda e: e.activation(out=sg[:, :, 0], in_=cx[:], func=AF.Sigmoid), reads=["smallin"], writes=["sg0"])
        p.op("act", lambda e: e.activation(out=sg[:, :, 1], in_=cc[:], func=AF.Sigmoid), reads=["smallin"], writes=["sg1"])
        p.op("dve", lambda e: e.tensor_tensor(out=cs[:, :, 0], in0=sg[:, :, 0], in1=cx[:], op=ALU.mult), reads=["sg0"], writes=["cs0"])
        p.op("dve", lambda e: e.tensor_tensor(out=cs[:, :, 1], in0=sg[:, :, 1], in1=cc[:], op=ALU.mult), reads=["sg1"], writes=["cs1"])
        p.op("dve", lambda e: e.tensor_copy(out=csrep[:], in_=cs[:, :, 0:1].broadcast_to([128, 8, 128])), reads=["cs0"], writes=["csrep"])
        p.op("dve", lambda e: e.tensor_tensor(out=vec["lb"][:], in0=lbP[:, 0:8], in1=lbP[:, 8:16], op=ALU.subtract), reads=["smallin"], writes=["lbd"])
        p.op("act", lambda e: e.activation(out=vec["lb"][:], in_=vec["lb"][:], func=AF.Sigmoid), reads=["lbd"], writes=["lb"])
        p.op("dve", lambda e: e.tensor_scalar(out=vec["oml"][:], in0=vec["lb"][:], scalar1=-1.0, scalar2=1.0, op0=ALU.mult, op1=ALU.add), reads=["lb"], writes=["oml"])
        p.op("dve", lambda e: e.tensor_scalar(out=vec["noml"][:], in0=vec["lb"][:], scalar1=-1.0, scalar2=None, op0=ALU.add), reads=["lb"], writes=["noml"])
        p.op("dve", lambda e: e.tensor_scalar(out=vec["nbgk"][:], in0=bgkP[:], scalar1=-1.0, scalar2=None, op0=ALU.mult), reads=["smallin"], writes=["nbgk"])

        if debug and debug.get("_stop") == "s0a":
            p.finish()
            return nc
        def mod_pp(j, wm, wkey):
            psv = bank(0)[:, 0:16].rearrange("p (n t) -> p n t", t=2)

            def g(e):
                last = None
                for nchk in range(8):
                    for kc in range(8):
                        last = e.matmul(psv[:, nchk, :], lhsT=wm[:, kc, nchk * 128:(nchk + 1) * 128], rhs=cs[:, kc, :],
                                        start=(kc == 0), stop=(kc == 7))
                return last
            p.op("pe", g, reads=[wkey, "cs0", "cs1"], writes=[pk(0)])
            p.op("dve", lambda e: e.tensor_tensor(out=modP[:, j], in0=psv,
                                                 in1=bmodP[:, j * 8:(j + 1) * 8].unsqueeze(2).broadcast_to([128, 8, 2]), op=ALU.add),
                 reads=[pk(0), "smallin"], writes=[("modP", j)])

        def load_wm(j, wm, wkey):
            p.dma("pool", [(wm[:], wmod_d[:, j * 1024:(j + 1) * 1024].rearrange("(c p) n -> p c n", p=128))], writes=[wkey])

        m_small = ar.lo
        hT = ar.alloc("hT", [128, 8, T], BF16)
        m_persist = ar.lo

        wmb = [ar.alloc("wm%d" % i, [128, 8, 1024], BF16) for i in range(2)]
        xb = [ar.alloc("xb%d" % i, [128, D], F32) for i in range(3)]
        NXN = 8
        xnb = [ar.alloc("xn%d" % i, [128, D], BF16) for i in range(NXN)]
        junk = ar.alloc("junk", [128, D], BF16)
        ssq = ar.alloc("ssq", [128, NT], F32)
        rstd = ar.alloc("rstd", [128, NT], F32)

        load_wm(0, wmb[0], ("wm", 0))
        load_wm(1, wmb[1], ("wm", 1))
        mod_pp(0, wmb[0], ("wm", 0))
        mod_pp(1, wmb[1], ("wm", 1))
        if debug and debug.get("_stop") == "s0b":
            p.finish()
            return nc
        for which, scn, shn in ((0, "sc1x", "sh1x"), (1, "sc1c", "sh1c")):
            p.op("dve", lambda e, which=which, scn=scn: e.scalar_tensor_tensor(out=vec[scn][:], in0=modP[:, 1, :, which], scalar=1.0, in1=npre1[:],
                                                                              op0=ALU.add, op1=ALU.mult),
                 reads=[("modP", 1), "smallin"], writes=[scn])
            p.op("dve", lambda e, which=which, shn=shn: e.tensor_copy(out=vec[shn][:], in_=modP[:, 0, :, which]), reads=[("modP", 0)], writes=[shn])

        def norm_transpose(i, src_ap, srckey, xt, xtkey, xn, xnkey, sc, sh, sckeys, dst, dstkey, col0, ssq_t, rstd_t, load=True, pair=3, phase="all"):
            if phase in ("all", "norm"):
                norm_part(i, src_ap, srckey, xt, xtkey, xn, xnkey, ssq_t, rstd_t, load)
            if phase in ("all", "tr"):
                tr_part(i, xn, xnkey, sc, sh, sckeys, dst, dstkey, col0, pair)

        def norm_part(i, src_ap, srckey, xt, xtkey, xn, xnkey, ssq_t, rstd_t, load):
            if load:
                p.dma("sp", [(xt[:], src_ap)], reads=[srckey] if srckey else [], writes=[xtkey])
            p.op("act", lambda e: e.activation(out=junk[:], in_=xt[:], func=AF.Square, accum_out=ssq_t[:, i:i + 1]),
                 reads=[xtkey], writes=["junk", ("ssq", i)])
            p.op("act", lambda e: e.activation(out=rstd_t[:, i:i + 1], in_=ssq_t[:, i:i + 1], func=AF.Ln, scale=1.0 / D, bias=EPS),
                 reads=[("ssq", i)], writes=[("rstd", i)])
            p.op("act", lambda e: e.activation(out=rstd_t[:, i:i + 1], in_=rstd_t[:, i:i + 1], func=AF.Exp, scale=-0.5),
                 reads=[("rstd", i)], writes=[("rstd", i)])
            p.op("dve", lambda e: e.tensor_scalar(out=xn[:], in0=xt[:], scalar1=rstd_t[:, i:i + 1], scalar2=None, op0=ALU.mult),
                 reads=[xtkey, ("rstd", i)], writes=[xnkey])

        def tr_part(i, xn, xnkey, sc, sh, sckeys, dst, dstkey, col0, pair):
            ptb = PSB[pair][:, :].rearrange("p (k t) -> p k t", t=128)
            pkeys = [pk(2 * pair), pk(2 * pair + 1)]

            def g(e):
                for kc in range(8):
                    e.matmul(ptb[:, kc, :], lhsT=xn[:, kc * 128:(kc + 1) * 128], rhs=identb[:], start=True, stop=True)
            p.op("pe", g, reads=[xnkey, "consts"], writes=pkeys)
            for kk in range(4):
                for kc, eng in ((kk, "act"), (kk + 4, "dve")):
                    o = dst[:, kc, col0:col0 + 128]
                    bkey = [pkeys[0] if kc < 4 else pkeys[1]]
                    if eng == "act":
                        p.op("act", lambda e: e.activation(out=o, in_=ptb[:, kc, :], func=AF.Identity, scale=sc[:, kc:kc + 1], bias=sh[:, kc:kc + 1]),
                             reads=bkey + sckeys, writes=[(dstkey, i, kc)])
                    else:
                        p.op("dve", lambda e: e.tensor_scalar(out=o, in0=ptb[:, kc, :], scalar1=sc[:, kc:kc + 1], scalar2=sh[:, kc:kc + 1],
                                                             op0=ALU.mult, op1=ALU.add),
                             reads=bkey + sckeys, writes=[(dstkey, i, kc)])

        s1_groups = [(0, 2), (2, 4), (6, 4), (10, 4), (14, 4)]

        def s1_norm(i):
            src = ctx_d[i * 128:(i + 1) * 128, :] if i < 2 else x_d[(i - 2) * 128:(i - 1) * 128, :]
            norm_part(i, src, None, xb[i % 3], ("xb", i % 3), xnb[i % NXN], ("xn", i % NXN), ssq, rstd, True)

        def s1_tr(t0, nt):
            sc, sh, keys = (vec["sc1c"], vec["sh1c"], ["sc1c", "sh1c"]) if t0 < 2 else (vec["sc1x"], vec["sh1x"], ["sc1x", "sh1x"])

            def g(e):
                for kc in range(8):
                    for t in range(nt):
                        i = t0 + t
                        e.matmul(bank(kc)[:, t * 128:(t + 1) * 128], lhsT=xnb[i % NXN][:, kc * 128:(kc + 1) * 128], rhs=identb[:],
                                 start=True, stop=True)
            p.op("pe", g, reads=[("xn", (t0 + t) % NXN) for t in range(nt)] + ["consts"], writes=[pk(k_) for k_ in range(8)])
            for kk in range(4):
                for kc, eng in ((kk, "act"), (kk + 4, "dve")):
                    o = hT[:, kc, t0 * 128:(t0 + nt) * 128]
                    srcp = bank(kc)[:, 0:nt * 128]
                    wk_ = [("hT", t0 + t, kc) for t in range(nt)]
                    if eng == "act":
                        p.op("act", lambda e: e.activation(out=o, in_=srcp, func=AF.Identity, scale=sc[:, kc:kc + 1], bias=sh[:, kc:kc + 1]),
                             reads=[pk(kc)] + keys, writes=wk_)
                    else:
                        p.op("dve", lambda e: e.tensor_scalar(out=o, in0=srcp, scalar1=sc[:, kc:kc + 1], scalar2=sh[:, kc:kc + 1],
                                                             op0=ALU.mult, op1=ALU.add),
                             reads=[pk(kc)] + keys, writes=wk_)

        for gi_, (t0_, nt_) in enumerate(s1_groups):
            for i in range(t0_, t0_ + nt_):
                s1_norm(i)
            if gi_ >= 1:
                s1_tr(*s1_groups[gi_ - 1])
        s1_tr(*s1_groups[-1])
        dbg("hT", hT[:, 0, :], [], [128, T])
        if debug and debug.get("_stop") == "s1":
            p.finish()
            return nc
        p.fence()
        ar.lo = m_persist

        def hT_keys(s, n):
            return [("hT", i, kc) for i in range(s // 128, (s + n) // 128) for kc in range(8)]

        wq = ar.alloc("wq", [128, 8, 128], BF16)
        wv = ar.alloc("wv", [128, 8, 128], BF16)
        wa = ar.alloc("wa", [128, 8, 128], BF16)
        wb = ar.alloc("wb", [128, 8, 128], BF16)
        wg = ar.alloc("wg", [128, 8, 128], BF16)
        wlr = ar.alloc("wlr", [128, 8, 32], BF16)
        wgk = ar.alloc("wgk", [16, 2, 512], BF16)
        X1 = ar.alloc("X1", [128, T], F32)
        CUM = ar.alloc("CUM", [128, T], F32)
        Kb = ar.alloc("Kb", [128, T], BF16)
        QT = ar.alloc("QT", [128, T], BF16)
        GT2 = [ar.alloc("GT%d" % i, [128, TL], BF16) for i in range(2)]
        VTM2 = [ar.alloc("VTM%d" % i, [128, NT, 128], BF16) for i in range(2)]
        Qt2 = [[ar.alloc("Qt%d%d" % (i, d), [128, T], BF16) for d in range(2)] for i in range(2)]
        Kt2 = [[ar.alloc("Kt%d%d" % (i, d), [128, T], BF16) for d in range(2)] for i in range(2)]
        smd2 = [[{nm: ar.alloc("%s%d%d" % (nm, i, d), [128, NCH], F32) for nm in ("MID", "DL", "E")} for d in range(2)] for i in range(2)]
        KTM = ar.alloc("KTM", [128, NT, 128], BF16)
        DS = ar.alloc("DS", [128, 128, NCH + 1], F32)
        DBC = ar.alloc("DBC", [128, 16, NCH + 1], F32)
        SPv = [ar.alloc("SPv%d" % d, [128, 128, NCH + 1], BF16) for d in range(2)]
        MS = [ar.alloc("MS%d" % d, [128, 512], BF16) for d in range(2)]
        for d_ in range(2):
            p.op("pool", lambda e: e.memset(MS[d_][:], 0.0), writes=[("MS", d_)])
        SQ = ar.alloc("SQ", [128, 512], BF16)
        Rr = ar.alloc("Rr", [128, 512], F32)
        ON = ar.alloc("ON", [128, 512], F32)
        OBt = [ar.alloc("OBt%d" % i, [128, 512], BF16) for i in range(2)]
        Dsc = ar.alloc("Dsc", [128, NCH + 1], F32)
        tmpA = ar.alloc("tmpA", [128, NCH], F32)
        tmpB = ar.alloc("tmpB", [128, NCH], F32)
        p.op("pool", lambda e: e.memset(Dsc[:], 0.0), writes=["Dsc"])
        p.op("pool", lambda e: e.memset(DS[:, :, 0:1], 0.0), writes=[("DS0",)])
        lrT = ar.alloc("lrT", [16, 2, T], BF16)
        p.dma("pool", [(wlr[:], win_d[:, 4608:4640].rearrange("(c p) n -> p c n", p=128)),
                       (wgk[:], wgk_d.rearrange("d r n -> r d n"))], writes=["wlr", "wgk"])
        GS = [s_ for (s_, n_) in GROUPS]
        X1b = X1[:].bitcast(BF16)
        nheads = 8 if not debug else debug.get("_nheads", [8])[0]

        def wcol(off):
            return win_d[:, off:off + 128].rearrange("(c p) n -> p c n", p=128)

        def head_p1(hh):
            br, hd, hp = hh // 4, hh % 4, hh % 2
            gs = 1.0 if br == 0 else -1.0 / 16.0
            GT, VTM, Qt, Kt = GT2[hp], VTM2[hp], Qt2[hp], Kt2[hp]
            def load_w(h2, which):
                if h2 >= nheads:
                    return
                b2, d2 = h2 // 4, h2 % 4
                if b2 == 0:
                    o2 = dict(q=d2 * 128, v=512 + d2 * 128, a=1024 + d2 * 128, b=1536 + d2 * 128, g=2048 + d2 * 128)
                else:
                    o2 = dict(q=2560 + d2 * 128, a=3072 + d2 * 128, v=3584 + d2 * 128, g=4096 + d2 * 128)
                bufs = dict(q=wq, v=wv, a=wa, b=wb, g=wg)
                for w_ in which:
                    if w_ in o2:
                        p.dma("pool", [(bufs[w_][:], wcol(o2[w_]))], writes=["w" + w_])
            if hh == 0:
                load_w(0, "qvgab")
            yield

            def proj_group(wt, wkey, gi, s, n, evac):
                b = gi % 2
                ps = bank(b)[:, 0:n]

                def g(e):
                    for kc in range(8):
                        e.matmul(ps, lhsT=wt[:, kc, :], rhs=hT[:, kc, s:s + n], start=(kc == 0), stop=(kc == 7))
                p.op("pe", g, reads=[wkey] + hT_keys(s, n), writes=[pk(b)])
                evac(s, n, ps, pk(b))

            if hh == 4:
                for d in range(2):
                    for gi, (s, n) in enumerate(GROUPS):
                        b = gi % 2
                        ps = bank(b)[0:16, 0:n]

                        def g(e):
                            for kc in range(8):
                                e.matmul(ps, lhsT=wlr[:, kc, d * 16:(d + 1) * 16], rhs=hT[:, kc, s:s + n], start=(kc == 0), stop=(kc == 7))
                        p.op("pe", g, reads=["wlr"] + hT_keys(s, n), writes=[pk(b)])
                        p.op("act", lambda e: e.activation(out=lrT[:, d, s:s + n], in_=ps, func=AF.Copy), reads=[pk(b)], writes=[("lrT", d, s)])
                        yield

            def sig_exp(s, n, ps, pkey):
                xs = X1[:, s:s + n]
                p.op("act", lambda e: e.activation(out=xs, in_=ps, func=AF.Exp, scale=-1.0), reads=[pkey], writes=[("X1", s), ("XA", s), ("XB", s)])
                p.op("act", lambda e: e.activation(out=xs, in_=xs, func=AF.Ln, bias=1.0), reads=[("X1", s)], writes=[("X1", s)])
                p.op("act", lambda e: e.activation(out=xs, in_=xs, func=AF.Exp, scale=-1.0), reads=[("X1", s)], writes=[("X1", s)])

            if br == 0:
                def evq(s, n, ps, pkey):
                    sig_exp(s, n, ps, pkey)
                    p.op("dve", lambda e: e.tensor_tensor(out=QT[:, s:s + n], in0=ps, in1=X1[:, s:s + n], op=ALU.mult),
                         reads=[pkey, ("X1", s)], writes=[("QT", s)])
            else:
                def evq(s, n, ps, pkey):
                    p.op("act", lambda e: e.activation(out=QT[:, s:s + n], in_=ps, func=AF.Copy, scale=128.0 ** -0.5), reads=[pkey], writes=[("QT", s)])

            def evg(s, n, ps, pkey):
                sig_exp(s, n, ps, pkey)
                p.op("dve", lambda e: e.tensor_tensor(out=GT[:, s - TC:s - TC + n], in0=ps, in1=X1[:, s:s + n], op=ALU.mult),
                     reads=[pkey, ("X1", s)], writes=[("GT", hp, s)])

            def v_group(i0):
                nt4 = min(4, NT - i0)
                psv = bank(2)

                def g(e):
                    for tl in range(nt4):
                        i = i0 + tl
                        for kc in range(8):
                            e.matmul(psv[:, tl * 128:(tl + 1) * 128], lhsT=hT[:, kc, i * 128:(i + 1) * 128], rhs=wv[:, kc, :],
                                     start=(kc == 0), stop=(kc == 7))
                p.op("pe", g, reads=["wv"] + hT_keys(i0 * 128, nt4 * 128), writes=[pk(2)])
                p.op("act", lambda e: e.activation(out=VTM[:, i0:i0 + nt4, :], in_=psv[:, 0:nt4 * 128].rearrange("p (t v) -> p t v", v=128), func=AF.Copy),
                     reads=[pk(2)], writes=[("VTM", hp, i0)])

            for gi, (s, n) in enumerate(GROUPS):
                if gi >= 1:
                    proj_group(wq, "wq", gi, s, n, evq)
                    proj_group(wg, "wg", gi + 1, s, n, evg)
                v_group(gi * 4)
                yield
            if br == 1:
                def evk(s, n, ps, pkey):
                    p.op("act", lambda e: e.activation(out=Kb[:, s:s + n], in_=ps, func=AF.Copy), reads=[pkey], writes=[("Kb", s)])
                for gi, (s, n) in enumerate(GROUPS):
                    proj_group(wa, "wa", gi, s, n, evk)
                    yield
            load_w(hh + 1, "qvg" if br == 0 else "qvga")

            for d in range(2):
                if d == 1:
                    yield "SPLIT"
                dh = d * 4 + hd
                smd = smd2[hp][d]
                pmid, pend = (31, 63) if d == 0 else (32, 0)
                chains = []
                for gi, (s, n) in enumerate(GROUPS):
                    c0, ncg = s // 64, n // 64
                    b = gi % 2
                    ps = bank(b)[:, 0:n]
                    xs = X1[:, s:s + n]
                    cu = CUM[:, s:s + n]
                    c3 = cu.rearrange("p (c l) -> p c l", l=64)
                    x3 = xs.rearrange("p (c l) -> p c l", l=64)
                    kX, kC, kK = ("X1", s), ("CUM", s), ("Kb", s)
                    if br == 0:
                        wt, wkey = (wa, "wa") if d == 0 else (wb, "wb")

                        def e0(s=s, n=n, ps=ps, b=b, wt=wt, wkey=wkey, xs=xs, kX=kX):
                            def g(e):
                                for kc in range(8):
                                    e.matmul(ps, lhsT=wt[:, kc, :], rhs=hT[:, kc, s:s + n], start=(kc == 0), stop=(kc == 7))
                            p.op("pe", g, reads=[wkey] + hT_keys(s, n), writes=[pk(b)])
                            p.op("act", lambda e: e.activation(out=xs, in_=ps, func=AF.Exp, scale=-1.0), reads=[pk(b)], writes=[kX, ("XA", s), ("XB", s)])

                        def e1(s=s, n=n, xs=xs, cu=cu, kX=kX, kK=kK, kC=kC):
                            p.op("act", lambda e: e.activation(out=cu, in_=xs, func=AF.Ln, bias=1.0), reads=[kX], writes=[kC])
                            p.op("act", lambda e: e.activation(out=xs, in_=cu, func=AF.Exp, scale=-1.0), reads=[kC], writes=[kX])
                            p.op("pool", lambda e: e.tensor_scalar(out=Kb[:, s:s + n], in0=xs, scalar1=vec["noml"][:, dh:dh + 1],
                                                                  scalar2=vec["oml"][:, dh:dh + 1], op0=ALU.mult, op1=ALU.add),
                                 reads=[kX, "noml", "oml"], writes=[kK])
                            p.op("act", lambda e: e.activation(out=xs, in_=xs, func=AF.Ln, scale=vec["oml"][:, dh:dh + 1], bias=vec["lb"][:, dh:dh + 1]),
                                 reads=[kX, "oml", "lb"], writes=[kX])
                    else:
                        def e0(s=s, n=n, ps=ps, b=b, xs=xs, kX=kX):
                            p.op("pe", lambda e: e.matmul(ps, lhsT=wgk[:, d, hd * 128:(hd + 1) * 128], rhs=lrT[:, d, s:s + n], start=True, stop=True),
                                 reads=["wgk", ("lrT", d, s)], writes=[pk(b)])
                            p.op("act", lambda e: e.activation(out=xs, in_=ps, func=AF.Exp, scale=-1.0, bias=vec["nbgk"][:, dh:dh + 1]),
                                 reads=[pk(b), "nbgk"], writes=[kX, ("XA", s), ("XB", s)])

                        def e1(xs=xs, kX=kX):
                            p.op("act", lambda e: e.activation(out=xs, in_=xs, func=AF.Ln, bias=1.0), reads=[kX], writes=[kX])

                    def e2(s=s, n=n, xs=xs, cu=cu, kX=kX, kC=kC):
                        if d == 0:
                            p.op("dve", lambda e: e.tensor_tensor_scan(out=cu, data0=cm[:, 0:n], data1=xs, initial=0.0, op0=ALU.mult, op1=ALU.add),
                                 reads=[kX, "consts"], writes=[kC])
                        else:
                            p.op("dve", lambda e: e.tensor_tensor_scan(out=cu[:, ::-1], data0=cm[:, 0:n], data1=xs[:, ::-1], initial=0.0,
                                                                      op0=ALU.mult, op1=ALU.add),
                                 reads=[kX, "consts"], writes=[kC])

                    def e3(s=s, c0=c0, ncg=ncg, c3=c3, kC=kC):
                        p.op("pool", lambda e: e.tensor_copy(out=smd["MID"][:, c0:c0 + ncg], in_=c3[:, :, pmid]), reads=[kC], writes=[("MID", hp, d, s)])
                        p.op("pool", lambda e: e.tensor_copy(out=smd["DL"][:, c0:c0 + ncg], in_=c3[:, :, pend]), reads=[kC], writes=[("DL", hp, d, s)])

                    def e4(s=s, c0=c0, ncg=ncg, c3=c3, kC=kC):
                        p.op("dve", lambda e: e.tensor_tensor(out=c3, in0=c3, in1=smd["MID"][:, c0:c0 + ncg].unsqueeze(2).broadcast_to([128, ncg, 64]),
                                                             op=ALU.subtract),
                             reads=[kC, ("MID", hp, d, s)], writes=[kC])

                    xa = X1b[:, 2 * s:2 * s + n]
                    xb_ = X1b[:, 2 * s + n:2 * s + 2 * n]

                    def e5(gi=gi, s=s, cu=cu, xa=xa, kX=kX, kC=kC):
                        if gi >= 1:
                            p.op("act", lambda e: e.activation(out=xa, in_=cu, func=AF.Exp, scale=gs), reads=[kC, kX], writes=[("XA", s)])

                    def e6(gi=gi, s=s, n=n, xa=xa):
                        if gi >= 1:
                            p.op("dve", lambda e: e.tensor_tensor(out=Qt[d][:, s:s + n], in0=QT[:, s:s + n], in1=xa, op=ALU.mult),
                                 reads=[("XA", s), ("QT", s)], writes=[("Qt", hp, d, s)])

                    def e7(s=s, cu=cu, xb_=xb_, kX=kX, kC=kC):
                        p.op("act", lambda e: e.activation(out=xb_, in_=cu, func=AF.Exp, scale=-gs), reads=[kC, kX], writes=[("XB", s)])

                    def e8(s=s, n=n, xb_=xb_, kK=kK):
                        p.op("dve", lambda e: e.tensor_tensor(out=Kt[d][:, s:s + n], in0=Kb[:, s:s + n], in1=xb_, op=ALU.mult),
                             reads=[("XB", s), kK], writes=[("Kt", hp, d, s)])
                    chains.append([e0, e1, e2, e3, e4, e5, e6, e7, e8])
                nel = len(chains[0])
                for step in range(nel + len(chains) - 1):
                    for gi in range(len(chains)):
                        k = step - gi
                        if 0 <= k < nel:
                            chains[gi][k]()
                    if step >= len(chains):
                        def gw(e):
                            for _ in range(3):
                                e.matmul(bank(2), lhsT=identb[:], rhs=hT[:, 0, 0:512], start=True, stop=True)
                        p.op("pe", gw, reads=["consts"], writes=[pk(2)])
                    yield
            if br == 0:
                load_w(hh + 1, "ab")

        def head_p2(hh):
            br, hd, hp = hh // 4, hh % 4, hh % 2
            gs = 1.0 if br == 0 else -1.0 / 16.0
            GT, VTM, Qt, Kt = GT2[hp], VTM2[hp], Qt2[hp], Kt2[hp]
            allk = lambda nm, d: [(nm, hp, d, s_) for s_ in GS]
            for d in range(2):
                if d == 1:
                    yield "SPLIT"
                smd = smd2[hp][d]
                MIDt, DLt = smd["MID"], smd["DL"]
                p.op("pool", lambda e: e.tensor_tensor(out=tmpA[:], in0=DLt[:], in1=MIDt[:], op=ALU.subtract),
                     reads=allk("DL", d) + allk("MID", d), writes=["tmpA"])
                if d == 0:
                    p.op("pool", lambda e: e.tensor_tensor(out=tmpB[:, 0:35], in0=MIDt[:, 1:36], in1=tmpA[:, 0:35], op=ALU.add),
                         reads=["tmpA"] + allk("MID", d), writes=["tmpB"])
                    p.op("act", lambda e: e.activation(out=Dsc[:, 1:36], in_=tmpB[:, 0:35], func=AF.Exp, scale=gs), reads=["tmpB"], writes=["Dsc"])
                else:
                    p.op("pool", lambda e: e.tensor_tensor(out=tmpB[:, 1:36], in0=MIDt[:, 0:35], in1=tmpA[:, 1:36], op=ALU.add),
                         reads=["tmpA"] + allk("MID", d), writes=["tmpB"])
                    p.op("pool", lambda e: e.tensor_tensor(out=tmpB[:, 0:1], in0=MIDt[:, 35:36], in1=tmpA[:, 0:1], op=ALU.add),
                         reads=["tmpA", "tmpB"] + allk("MID", d), writes=["tmpB"])
                    p.op("act", lambda e: e.activation(out=Dsc[:, 1:5], in_=tmpB[:, 3::-1], func=AF.Exp, scale=gs), reads=["tmpB"], writes=["Dsc"])
                    p.op("act", lambda e: e.activation(out=Dsc[:, 5:36], in_=tmpB[:, 35:4:-1], func=AF.Exp, scale=gs), reads=["tmpB", "Dsc"], writes=["Dsc"])
                p.op("pool", lambda e: e.tensor_copy(out=DBC[:], in_=Dsc[:].unsqueeze(1).broadcast_to([128, 16, NCH + 1])), reads=["Dsc"], writes=["DBC"])
                yield
                for i0 in range(0, NT, 4):
                    nt4 = min(4, NT - i0)
                    b = 3 + (i0 // 4) % 2
                    psk = bank(b)

                    def g(e):
                        for tl in range(nt4):
                            i = i0 + tl
                            e.matmul(psk[:, tl * 128:(tl + 1) * 128], lhsT=Kt[d][:, i * 128:(i + 1) * 128], rhs=identb[:], start=True, stop=True)
                    p.op("pe", g, reads=allk("Kt", d) + ["consts"], writes=[pk(b)])
                    p.op("act", lambda e: e.activation(out=KTM[:, i0:i0 + nt4, :], in_=psk[:, 0:nt4 * 128].rearrange("p (t v) -> p t v", v=128), func=AF.Copy),
                         reads=[pk(b)], writes=[("KTM", i0)])
                    yield
                for si, (ti0, ntl) in enumerate(((0, 2), (2, 4), (6, 4), (10, 4), (14, 4))):
                    for half in range(2):
                        b = 3 + half
                        psd = bank(b)

                        def g(e):
                            for m in range(ntl):
                                i = ti0 + m
                                e.matmul(psd[:, m * 128:(m + 1) * 128], lhsT=KTM[half * 64:(half + 1) * 64, i, :],
                                         rhs=VTM[half * 64:(half + 1) * 64, i, :], start=True, stop=True)
                        p.op("pe", g, reads=[("KTM", i0_) for i0_ in range(0, NT, 4)] + [("VTM", hp, i0_) for i0_ in range(0, NT, 4)], writes=[pk(b)])
                        cfirst = 2 * ti0 + half
                        if d == 0:
                            dsv = DS[:, :, 1 + cfirst:1 + cfirst + 2 * ntl - 1:2]
                        elif ti0 == 0:
                            dsv = DS[:, :, 4 - half:4 - half - 2 * ntl + 1:-2]
                        else:
                            jst = 40 - cfirst
                            dsv = DS[:, :, jst:jst - 2 * ntl + 1:-2]
                        p.op("act", lambda e: e.activation(out=dsv.rearrange("p v j -> p j v"),
                                                          in_=psd[:, 0:ntl * 128].rearrange("p (c v) -> p c v", v=128), func=AF.Copy),
                             reads=[pk(b)], writes=[("DS", si, half)])
                    yield
                dsk = [("DS", si, half) for si in range(5) for half in range(2)]
                for qv in range(4):
                    for hv in range(2):
                        v0 = qv * 32 + hv * 16
                        v2 = DS[:, v0:v0 + 16, :].rearrange("p v j -> p (v j)")
                        o2 = SPv[d][:, v0:v0 + 16, :].rearrange("p v j -> p (v j)")
                        p.op("dve", lambda e: e.tensor_tensor_scan(out=o2, data0=v2, data1=DBC[:].rearrange("p v j -> p (v j)"), initial=0.0,
                                                                  op0=ALU.add, op1=ALU.mult),
                             reads=dsk + ["DBC", ("DS0",)], writes=[("SP", d, qv, hv)])
                    yield
            spk = [("SP", d_, q_, h_) for d_ in range(2) for q_ in range(4) for h_ in range(2)]
            chains = []
            for gl in range(4):
                s0 = 256 + gl * 512
                otb = 5 + gl % 2
                pot = bank(otb)
                obt = OBt[gl % 2]

                def f0(gl=gl, s0=s0):
                    for d in range(2):
                        psc = bank(3 + d)

                        def g(e):
                            for tl in range(4):
                                i = 2 + 4 * gl + tl
                                e.matmul(psc[:, tl * 128:(tl + 1) * 128], lhsT=Kt[d][:, i * 128:(i + 1) * 128], rhs=Qt[d][:, i * 128:(i + 1) * 128],
                                         start=True, stop=True)
                        p.op("pe", g, reads=[("Kt", hp, d, s0), ("Qt", hp, d, s0)], writes=[pk(3 + d)])
                        p.op("dve", lambda e: e.copy_predicated(out=MS[d][:], mask=(maskF if d == 0 else maskB)[:].bitcast(mybir.dt.uint16), data=psc),
                             reads=[pk(3 + d), "consts"], writes=[("MS", d)])

                def f1(gl=gl, s0=s0, otb=otb, pot=pot):
                    def g(e):
                        for tl in range(4):
                            i = 2 + 4 * gl + tl
                            cols = slice(tl * 128, (tl + 1) * 128)
                            e.matmul(pot[:, cols], lhsT=VTM[:, i, :], rhs=MS[0][:, cols], start=True, stop=False)
                            e.matmul(pot[:, cols], lhsT=VTM[:, i, :], rhs=MS[1][:, cols], start=False, stop=False)
                            for half in range(2):
                                c = 2 * i + half
                                cc_ = slice(tl * 128 + half * 64, tl * 128 + half * 64 + 64)
                                e.matmul(pot[:, cc_], lhsT=SPv[0][:, :, c], rhs=Qt[0][:, c * 64:(c + 1) * 64], start=False, stop=False)
                                e.matmul(pot[:, cc_], lhsT=SPv[1][:, :, 39 - c], rhs=Qt[1][:, c * 64:(c + 1) * 64], start=False, stop=True)
                    p.op("pe", g, reads=[("MS", 0), ("MS", 1), ("Qt", hp, 0, s0), ("Qt", hp, 1, s0)] + spk + [("VTM", hp, i0_) for i0_ in range(0, NT, 4)],
                         writes=[pk(otb)])

                def f2(otb=otb, pot=pot):
                    p.op("act", lambda e: e.activation(out=SQ[:], in_=pot, func=AF.Square), reads=[pk(otb)], writes=["SQ"])
                    p.op("pe", lambda e: e.matmul(bank(7), lhsT=onesb[:], rhs=SQ[:], start=True, stop=True), reads=["SQ", "consts"], writes=[pk(7)])
                    p.op("act", lambda e: e.activation(out=Rr[:], in_=bank(7), func=AF.Ln, scale=1.0 / 128, bias=EPS), reads=[pk(7)], writes=["Rr"])
                    p.op("act", lambda e: e.activation(out=Rr[:], in_=Rr[:], func=AF.Exp, scale=-0.5), reads=["Rr"], writes=["Rr"])

                def f3(gl=gl, s0=s0, otb=otb, pot=pot, obt=obt):
                    p.op("dve", lambda e: e.scalar_tensor_tensor(out=ON[:], in0=pot, scalar=onormP[:, br:br + 1], in1=Rr[:], op0=ALU.mult, op1=ALU.mult),
                         reads=[pk(otb), "Rr", "smallin"], writes=["ON"])
                    p.op("pool", lambda e: e.tensor_tensor(out=obt[:], in0=ON[:], in1=GT[:, gl * 512:(gl + 1) * 512], op=ALU.mult),
                         reads=["ON", ("GT", hp, s0)], writes=[("OBt", gl % 2)])
                    p.dma("sp", [(ob_d[hh, :, gl * 512:(gl + 1) * 512], obt[:])], reads=[("OBt", gl % 2)], writes=[("obd", hh, gl)])
                chains.append([f0, f1, f2, f3])
            nel = 4
            for step in range(nel + len(chains) - 1):
                for gl in range(len(chains)):
                    k = step - gl
                    if 0 <= k < nel:
                        chains[gl][k]()
                yield

        def co_run(ga, gb, stop_a, stop_b):
            act = [ga is not None, gb is not None]
            gens = [ga, gb]
            stops = [stop_a, stop_b]
            while act[0] or act[1]:
                for k in range(2):
                    if not act[k]:
                        continue
                    try:
                        v = next(gens[k])
                    except StopIteration:
                        act[k] = False
                        continue
                    if v == "SPLIT" and stops[k]:
                        act[k] = False

        _lo_s2 = ar.lo
        ar.lo = m_persist
        OB = ar.alloc("OB", [128, 8, TL], BF16)
        w3 = []
        for i in range(2):
            w3.append(dict(gh=ar.alloc("wgh%d" % i, [128, 8, 128], BF16), gg=ar.alloc("wgg%d" % i, [128, 8, 128], BF16),
                           bh=ar.alloc("wbh%d" % i, [128, 4, 128], BF16), bg=ar.alloc("wbg%d" % i, [128, 4, 128], BF16)))
            if i == 0:
                assert ar.lo <= m_persist + 5 * 2048 + 512 + 2048 + 2 * 9216 + 2 * 4608 + 4096
        lo_after_w3 = ar.lo
        ar.lo = _lo_s2

        def load_w3(nn, deps=()):
            if nn >= 8:
                return
            w = w3[nn % 2]
            p.dma("pool", [(w["gh"][:], wcol(4640 + nn * 128)), (w["gg"][:], wcol(5664 + nn * 128)),
                           (w["bh"][:], wbrh_d[:, nn * 128:(nn + 1) * 128].rearrange("(h p) n -> p h n", p=128)),
                           (w["bg"][:], wbrg_d[:, nn * 128:(nn + 1) * 128].rearrange("(h p) n -> p h n", p=128))], writes=[("w3", nn % 2)], deps=deps)

        def load_ob(h_, deps=()):
            for gl_ in range(4):
                p.dma("sp", [(OB[:, h_, gl_ * 512:(gl_ + 1) * 512], ob_d[h_, :, gl_ * 512:(gl_ + 1) * 512])],
                      reads=[("obd", h_, gl_)], writes=[("OBl", h_, gl_)], deps=deps)

        g2 = None
        for hh in range(nheads):
            g1 = head_p1(hh)
            co_run(g1, g2, True, False)
            g2 = head_p2(hh)
            co_run(g1, g2, False, True)
        snap = [("s_" + e_, p.cnt[e_]) for e_ in ENGS if p.cnt[e_] > 0] + \
               [("r%d" % i_, p.ring_total[i_]) for i_ in range(NRING) if p.ring_total[i_] > 0]
        for h_ in range(nheads - 1):
            load_ob(h_, deps=snap)
        load_w3(0, deps=snap)
        co_run(None, g2, False, False)
        if debug and debug.get("_stop") == "s2":
            p.finish()
            return nc
        p.fence()
        ar.lo = m_persist

        load_ob(nheads - 1)
        ar.lo = lo_after_w3
        grow = [ar.alloc_top("grow%d" % i, [128, D], F32) for i in range(2)]
        hi_grow = ar.hi
        MT = ar.alloc_top("MT", [128, 8, TL], BF16)
        wout = ar.alloc_top("wout", [128, 8, D], BF16)
        S12 = [ar.alloc("S12_%d" % i, [128, 512], F32) for i in range(2)]
        M12 = [ar.alloc("M12_%d" % i, [128, 512], F32) for i in range(2)]
        wmr = ar.alloc("wmr", [128, 8, 1024], BF16)
        brow = ar.alloc("brow", [128, D], F32)
        nrow = ar.alloc("nrow", [128, D], F32)
        woutk = [("wout", kc) for kc in range(8)]
        growk = [[("grow", gi_, 0), ("grow", gi_, 1)] for gi_ in range(2)]

        def load_wmr(j):
            p.dma("pool", [(wmr[:], wmod_d[:, j * 1024:(j + 1) * 1024].rearrange("(c p) n -> p c n", p=128))], writes=["wmr"])

        def gate_row(gi_, j, npost_d):
            p.dma("sp", [(brow[:], bmodR_d[:, j * 1024:(j + 1) * 1024].partition_broadcast(128)),
                         (nrow[:], npost_d.partition_broadcast(128))], writes=["brow"])
            for half in range(2):
                psr = bank(half)

                def g(e, psr=psr, half=half):
                    for kc in range(8):
                        e.matmul(psr, lhsT=csrep[:, kc, :], rhs=wmr[:, kc, half * 512:(half + 1) * 512], start=(kc == 0), stop=(kc == 7))
                p.op("pe", g, reads=["wmr"], writes=[pk(half)])
                p.op("dve", lambda e, psr=psr, half=half: e.tensor_tensor(out=grow[gi_][:, half * 512:(half + 1) * 512], in0=psr,
                                                                         in1=brow[:, half * 512:(half + 1) * 512], op=ALU.add),
                     reads=[pk(half), "brow"], writes=[("grow", gi_, half)])
            p.op("pool", lambda e: e.tensor_tensor(out=grow[gi_][:], in0=grow[gi_][:], in1=nrow[:], op=ALU.mult),
                 reads=[("grow", gi_, 0), ("grow", gi_, 1), "brow"], writes=[("grow", gi_, 0), ("grow", gi_, 1)])

        def side_work(nn):
            if nn == 0:
                for kc in range(8):
                    p.dma("pool", [(wout[:, kc, :], wout_d[kc * 128:(kc + 1) * 128, :])], writes=[("wout", kc)])
                mod_pp(3, wmr, "wmr")
                load_wmr(4)
            elif nn == 1:
                mod_pp(4, wmr, "wmr")
                load_wmr(2)
                p.op("dve", lambda e: e.scalar_tensor_tensor(out=vec["sc2x"][:], in0=modP[:, 4, :, 0], scalar=1.0, in1=npre2[:], op0=ALU.add, op1=ALU.mult),
                     reads=[("modP", 4), "smallin"], writes=["sc2x"])
                p.op("dve", lambda e: e.tensor_copy(out=vec["sh2x"][:], in_=modP[:, 3, :, 0]), reads=[("modP", 3)], writes=["sh2x"])
            elif nn == 2:
                gate_row(0, 2, npost1_d)
                load_wmr(5)
            elif nn == 3:
                gate_row(1, 5, npost2_d)

        load_w3(1)
        load_wmr(3)
        itn = 0
        for nn in range(8):
            w = w3[nn % 2]
            wk = ("w3", nn % 2)
            for gl in range(4):
                s = gl * 512
                for half in range(2):
                    bset = (itn % 3) * 2
                    itn += 1
                    pg_, pb_ = bank(bset), bank(bset + 1)
                    wgt, wbr = (w["gh"], w["bh"]) if half == 0 else (w["gg"], w["bg"])

                    def g(e, pg_=pg_, pb_=pb_, wgt=wgt, wbr=wbr, s=s, half=half):
                        for kc in range(8):
                            e.matmul(pg_, lhsT=wgt[:, kc, :], rhs=hT[:, kc, TC + s:TC + s + 512], start=(kc == 0), stop=(kc == 7))
                        for h in range(4):
                            e.matmul(pb_, lhsT=wbr[:, h, :], rhs=OB[:, half * 4 + h, s:s + 512], start=(h == 0), stop=(h == 3))
                    p.op("pe", g, reads=[wk] + [("OBl", half * 4 + h_, gl) for h_ in range(4)], writes=[pk(bset), pk(bset + 1)])
                    p.op("act", lambda e, pg_=pg_, half=half: e.activation(out=S12[half][:], in_=pg_, func=AF.Sigmoid),
                         reads=[pk(bset)], writes=[("S12", half)])
                    p.op("dve", lambda e, pb_=pb_, half=half: e.tensor_tensor(out=M12[half][:], in0=pb_, in1=S12[half][:], op=ALU.mult),
                         reads=[pk(bset + 1), ("S12", half)], writes=[("M12", half)])
                p.op("pool", lambda e, nn=nn, s=s: e.tensor_tensor(out=MT[:, nn, s:s + 512], in0=M12[0][:], in1=M12[1][:], op=ALU.add),
                     reads=[("M12", 0), ("M12", 1)], writes=[("MT", nn, gl)])
            load_w3(nn + 2)
            side_work(nn)
        dbg("MT", MT[:, 0, :], [], [128, TL])
        if debug and debug.get("_stop") == "3a":
            p.finish()
            return nc
        p.fence()
        ar.lo = m_small

        h2T = ar.alloc("h2T", [128, 8, TL], BF16)
        wdn = ar.alloc("wdn", [128, FC, D], BF16)
        m3 = ar.lo
        for fc in range(FC):
            p.dma("pool", [(wdn[:, fc, :], wfd_d[fc * 128:(fc + 1) * 128, :])], writes=[("wdn", fc)])
        NB3 = 3
        xb = [ar.alloc("xb3_%d" % i, [128, D], F32) for i in range(NB3)]
        T1 = [ar.alloc("T1_%d" % i, [128, D], F32) for i in range(NB3)]
        Z1 = [ar.alloc("Z1_%d" % i, [128, D], F32) for i in range(NB3)]
        XN = [ar.alloc("XN_%d" % i, [128, D], BF16) for i in range(NB3)]
        junk = ar.alloc("junk3", [128, D], BF16)
        ss1 = ar.alloc("ss1", [128, 16], F32)
        r1 = ar.alloc("r1", [128, 16], F32)
        ss2 = ar.alloc("ss2", [128, 16], F32)
        r2 = ar.alloc("r2", [128, 16], F32)
        def tile_a(i):
            yp = PSB[i % 3]
            ypk = [pk(2 * (i % 3)), pk(2 * (i % 3) + 1)]

            def g(e, i=i, yp=yp):
                last = None
                for half in range(2):
                    for kc in range(8):
                        last = e.matmul(yp[:, half * 512:(half + 1) * 512], lhsT=MT[:, kc, i * 128:(i + 1) * 128],
                                        rhs=wout[:, kc, half * 512:(half + 1) * 512], start=(kc == 0), stop=(kc == 7))
                return last
            p.op("pe", g, reads=woutk, writes=ypk)
            p.op("act", lambda e, i=i, yp=yp: e.activation(out=junk[:], in_=yp[:, :], func=AF.Square, accum_out=ss1[:, i:i + 1]),
                 reads=ypk, writes=["junk", ("ss1", i)])
            p.op("act", lambda e, i=i: e.activation(out=r1[:, i:i + 1], in_=ss1[:, i:i + 1], func=AF.Ln, scale=1.0 / D, bias=EPS),
                 reads=[("ss1", i)], writes=[("r1", i)])
            p.op("act", lambda e, i=i: e.activation(out=r1[:, i:i + 1], in_=r1[:, i:i + 1], func=AF.Exp, scale=-0.5),
                 reads=[("r1", i)], writes=[("r1", i)])
            p.op("dve", lambda e, i=i, yp=yp: e.scalar_tensor_tensor(out=T1[i % NB3][:], in0=yp[:, :], scalar=r1[:, i:i + 1], in1=grow[0][:],
                                                                   op0=ALU.mult, op1=ALU.mult),
                 reads=ypk + [("r1", i)] + growk[0], writes=[("T1", i % NB3)])

        def tile_a2(i):
            p.op("pool", lambda e, i=i: e.tensor_tensor(out=Z1[i % NB3][:], in0=T1[i % NB3][:], in1=xb[i % NB3][:], op=ALU.add),
                 reads=[("T1", i % NB3), ("xb", i % NB3)], writes=[("Z1", i % NB3)])
            p.dma("pool", [(z1_d[i * 128:(i + 1) * 128, :], Z1[i % NB3][:])], reads=[("Z1", i % NB3)], writes=[("z1d", i)])
            norm_transpose(i, None, None, Z1[i % NB3], ("Z1", i % NB3), XN[i % NB3], ("XN", i % NB3), vec["sc2x"], vec["sh2x"], ["sc2x", "sh2x"],
                           h2T, "h2T", i * 128, ss2, r2, load=False, pair=3, phase="norm")

        def tile_b(i):
            norm_transpose(i, None, None, Z1[i % NB3], ("Z1", i % NB3), XN[i % NB3], ("XN", i % NB3), vec["sc2x"], vec["sh2x"], ["sc2x", "sh2x"],
                           h2T, "h2T", i * 128, ss2, r2, load=False, pair=3, phase="tr")

        for i_ in range(2):
            p.dma("sp", [(xb[i_][:], x_d[i_ * 128:(i_ + 1) * 128, :])], writes=[("xb", i_)])
        for i in range(18):
            if i < 16:
                tile_a(i)
            if 1 <= i < 17:
                tile_a2(i - 1)
            if i + 2 < 16:
                p.dma("sp", [(xb[(i + 2) % NB3][:], x_d[(i + 2) * 128:(i + 3) * 128, :])], writes=[("xb", (i + 2) % NB3)])
            if i >= 2:
                tile_b(i - 2)
        dbg("h2T", h2T[:, 0, :], [], [128, TL])
        if debug and debug.get("_stop") == "3b":
            p.finish()
            return nc
        p.fence()
        ar.lo = m3
        ar.hi = hi_grow

        HID = ar.alloc("HID", [128, FC, 1024], BF16)
        wfg = [ar.alloc("wfg%d" % i, [128, 8, 128], BF16) for i in range(3)]
        wfu = [ar.alloc("wfu%d" % i, [128, 8, 128], BF16) for i in range(3)]
        SG = [ar.alloc("SG%d" % i, [128, 512], F32) for i in range(2)]
        T2 = [ar.alloc("T2_%d" % i, [128, D], F32) for i in range(2)]
        zt = [ar.alloc("zt%d" % i, [128, D], F32) for i in range(3)]
        OT_ = [ar.alloc("OT%d" % i, [128, D], F32) for i in range(2)]
        junk = ar.alloc("junk4", [128, D], BF16)
        ss3 = ar.alloc("ss3", [128, 16], F32)
        r3 = ar.alloc("r3", [128, 16], F32)
        wdnk = [("wdn", fc) for fc in range(FC)]
        itn = 0
        for grp in range(2):
            for fc in range(FC):
                wi = fc % 3
                p.dma("pool", [(wfg[wi][:], wfg_d[:, fc * 128:(fc + 1) * 128].rearrange("(c p) n -> p c n", p=128)),
                               (wfu[wi][:], wfu_d[:, fc * 128:(fc + 1) * 128].rearrange("(c p) n -> p c n", p=128))], writes=[("wf", wi)])
                for sub in range(2):
                    t0 = grp * 1024 + sub * 512
                    bset = (itn % 3) * 2
                    itn += 1
                    pg_, pu_ = bank(bset), bank(bset + 1)

                    def g(e, pg_=pg_, pu_=pu_, wi=wi, t0=t0):
                        last = None
                        for kc in range(8):
                            e.matmul(pg_, lhsT=wfg[wi][:, kc, :], rhs=h2T[:, kc, t0:t0 + 512], start=(kc == 0), stop=(kc == 7))
                        for kc in range(8):
                            last = e.matmul(pu_, lhsT=wfu[wi][:, kc, :], rhs=h2T[:, kc, t0:t0 + 512], start=(kc == 0), stop=(kc == 7))
                        return last
                    p.op("pe", g, reads=[("wf", wi)], writes=[pk(bset), pk(bset + 1)])
                    p.op("act", lambda e, pg_=pg_, sub=sub: e.activation(out=SG[sub][:], in_=pg_, func=AF.Silu), reads=[pk(bset)], writes=[("SG", sub)])
                    p.op("dve", lambda e, pu_=pu_, sub=sub, fc=fc: e.tensor_tensor(out=HID[:, fc, sub * 512:(sub + 1) * 512], in0=pu_, in1=SG[sub][:], op=ALU.mult),
                         reads=[pk(bset + 1), ("SG", sub)], writes=[("HID", fc, sub)])
            for i_ in range(grp * 8, grp * 8 + 2):
                p.dma("sp", [(zt[i_ % 3][:], z1_d[i_ * 128:(i_ + 1) * 128, :])], writes=[("zt", i_ % 3)])
            for tt in range(8):
                i = grp * 8 + tt
                yp = PSB[i % 2]
                ypk = [pk(2 * (i % 2)), pk(2 * (i % 2) + 1)]

                def g(e, tt=tt, yp=yp):
                    last = None
                    for half in range(2):
                        for fc in range(FC):
                            last = e.matmul(yp[:, half * 512:(half + 1) * 512], lhsT=HID[:, fc, tt * 128:(tt + 1) * 128],
                                            rhs=wdn[:, fc, half * 512:(half + 1) * 512], start=(fc == 0), stop=(fc == FC - 1))
                    return last
                p.op("pe", g, reads=wdnk + [("HID", fc, tt // 4) for fc in range(FC)], writes=ypk)
                p.op("act", lambda e, i=i, yp=yp: e.activation(out=junk[:], in_=yp[:, :], func=AF.Square, accum_out=ss3[:, i:i + 1]),
                     reads=ypk, writes=["junk", ("ss3", i)])
                p.op("act", lambda e, i=i: e.activation(out=r3[:, i:i + 1], in_=ss3[:, i:i + 1], func=AF.Ln, scale=1.0 / D, bias=EPS),
                     reads=[("ss3", i)], writes=[("r3", i)])
                p.op("act", lambda e, i=i: e.activation(out=r3[:, i:i + 1], in_=r3[:, i:i + 1], func=AF.Exp, scale=-0.5),
                     reads=[("r3", i)], writes=[("r3", i)])
                p.op("dve", lambda e, i=i, yp=yp: e.scalar_tensor_tensor(out=T2[i % 2][:], in0=yp[:, :], scalar=r3[:, i:i + 1], in1=grow[1][:],
                                                                       op0=ALU.mult, op1=ALU.mult),
                     reads=ypk + [("r3", i)], writes=[("T2", i % 2)])
                if tt + 2 < 8:
                    p.dma("sp", [(zt[(i + 2) % 3][:], z1_d[(i + 2) * 128:(i + 3) * 128, :])], writes=[("zt", (i + 2) % 3)])
                p.op("pool", lambda e, i=i: e.tensor_tensor(out=OT_[i % 2][:], in0=T2[i % 2][:], in1=zt[i % 3][:], op=ALU.add),
                     reads=[("T2", i % 2), ("zt", i % 3)], writes=[("OTo", i % 2)])
                p.dma("pool", [(out_d[i * 128:(i + 1) * 128, :], OT_[i % 2][:])], reads=[("OTo", i % 2)], writes=[("outd", i)])
        p.finish()
    return nc


_NC_CACHE = {}


def _host_inputs(inputs):
    f32 = np.float32
    g = lambda k: np.asarray(inputs[k], dtype=f32)
    x, c, ctx, c_ctx = g("x"), g("c"), g("ctx"), g("c_ctx")
    col = lambda v: np.ascontiguousarray(v.reshape(-1, 128).T)
    shared = {
        "cctx": col(c_ctx),
        "w_mod": np.ascontiguousarray(g("w_mod")[0]),
        "bmodP": col(g("b_mod")[0]),
        "bmodR": np.ascontiguousarray(g("b_mod")[0].reshape(1, -1)),
        "npre1P": col(g("norm_pre1")[0]),
        "npre2P": col(g("norm_pre2")[0]),
        "npost1R": np.ascontiguousarray(g("norm_post1")[0].reshape(1, -1)),
        "npost2R": np.ascontiguousarray(g("norm_post2")[0].reshape(1, -1)),
        "w_in": np.ascontiguousarray(g("w_in")[0]),
        "lbP": np.ascontiguousarray(g("hg_lb").reshape(2, 2, 4, 128).transpose(3, 0, 1, 2).reshape(128, 16)),
        "onormP": np.ascontiguousarray(np.stack([g("hg_onorm")[0], g("gla_onorm")[0]], axis=1)),
        "bgkP": np.ascontiguousarray(g("gla_b_gk")[0].reshape(2, 4, 128).transpose(2, 0, 1).reshape(128, 8)),
        "wgk": np.ascontiguousarray(g("gla_w_gk")[0]),
        "w_br_hg": np.ascontiguousarray(g("w_br_hg")[0]),
        "w_br_gla": np.ascontiguousarray(g("w_br_gla")[0]),
        "w_out": np.ascontiguousarray(g("w_out")[0]),
        "w_ff_gate": np.ascontiguousarray(g("w_ff_gate")[0]),
        "w_ff_up": np.ascontiguousarray(g("w_ff_up")[0]),
        "w_ff_down": np.ascontiguousarray(g("w_ff_down")[0]),
    }
    s = np.arange(128)[:, None]
    t = np.arange(128)[None, :]
    same = (s // 64) == (t // 64)
    shared["ident"] = np.eye(128, dtype=f32)
    shared["ones"] = np.ones((128, 128), f32)
    shared["maskF4"] = np.tile((same & (s <= t)).astype(f32), (1, 4))
    shared["maskB4"] = np.tile((same & (s >= t)).astype(f32), (1, 4))
    cmr = np.ones((128, T), f32)
    cmr[:, ::64] = 0.0
    shared["cm"] = cmr
    maps = []
    for b in range(NCORES):
        m = dict(shared)
        m["x"] = np.ascontiguousarray(x[b])
        m["ctxx"] = np.ascontiguousarray(ctx[b])
        m["cx"] = col(c[b])
        maps.append(m)
    return maps


def kernel(**inputs):
    if "nc" not in _NC_CACHE:
        _NC_CACHE["nc"] = build_program()
    nc = _NC_CACHE["nc"]
    maps = _host_inputs(inputs)
    res = run_bass_kernel_spmd(nc, maps, core_ids=list(range(NCORES)))
    out = np.stack([np.asarray(res.results[b]["out"], dtype=np.float32) for b in range(NCORES)], axis=0)
    return out
```

```python
import numpy as np
from contextlib import ExitStack
import concourse.bass as bass
import concourse.mybir as mybir
from concourse.bass_utils import run_bass_kernel_spmd

F32 = mybir.dt.float32
BF16 = mybir.dt.bfloat16
AF = mybir.ActivationFunctionType
ALU = mybir.AluOpType

D = 1024
TC = 256
TL = 2048
T = TC + TL
NT = T // 128
NCH = T // 64
DFF = 2816
FC = DFF // 128
EPS = 1e-6
NCORES = 8
GROUPS = [(0, 256), (256, 512), (768, 512), (1280, 512), (1792, 512)]

ENGS = ["pe", "act", "dve", "pool", "sp"]
NRING = 40
SBUF_BASE = 16576
SBUF_CAP = 229376 - 128


class Prog:
    def __init__(self, nc, stack):
        self.nc = nc
        self.ops = {e: [] for e in ENGS}
        self.semobj = {}
        for e in ENGS:
            self.semobj["s_" + e] = stack.enter_context(nc.semaphore("s_" + e))
        for i in range(NRING):
            self.semobj["r%d" % i] = stack.enter_context(nc.semaphore("r%d" % i))
        self.cnt = {e: 0 for e in ENGS}
        self.seen = {e: {} for e in ENGS}
        self.ring_total = [0] * NRING
        self.ring_next = 0
        self.reg = {}
        self.fence_deps = []

    def _waits(self, eng, deps):
        best = {}
        for d in deps:
            if d is None:
                continue
            key, val = d
            if eng == "pe" and key == "s_pe":
                continue
            if val > best.get(key, 0):
                best[key] = val
        waits = []
        for key, val in best.items():
            if self.seen[eng].get(key, 0) >= val:
                continue
            self.seen[eng][key] = val
            waits.append((key, val))
        return waits

    def _deps(self, reads, writes):
        deps = list(self.fence_deps)
        for r in reads:
            st = self.reg.get(r)
            if st is not None:
                deps.append(st[0])
        for w in writes:
            st = self.reg.get(w)
            if st is not None:
                deps.append(st[0])
                deps += list(st[1].items())
        return deps

    def _record(self, h, reads, writes):
        for r in reads:
            st = self.reg.get(r)
            if st is None:
                st = self.reg[r] = [None, {}]
            if h[1] > st[1].get(h[0], 0):
                st[1][h[0]] = h[1]
        for w in writes:
            self.reg[w] = [h, {}]

    def op(self, eng, fn, reads=(), writes=(), deps=()):
        rec = _Recorder()
        fn(rec)
        specs = rec.specs
        assert specs
        fn = (lambda e, specs=specs: _play(e, specs))
        writes = list(writes) + [r for r in reads if isinstance(r, tuple) and r[0] == "ps"]
        d = self._deps(reads, writes) + list(deps)
        waits = self._waits(eng, d)
        self.cnt[eng] += 1
        h = ("s_" + eng, self.cnt[eng])
        self.ops[eng].append((waits, fn, ("s_" + eng, 1)))
        self._record(h, reads, writes)
        return h

    def dma(self, queue, pairs, reads=(), writes=(), deps=()):
        slot = self.ring_next
        self.ring_next = (self.ring_next + 1) % NRING
        key = "r%d" % slot
        d = self._deps(reads, writes) + list(deps)
        if self.ring_total[slot] > 0:
            d.append((key, self.ring_total[slot]))
        waits = self._waits(queue, d)
        first = True
        for (o, i) in pairs:
            self.ring_total[slot] += 16
            self.ops[queue].append((waits if first else [], (lambda e, o=o, i=i: e.dma_start(out=o, in_=i)), (key, 16)))
            first = False
        h = (key, self.ring_total[slot])
        self._record(h, reads, writes)
        return h

    def fence(self):
        self.fence_deps = [("s_" + e, self.cnt[e]) for e in ENGS if self.cnt[e] > 0] + \
                          [("r%d" % i, self.ring_total[i]) for i in range(NRING) if self.ring_total[i] > 0]
        self.reg = {}

    def finish(self):
        self.fence()
        self.ops["sp"].append((self._waits("sp", self.fence_deps), None, None))
        nc = self.nc

        def replay(e, name):
            for waits, fn, inc in self.ops[name]:
                for key, val in waits:
                    e.wait_ge(self.semobj[key], val)
                if fn is None:
                    continue
                ins = fn(e)
                if inc is not None:
                    ins.then_inc(self.semobj[inc[0]], inc[1])

        with nc.Block() as block:
            block.tensor(lambda e: replay(e, "pe"))
            block.scalar(lambda e: replay(e, "act"))
            block.vector(lambda e: replay(e, "dve"))
            block.gpsimd(lambda e: replay(e, "pool"))
            block.sync(lambda e: replay(e, "sp"))


class _Recorder:
    def __init__(self):
        self.specs = []

    def __getattr__(self, name):
        def f(*a, **k):
            self.specs.append((name, a, k))
            return None
        return f


def _play(e, specs):
    ins = None
    for (name, a, k) in specs:
        ins = getattr(e, name)(*a, **k)
    return ins


class Arena:
    def __init__(self, nc, cap):
        self.nc = nc
        self.lo = SBUF_BASE
        self.hi = cap
        self.n = 0

    @staticmethod
    def _size(shape, dt):
        n = 1
        for s in shape[1:]:
            n *= s
        return n * (4 if dt == F32 else 2)

    def alloc(self, name, shape, dt):
        off = (self.lo + 31) // 32 * 32
        sz = self._size(shape, dt)
        assert off + sz <= self.hi, "SBUF overflow at %s: %d + %d > %d" % (name, off, sz, self.hi)
        self.lo = off + sz
        self.n += 1
        return self.nc.alloc_sbuf_tensor_at("%s_%d" % (name, self.n), shape, dt, offset=off)

    def alloc_top(self, name, shape, dt):
        sz = self._size(shape, dt)
        off = (self.hi - sz) // 32 * 32
        assert off >= self.lo, "SBUF overflow (top) at %s" % name
        self.hi = off
        self.n += 1
        return self.nc.alloc_sbuf_tensor_at("%s_%d" % (name, self.n), shape, dt, offset=off)


def build_program(debug=None):
    nc = bass.Bass("TRN2", target_bir_lowering=False)

    def din(name, shape, dt=F32):
        return nc.dram_tensor(name, list(shape), dt, kind="ExternalInput").ap()

    x_d = din("x", [TL, D])
    ctx_d = din("ctxx", [TC, D])
    cx_d = din("cx", [128, 8])
    cctx_d = din("cctx", [128, 8])
    wmod_d = din("w_mod", [D, 6 * D])
    bmodP_d = din("bmodP", [128, 48])
    bmodR_d = din("bmodR", [1, 6 * D])
    npre1_d = din("npre1P", [128, 8])
    npre2_d = din("npre2P", [128, 8])
    npost1_d = din("npost1R", [1, D])
    npost2_d = din("npost2R", [1, D])
    win_d = din("w_in", [D, 6688])
    lbP_d = din("lbP", [128, 16])
    onorm_d = din("onormP", [128, 2])
    bgk_d = din("bgkP", [128, 8])
    wgk_d = din("wgk", [2, 16, 512])
    wbrh_d = din("w_br_hg", [512, D])
    wbrg_d = din("w_br_gla", [512, D])
    wout_d = din("w_out", [D, D])
    wfg_d = din("w_ff_gate", [D, DFF])
    wfu_d = din("w_ff_up", [D, DFF])
    wfd_d = din("w_ff_down", [DFF, D])
    ident_d = din("ident", [128, 128])
    ones_d = din("ones", [128, 128])
    maskF_d = din("maskF4", [128, 512])
    maskB_d = din("maskB4", [128, 512])
    cm_d = din("cm", [128, T])
    out_d = nc.dram_tensor("out", [TL, D], F32, kind="ExternalOutput").ap()
    z1_d = nc.dram_tensor("z1_scratch", [TL, D], F32, kind="Internal").ap()
    ob_d = nc.dram_tensor("ob_scratch", [8, 128, TL], BF16, kind="Internal").ap()
    dbg_d = {}
    if debug:
        for name, shape in debug.items():
            if name.startswith("_"):
                continue
            dbg_d[name] = nc.dram_tensor("dbg_" + name, list(shape), F32, kind="ExternalOutput").ap()

    with ExitStack() as st:
        p = Prog(nc, st)
        ar = Arena(nc, SBUF_CAP)
        PSB = [st.enter_context(nc.psum_tensor("psb%d" % i, [128, 1024], F32)) for i in range(4)]

        def bank(i):
            return PSB[i // 2][:, (i % 2) * 512:(i % 2 + 1) * 512]

        def pk(i):
            return ("ps", i)

        def dbg(name, src_ap, reads, shape):
            if not debug or name not in debug:
                return
            m = ar.lo
            tmp = ar.alloc("dbgtmp", list(shape), F32)
            p.op("pool", lambda e: e.tensor_copy(out=tmp[:], in_=src_ap), reads=reads, writes=[("dbgtmp", name)])
            p.dma("sp", [(dbg_d[name], tmp[:])], reads=[("dbgtmp", name)])
            p.fence()
            ar.lo = m

        identb = ar.alloc("identb", [128, 128], BF16)
        onesb = ar.alloc("onesb", [128, 128], BF16)
        maskF = ar.alloc("maskF", [128, 512], BF16)
        maskB = ar.alloc("maskB", [128, 512], BF16)
        cm = ar.alloc("cm", [128, T], BF16)
        p.dma("pool", [(identb[:], ident_d), (onesb[:], ones_d), (maskF[:], maskF_d), (maskB[:], maskB_d),
                       (cm[:, 0:1152], cm_d[:, 0:1152]), (cm[:, 1152:T], cm_d[:, 1152:T])],
              writes=["consts"])
        modP = ar.alloc("modP", [128, 6, 8, 2], F32)
        bmodP = ar.alloc("bmodP", [128, 48], F32)
        npre1 = ar.alloc("npre1", [128, 8], F32)
        npre2 = ar.alloc("npre2", [128, 8], F32)
        lbP = ar.alloc("lbP", [128, 16], F32)
        onormP = ar.alloc("onormP", [128, 2], F32)
        bgkP = ar.alloc("bgkP", [128, 8], F32)
        cx = ar.alloc("cx", [128, 8], F32)
        cc = ar.alloc("cc", [128, 8], F32)
        p.dma("sp", [(bmodP[:], bmodP_d), (npre1[:], npre1_d), (npre2[:], npre2_d), (lbP[:], lbP_d),
                     (onormP[:], onorm_d), (bgkP[:], bgk_d), (cx[:], cx_d), (cc[:], cctx_d)], writes=["smallin"])
        vec = {}
        for nm in ["sc1x", "sh1x", "sc1c", "sh1c", "sc2x", "sh2x", "lb", "oml", "noml", "nbgk"]:
            vec[nm] = ar.alloc(nm, [128, 8], F32)
        cs = ar.alloc("cs", [128, 8, 2], BF16)
        csrep = ar.alloc("csrep", [128, 8, 128], BF16)
        sg = ar.alloc("sgc", [128, 8, 2], F32)
        p.op("act", lambda e: e.activation(out=sg[:, :, 0], in_=cx[:], func=AF.Sigmoid), reads=["smallin"], writes=["sg0"])
        p.op("act", lambda e: e.activation(out=sg[:, :, 1], in_=cc[:], func=AF.Sigmoid), reads=["smallin"], writes=["sg1"])
        p.op("dve", lambda e: e.tensor_tensor(out=cs[:, :, 0], in0=sg[:, :, 0], in1=cx[:], op=ALU.mult), reads=["sg0"], writes=["cs0"])
        p.op("dve", lambda e: e.tensor_tensor(out=cs[:, :, 1], in0=sg[:, :, 1], in1=cc[:], op=ALU.mult), reads=["sg1"], writes=["cs1"])
        p.op("dve", lambda e: e.tensor_copy(out=csrep[:], in_=cs[:, :, 0:1].broadcast_to([128, 8, 128])), reads=["cs0"], writes=["csrep"])
        p.op("dve", lambda e: e.tensor_tensor(out=vec["lb"][:], in0=lbP[:, 0:8], in1=lbP[:, 8:16], op=ALU.subtract), reads=["smallin"], writes=["lbd"])
        p.op("act", lambda e: e.activation(out=vec["lb"][:], in_=vec["lb"][:], func=AF.Sigmoid), reads=["lbd"], writes=["lb"])
        p.op("dve", lambda e: e.tensor_scalar(out=vec["oml"][:], in0=vec["lb"][:], scalar1=-1.0, scalar2=1.0, op0=ALU.mult, op1=ALU.add), reads=["lb"], writes=["oml"])
        p.op("dve", lambda e: e.tensor_scalar(out=vec["noml"][:], in0=vec["lb"][:], scalar1=-1.0, scalar2=None, op0=ALU.add), reads=["lb"], writes=["noml"])
        p.op("dve", lambda e: e.tensor_scalar(out=vec["nbgk"][:], in0=bgkP[:], scalar1=-1.0, scalar2=None, op0=ALU.mult), reads=["smallin"], writes=["nbgk"])

        if debug and debug.get("_stop") == "s0a":
            p.finish()
            return nc
        def mod_pp(j, wm, wkey):
            psv = bank(0)[:, 0:16].rearrange("p (n t) -> p n t", t=2)

            def g(e):
                last = None
                for nchk in range(8):
                    for kc in range(8):
                        last = e.matmul(psv[:, nchk, :], lhsT=wm[:, kc, nchk * 128:(nchk + 1) * 128], rhs=cs[:, kc, :],
                                        start=(kc == 0), stop=(kc == 7))
                return last
            p.op("pe", g, reads=[wkey, "cs0", "cs1"], writes=[pk(0)])
            p.op("dve", lambda e: e.tensor_tensor(out=modP[:, j], in0=psv,
                                                 in1=bmodP[:, j * 8:(j + 1) * 8].unsqueeze(2).broadcast_to([128, 8, 2]), op=ALU.add),
                 reads=[pk(0), "smallin"], writes=[("modP", j)])

        def load_wm(j, wm, wkey):
            p.dma("pool", [(wm[:], wmod_d[:, j * 1024:(j + 1) * 1024].rearrange("(c p) n -> p c n", p=128))], writes=[wkey])

        m_small = ar.lo
        hT = ar.alloc("hT", [128, 8, T], BF16)
        m_persist = ar.lo

        wmb = [ar.alloc("wm%d" % i, [128, 8, 1024], BF16) for i in range(2)]
        xb = [ar.alloc("xb%d" % i, [128, D], F32) for i in range(3)]
        NXN = 8
        xnb = [ar.alloc("xn%d" % i, [128, D], BF16) for i in range(NXN)]
        junk = ar.alloc("junk", [128, D], BF16)
        ssq = ar.alloc("ssq", [128, NT], F32)
        rstd = ar.alloc("rstd", [128, NT], F32)

        load_wm(0, wmb[0], ("wm", 0))
        load_wm(1, wmb[1], ("wm", 1))
        mod_pp(0, wmb[0], ("wm", 0))
        mod_pp(1, wmb[1], ("wm", 1))
        if debug and debug.get("_stop") == "s0b":
            p.finish()
            return nc
        for which, scn, shn in ((0, "sc1x", "sh1x"), (1, "sc1c", "sh1c")):
            p.op("dve", lambda e, which=which, scn=scn: e.scalar_tensor_tensor(out=vec[scn][:], in0=modP[:, 1, :, which], scalar=1.0, in1=npre1[:],
                                                                              op0=ALU.add, op1=ALU.mult),
                 reads=[("modP", 1), "smallin"], writes=[scn])
            p.op("dve", lambda e, which=which, shn=shn: e.tensor_copy(out=vec[shn][:], in_=modP[:, 0, :, which]), reads=[("modP", 0)], writes=[shn])

        def norm_transpose(i, src_ap, srckey, xt, xtkey, xn, xnkey, sc, sh, sckeys, dst, dstkey, col0, ssq_t, rstd_t, load=True, pair=3, phase="all"):
            if phase in ("all", "norm"):
                norm_part(i, src_ap, srckey, xt, xtkey, xn, xnkey, ssq_t, rstd_t, load)
            if phase in ("all", "tr"):
                tr_part(i, xn, xnkey, sc, sh, sckeys, dst, dstkey, col0, pair)

        def norm_part(i, src_ap, srckey, xt, xtkey, xn, xnkey, ssq_t, rstd_t, load):
            if load:
                p.dma("sp", [(xt[:], src_ap)], reads=[srckey] if srckey else [], writes=[xtkey])
            p.op("act", lambda e: e.activation(out=junk[:], in_=xt[:], func=AF.Square, accum_out=ssq_t[:, i:i + 1]),
                 reads=[xtkey], writes=["junk", ("ssq", i)])
            p.op("act", lambda e: e.activation(out=rstd_t[:, i:i + 1], in_=ssq_t[:, i:i + 1], func=AF.Ln, scale=1.0 / D, bias=EPS),
                 reads=[("ssq", i)], writes=[("rstd", i)])
            p.op("act", lambda e: e.activation(out=rstd_t[:, i:i + 1], in_=rstd_t[:, i:i + 1], func=AF.Exp, scale=-0.5),
                 reads=[("rstd", i)], writes=[("rstd", i)])
            p.op("dve", lambda e: e.tensor_scalar(out=xn[:], in0=xt[:], scalar1=rstd_t[:, i:i + 1], scalar2=None, op0=ALU.mult),
                 reads=[xtkey, ("rstd", i)], writes=[xnkey])

        def tr_part(i, xn, xnkey, sc, sh, sckeys, dst, dstkey, col0, pair):
            ptb = PSB[pair][:, :].rearrange("p (k t) -> p k t", t=128)
            pkeys = [pk(2 * pair), pk(2 * pair + 1)]

            def g(e):
                for kc in range(8):
                    e.matmul(ptb[:, kc, :], lhsT=xn[:, kc * 128:(kc + 1) * 128], rhs=identb[:], start=True, stop=True)
            p.op("pe", g, reads=[xnkey, "consts"], writes=pkeys)
            for kk in range(4):
                for kc, eng in ((kk, "act"), (kk + 4, "dve")):
                    o = dst[:, kc, col0:col0 + 128]
                    bkey = [pkeys[0] if kc < 4 else pkeys[1]]
                    if eng == "act":
                        p.op("act", lambda e: e.activation(out=o, in_=ptb[:, kc, :], func=AF.Identity, scale=sc[:, kc:kc + 1], bias=sh[:, kc:kc + 1]),
                             reads=bkey + sckeys, writes=[(dstkey, i, kc)])
                    else:
                        p.op("dve", lambda e: e.tensor_scalar(out=o, in0=ptb[:, kc, :], scalar1=sc[:, kc:kc + 1], scalar2=sh[:, kc:kc + 1],
                                                             op0=ALU.mult, op1=ALU.add),
                             reads=bkey + sckeys, writes=[(dstkey, i, kc)])

        s1_groups = [(0, 2), (2, 4), (6, 4), (10, 4), (14, 4)]

        def s1_norm(i):
            src = ctx_d[i * 128:(i + 1) * 128, :] if i < 2 else x_d[(i - 2) * 128:(i - 1) * 128, :]
            norm_part(i, src, None, xb[i % 3], ("xb", i % 3), xnb[i % NXN], ("xn", i % NXN), ssq, rstd, True)

        def s1_tr(t0, nt):
            sc, sh, keys = (vec["sc1c"], vec["sh1c"], ["sc1c", "sh1c"]) if t0 < 2 else (vec["sc1x"], vec["sh1x"], ["sc1x", "sh1x"])

            def g(e):
                for kc in range(8):
                    for t in range(nt):
                        i = t0 + t
                        e.matmul(bank(kc)[:, t * 128:(t + 1) * 128], lhsT=xnb[i % NXN][:, kc * 128:(kc + 1) * 128], rhs=identb[:],
                                 start=True, stop=True)
            p.op("pe", g, reads=[("xn", (t0 + t) % NXN) for t in range(nt)] + ["consts"], writes=[pk(k_) for k_ in range(8)])
            for kk in range(4):
                for kc, eng in ((kk, "act"), (kk + 4, "dve")):
                    o = hT[:, kc, t0 * 128:(t0 + nt) * 128]
                    srcp = bank(kc)[:, 0:nt * 128]
                    wk_ = [("hT", t0 + t, kc) for t in range(nt)]
                    if eng == "act":
                        p.op("act", lambda e: e.activation(out=o, in_=srcp, func=AF.Identity, scale=sc[:, kc:kc + 1], bias=sh[:, kc:kc + 1]),
                             reads=[pk(kc)] + keys, writes=wk_)
                    else:
                        p.op("dve", lambda e: e.tensor_scalar(out=o, in0=srcp, scalar1=sc[:, kc:kc + 1], scalar2=sh[:, kc:kc + 1],
                                                             op0=ALU.mult, op1=ALU.add),
                             reads=[pk(kc)] + keys, writes=wk_)

        for gi_, (t0_, nt_) in enumerate(s1_groups):
            for i in range(t0_, t0_ + nt_):
                s1_norm(i)
            if gi_ >= 1:
                s1_tr(*s1_groups[gi_ - 1])
        s1_tr(*s1_groups[-1])
        dbg("hT", hT[:, 0, :], [], [128, T])
        if debug and debug.get("_stop") == "s1":
            p.finish()
            return nc
        p.fence()
        ar.lo = m_persist

        def hT_keys(s, n):
            return [("hT", i, kc) for i in range(s // 128, (s + n) // 128) for kc in range(8)]

        wq = ar.alloc("wq", [128, 8, 128], BF16)
        wv = ar.alloc("wv", [128, 8, 128], BF16)
        wa = ar.alloc("wa", [128, 8, 128], BF16)
        wb = ar.alloc("wb", [128, 8, 128], BF16)
        wg = ar.alloc("wg", [128, 8, 128], BF16)
        wlr = ar.alloc("wlr", [128, 8, 32], BF16)
        wgk = ar.alloc("wgk", [16, 2, 512], BF16)
        X1 = ar.alloc("X1", [128, T], F32)
        CUM = ar.alloc("CUM", [128, T], F32)
        Kb = ar.alloc("Kb", [128, T], BF16)
        QT = ar.alloc("QT", [128, T], BF16)
        GT2 = [ar.alloc("GT%d" % i, [128, TL], BF16) for i in range(2)]
        VTM2 = [ar.alloc("VTM%d" % i, [128, NT, 128], BF16) for i in range(2)]
        Qt2 = [[ar.alloc("Qt%d%d" % (i, d), [128, T], BF16) for d in range(2)] for i in range(2)]
        Kt2 = [[ar.alloc("Kt%d%d" % (i, d), [128, T], BF16) for d in range(2)] for i in range(2)]
        smd2 = [[{nm: ar.alloc("%s%d%d" % (nm, i, d), [128, NCH], F32) for nm in ("MID", "DL", "E")} for d in range(2)] for i in range(2)]
        KTM = ar.alloc("KTM", [128, NT, 128], BF16)
        DS = ar.alloc("DS", [128, 128, NCH + 1], F32)
        DBC = ar.alloc("DBC", [128, 16, NCH + 1], F32)
        SPv = [ar.alloc("SPv%d" % d, [128, 128, NCH + 1], BF16) for d in range(2)]
        MS = [ar.alloc("MS%d" % d, [128, 512], BF16) for d in range(2)]
        for d_ in range(2):
            p.op("pool", lambda e: e.memset(MS[d_][:], 0.0), writes=[("MS", d_)])
        SQ = ar.alloc("SQ", [128, 512], BF16)
        Rr = ar.alloc("Rr", [128, 512], F32)
        ON = ar.alloc("ON", [128, 512], F32)
        OBt = [ar.alloc("OBt%d" % i, [128, 512], BF16) for i in range(2)]
        Dsc = ar.alloc("Dsc", [128, NCH + 1], F32)
        tmpA = ar.alloc("tmpA", [128, NCH], F32)
        tmpB = ar.alloc("tmpB", [128, NCH], F32)
        p.op("pool", lambda e: e.memset(Dsc[:], 0.0), writes=["Dsc"])
        p.op("pool", lambda e: e.memset(DS[:, :, 0:1], 0.0), writes=[("DS0",)])
        lrT = ar.alloc("lrT", [16, 2, T], BF16)
        p.dma("pool", [(wlr[:], win_d[:, 4608:4640].rearrange("(c p) n -> p c n", p=128)),
                       (wgk[:], wgk_d.rearrange("d r n -> r d n"))], writes=["wlr", "wgk"])
        GS = [s_ for (s_, n_) in GROUPS]
        X1b = X1[:].bitcast(BF16)
        nheads = 8 if not debug else debug.get("_nheads", [8])[0]

        def wcol(off):
            return win_d[:, off:off + 128].rearrange("(c p) n -> p c n", p=128)

        def head_p1(hh):
            br, hd, hp = hh // 4, hh % 4, hh % 2
            gs = 1.0 if br == 0 else -1.0 / 16.0
            GT, VTM, Qt, Kt = GT2[hp], VTM2[hp], Qt2[hp], Kt2[hp]
            def load_w(h2, which):
                if h2 >= nheads:
                    return
                b2, d2 = h2 // 4, h2 % 4
                if b2 == 0:
                    o2 = dict(q=d2 * 128, v=512 + d2 * 128, a=1024 + d2 * 128, b=1536 + d2 * 128, g=2048 + d2 * 128)
                else:
                    o2 = dict(q=2560 + d2 * 128, a=3072 + d2 * 128, v=3584 + d2 * 128, g=4096 + d2 * 128)
                bufs = dict(q=wq, v=wv, a=wa, b=wb, g=wg)
                for w_ in which:
                    if w_ in o2:
                        p.dma("pool", [(bufs[w_][:], wcol(o2[w_]))], writes=["w" + w_])
            if hh == 0:
                load_w(0, "qvgab")
            yield

            def proj_group(wt, wkey, gi, s, n, evac):
                b = gi % 2
                ps = bank(b)[:, 0:n]

                def g(e):
                    for kc in range(8):
                        e.matmul(ps, lhsT=wt[:, kc, :], rhs=hT[:, kc, s:s + n], start=(kc == 0), stop=(kc == 7))
                p.op("pe", g, reads=[wkey] + hT_keys(s, n), writes=[pk(b)])
                evac(s, n, ps, pk(b))

            if hh == 4:
                for d in range(2):
                    for gi, (s, n) in enumerate(GROUPS):
                        b = gi % 2
                        ps = bank(b)[0:16, 0:n]

                        def g(e):
                            for kc in range(8):
                                e.matmul(ps, lhsT=wlr[:, kc, d * 16:(d + 1) * 16], rhs=hT[:, kc, s:s + n], start=(kc == 0), stop=(kc == 7))
                        p.op("pe", g, reads=["wlr"] + hT_keys(s, n), writes=[pk(b)])
                        p.op("act", lambda e: e.activation(out=lrT[:, d, s:s + n], in_=ps, func=AF.Copy), reads=[pk(b)], writes=[("lrT", d, s)])
                        yield

            def sig_exp(s, n, ps, pkey):
                xs = X1[:, s:s + n]
                p.op("act", lambda e: e.activation(out=xs, in_=ps, func=AF.Exp, scale=-1.0), reads=[pkey], writes=[("X1", s), ("XA", s), ("XB", s)])
                p.op("act", lambda e: e.activation(out=xs, in_=xs, func=AF.Ln, bias=1.0), reads=[("X1", s)], writes=[("X1", s)])
                p.op("act", lambda e: e.activation(out=xs, in_=xs, func=AF.Exp, scale=-1.0), reads=[("X1", s)], writes=[("X1", s)])

            if br == 0:
                def evq(s, n, ps, pkey):
                    sig_exp(s, n, ps, pkey)
                    p.op("dve", lambda e: e.tensor_tensor(out=QT[:, s:s + n], in0=ps, in1=X1[:, s:s + n], op=ALU.mult),
                         reads=[pkey, ("X1", s)], writes=[("QT", s)])
            else:
                def evq(s, n, ps, pkey):
                    p.op("act", lambda e: e.activation(out=QT[:, s:s + n], in_=ps, func=AF.Copy, scale=128.0 ** -0.5), reads=[pkey], writes=[("QT", s)])

            def evg(s, n, ps, pkey):
                sig_exp(s, n, ps, pkey)
                p.op("dve", lambda e: e.tensor_tensor(out=GT[:, s - TC:s - TC + n], in0=ps, in1=X1[:, s:s + n], op=ALU.mult),
                     reads=[pkey, ("X1", s)], writes=[("GT", hp, s)])

            def v_group(i0):
                nt4 = min(4, NT - i0)
                psv = bank(2)

                def g(e):
                    for tl in range(nt4):
                        i = i0 + tl
                        for kc in range(8):
                            e.matmul(psv[:, tl * 128:(tl + 1) * 128], lhsT=hT[:, kc, i * 128:(i + 1) * 128], rhs=wv[:, kc, :],
                                     start=(kc == 0), stop=(kc == 7))
                p.op("pe", g, reads=["wv"] + hT_keys(i0 * 128, nt4 * 128), writes=[pk(2)])
                p.op("act", lambda e: e.activation(out=VTM[:, i0:i0 + nt4, :], in_=psv[:, 0:nt4 * 128].rearrange("p (t v) -> p t v", v=128), func=AF.Copy),
                     reads=[pk(2)], writes=[("VTM", hp, i0)])

            for gi, (s, n) in enumerate(GROUPS):
                if gi >= 1:
                    proj_group(wq, "wq", gi, s, n, evq)
                    proj_group(wg, "wg", gi + 1, s, n, evg)
                v_group(gi * 4)
                yield
            if br == 1:
                def evk(s, n, ps, pkey):
                    p.op("act", lambda e: e.activation(out=Kb[:, s:s + n], in_=ps, func=AF.Copy), reads=[pkey], writes=[("Kb", s)])
                for gi, (s, n) in enumerate(GROUPS):
                    proj_group(wa, "wa", gi, s, n, evk)
                    yield
            load_w(hh + 1, "qvg" if br == 0 else "qvga")

            for d in range(2):
                if d == 1:
                    yield "SPLIT"
                dh = d * 4 + hd
                smd = smd2[hp][d]
                pmid, pend = (31, 63) if d == 0 else (32, 0)
                chains = []
                for gi, (s, n) in enumerate(GROUPS):
                    c0, ncg = s // 64, n // 64
                    b = gi % 2
                    ps = bank(b)[:, 0:n]
                    xs = X1[:, s:s + n]
                    cu = CUM[:, s:s + n]
                    c3 = cu.rearrange("p (c l) -> p c l", l=64)
                    x3 = xs.rearrange("p (c l) -> p c l", l=64)
                    kX, kC, kK = ("X1", s), ("CUM", s), ("Kb", s)
                    if br == 0:
                        wt, wkey = (wa, "wa") if d == 0 else (wb, "wb")

                        def e0(s=s, n=n, ps=ps, b=b, wt=wt, wkey=wkey, xs=xs, kX=kX):
                            def g(e):
                                for kc in range(8):
                                    e.matmul(ps, lhsT=wt[:, kc, :], rhs=hT[:, kc, s:s + n], start=(kc == 0), stop=(kc == 7))
                            p.op("pe", g, reads=[wkey] + hT_keys(s, n), writes=[pk(b)])
                            p.op("act", lambda e: e.activation(out=xs, in_=ps, func=AF.Exp, scale=-1.0), reads=[pk(b)], writes=[kX, ("XA", s), ("XB", s)])

                        def e1(s=s, n=n, xs=xs, cu=cu, kX=kX, kK=kK, kC=kC):
                            p.op("act", lambda e: e.activation(out=cu, in_=xs, func=AF.Ln, bias=1.0), reads=[kX], writes=[kC])
                            p.op("act", lambda e: e.activation(out=xs, in_=cu, func=AF.Exp, scale=-1.0), reads=[kC], writes=[kX])
                            p.op("act", lambda e: e.activation(out=Kb[:, s:s + n], in_=xs, func=AF.Identity, scale=vec["noml"][:, dh:dh + 1],
                                                              bias=vec["oml"][:, dh:dh + 1]),
                                 reads=[kX, "noml", "oml"], writes=[kK])
                            p.op("act", lambda e: e.activation(out=xs, in_=xs, func=AF.Ln, scale=vec["oml"][:, dh:dh + 1], bias=vec["lb"][:, dh:dh + 1]),
                                 reads=[kX, "oml", "lb"], writes=[kX])
                    else:
                        def e0(s=s, n=n, ps=ps, b=b, xs=xs, kX=kX):
                            p.op("pe", lambda e: e.matmul(ps, lhsT=wgk[:, d, hd * 128:(hd + 1) * 128], rhs=lrT[:, d, s:s + n], start=True, stop=True),
                                 reads=["wgk", ("lrT", d, s)], writes=[pk(b)])
                            p.op("act", lambda e: e.activation(out=xs, in_=ps, func=AF.Exp, scale=-1.0, bias=vec["nbgk"][:, dh:dh + 1]),
                                 reads=[pk(b), "nbgk"], writes=[kX, ("XA", s), ("XB", s)])

                        def e1(xs=xs, kX=kX):
                            p.op("act", lambda e: e.activation(out=xs, in_=xs, func=AF.Ln, bias=1.0), reads=[kX], writes=[kX])

                    def e2(s=s, n=n, xs=xs, cu=cu, kX=kX, kC=kC):
                        if d == 0:
                            p.op("dve", lambda e: e.tensor_tensor_scan(out=cu, data0=cm[:, 0:n], data1=xs, initial=0.0, op0=ALU.mult, op1=ALU.add),
                                 reads=[kX, "consts"], writes=[kC])
                        else:
                            p.op("dve", lambda e: e.tensor_tensor_scan(out=cu[:, ::-1], data0=cm[:, 0:n], data1=xs[:, ::-1], initial=0.0,
                                                                      op0=ALU.mult, op1=ALU.add),
                                 reads=[kX, "consts"], writes=[kC])

                    def e3(s=s, c0=c0, ncg=ncg, c3=c3, kC=kC):
                        p.op("pool", lambda e: e.tensor_copy(out=smd["MID"][:, c0:c0 + ncg], in_=c3[:, :, pmid]), reads=[kC], writes=[("MID", hp, d, s)])
                        p.op("pool", lambda e: e.tensor_copy(out=smd["DL"][:, c0:c0 + ncg], in_=c3[:, :, pend]), reads=[kC], writes=[("DL", hp, d, s)])

                    def e4(s=s, c0=c0, ncg=ncg, c3=c3, kC=kC):
                        p.op("dve", lambda e: e.tensor_tensor(out=c3, in0=c3, in1=smd["MID"][:, c0:c0 + ncg].unsqueeze(2).broadcast_to([128, ncg, 64]),
                                                             op=ALU.subtract),
                             reads=[kC, ("MID", hp, d, s)], writes=[kC])

                    xa = X1b[:, 2 * s:2 * s + n]
                    xb_ = X1b[:, 2 * s + n:2 * s + 2 * n]

                    def e5(gi=gi, s=s, cu=cu, xa=xa, kX=kX, kC=kC):
                        if gi >= 1:
                            p.op("act", lambda e: e.activation(out=xa, in_=cu, func=AF.Exp, scale=gs), reads=[kC, kX], writes=[("XA", s)])

                    def e6(gi=gi, s=s, n=n, xa=xa):
                        if gi >= 1:
                            p.op("dve", lambda e: e.tensor_tensor(out=Qt[d][:, s:s + n], in0=QT[:, s:s + n], in1=xa, op=ALU.mult),
                                 reads=[("XA", s), ("QT", s)], writes=[("Qt", hp, d, s)])

                    def e7(s=s, cu=cu, xb_=xb_, kX=kX, kC=kC):
                        p.op("act", lambda e: e.activation(out=xb_, in_=cu, func=AF.Exp, scale=-gs), reads=[kC, kX], writes=[("XB", s)])

                    def e8(s=s, n=n, xb_=xb_, kK=kK):
                        p.op("dve", lambda e: e.tensor_tensor(out=Kt[d][:, s:s + n], in0=Kb[:, s:s + n], in1=xb_, op=ALU.mult),
                             reads=[("XB", s), kK], writes=[("Kt", hp, d, s)])
                    chains.append([e0, e1, e2, e3, e4, e5, e6, e7, e8])
                nel = len(chains[0])
                for step in range(nel + len(chains) - 1):
                    for gi in range(len(chains)):
                        k = step - gi
                        if 0 <= k < nel:
                            chains[gi][k]()
                    yield
            if br == 0:
                load_w(hh + 1, "ab")

        def head_p2(hh):
            br, hd, hp = hh // 4, hh % 4, hh % 2
            gs = 1.0 if br == 0 else -1.0 / 16.0
            GT, VTM, Qt, Kt = GT2[hp], VTM2[hp], Qt2[hp], Kt2[hp]
            allk = lambda nm, d: [(nm, hp, d, s_) for s_ in GS]
            for d in range(2):
                if d == 1:
                    yield "SPLIT"
                smd = smd2[hp][d]
                MIDt, DLt = smd["MID"], smd["DL"]
                p.op("pool", lambda e: e.tensor_tensor(out=tmpA[:], in0=DLt[:], in1=MIDt[:], op=ALU.subtract),
                     reads=allk("DL", d) + allk("MID", d), writes=["tmpA"])
                if d == 0:
                    p.op("pool", lambda e: e.tensor_tensor(out=tmpB[:, 0:35], in0=MIDt[:, 1:36], in1=tmpA[:, 0:35], op=ALU.add),
                         reads=["tmpA"] + allk("MID", d), writes=["tmpB"])
                    p.op("act", lambda e: e.activation(out=Dsc[:, 1:36], in_=tmpB[:, 0:35], func=AF.Exp, scale=gs), reads=["tmpB"], writes=["Dsc"])
                else:
                    p.op("pool", lambda e: e.tensor_tensor(out=tmpB[:, 1:36], in0=MIDt[:, 0:35], in1=tmpA[:, 1:36], op=ALU.add),
                         reads=["tmpA"] + allk("MID", d), writes=["tmpB"])
                    p.op("pool", lambda e: e.tensor_tensor(out=tmpB[:, 0:1], in0=MIDt[:, 35:36], in1=tmpA[:, 0:1], op=ALU.add),
                         reads=["tmpA", "tmpB"] + allk("MID", d), writes=["tmpB"])
                    p.op("act", lambda e: e.activation(out=Dsc[:, 1:5], in_=tmpB[:, 3::-1], func=AF.Exp, scale=gs), reads=["tmpB"], writes=["Dsc"])
                    p.op("act", lambda e: e.activation(out=Dsc[:, 5:36], in_=tmpB[:, 35:4:-1], func=AF.Exp, scale=gs), reads=["tmpB", "Dsc"], writes=["Dsc"])
                p.op("pool", lambda e: e.tensor_copy(out=DBC[:], in_=Dsc[:].unsqueeze(1).broadcast_to([128, 16, NCH + 1])), reads=["Dsc"], writes=["DBC"])
                yield
                for i0 in range(0, NT, 4):
                    nt4 = min(4, NT - i0)
                    b = 3 + (i0 // 4) % 2
                    psk = bank(b)

                    def g(e):
                        for tl in range(nt4):
                            i = i0 + tl
                            e.matmul(psk[:, tl * 128:(tl + 1) * 128], lhsT=Kt[d][:, i * 128:(i + 1) * 128], rhs=identb[:], start=True, stop=True)
                    p.op("pe", g, reads=allk("Kt", d) + ["consts"], writes=[pk(b)])
                    p.op("act", lambda e: e.activation(out=KTM[:, i0:i0 + nt4, :], in_=psk[:, 0:nt4 * 128].rearrange("p (t v) -> p t v", v=128), func=AF.Copy),
                         reads=[pk(b)], writes=[("KTM", i0)])
                    yield
                for si, (ti0, ntl) in enumerate(((0, 2), (2, 4), (6, 4), (10, 4), (14, 4))):
                    for half in range(2):
                        b = 3 + half
                        psd = bank(b)

                        def g(e):
                            for m in range(ntl):
                                i = ti0 + m
                                e.matmul(psd[:, m * 128:(m + 1) * 128], lhsT=KTM[half * 64:(half + 1) * 64, i, :],
                                         rhs=VTM[half * 64:(half + 1) * 64, i, :], start=True, stop=True)
                        p.op("pe", g, reads=[("KTM", i0_) for i0_ in range(0, NT, 4)] + [("VTM", hp, i0_) for i0_ in range(0, NT, 4)], writes=[pk(b)])
                        cfirst = 2 * ti0 + half
                        if d == 0:
                            dsv = DS[:, :, 1 + cfirst:1 + cfirst + 2 * ntl - 1:2]
                        elif ti0 == 0:
                            dsv = DS[:, :, 4 - half:4 - half - 2 * ntl + 1:-2]
                        else:
                            jst = 40 - cfirst
                            dsv = DS[:, :, jst:jst - 2 * ntl + 1:-2]
                        p.op("act", lambda e: e.activation(out=dsv.rearrange("p v j -> p j v"),
                                                          in_=psd[:, 0:ntl * 128].rearrange("p (c v) -> p c v", v=128), func=AF.Copy),
                             reads=[pk(b)], writes=[("DS", si, half)])
                    yield
                dsk = [("DS", si, half) for si in range(5) for half in range(2)]
                for qv in range(4):
                    for hv in range(2):
                        v0 = qv * 32 + hv * 16
                        v2 = DS[:, v0:v0 + 16, :].rearrange("p v j -> p (v j)")
                        o2 = SPv[d][:, v0:v0 + 16, :].rearrange("p v j -> p (v j)")
                        p.op("dve", lambda e: e.tensor_tensor_scan(out=o2, data0=v2, data1=DBC[:].rearrange("p v j -> p (v j)"), initial=0.0,
                                                                  op0=ALU.add, op1=ALU.mult),
                             reads=dsk + ["DBC", ("DS0",)], writes=[("SP", d, qv, hv)])
                    yield
            spk = [("SP", d_, q_, h_) for d_ in range(2) for q_ in range(4) for h_ in range(2)]
            chains = []
            for gl in range(4):
                s0 = 256 + gl * 512
                otb = 5 + gl % 2
                pot = bank(otb)
                obt = OBt[gl % 2]

                def f0(gl=gl, s0=s0):
                    for d in range(2):
                        psc = bank(3 + d)

                        def g(e):
                            for tl in range(4):
                                i = 2 + 4 * gl + tl
                                e.matmul(psc[:, tl * 128:(tl + 1) * 128], lhsT=Kt[d][:, i * 128:(i + 1) * 128], rhs=Qt[d][:, i * 128:(i + 1) * 128],
                                         start=True, stop=True)
                        p.op("pe", g, reads=[("Kt", hp, d, s0), ("Qt", hp, d, s0)], writes=[pk(3 + d)])
                        p.op("dve", lambda e: e.copy_predicated(out=MS[d][:], mask=(maskF if d == 0 else maskB)[:].bitcast(mybir.dt.uint16), data=psc),
                             reads=[pk(3 + d), "consts"], writes=[("MS", d)])

                def f1(gl=gl, s0=s0, otb=otb, pot=pot):
                    def g(e):
                        for tl in range(4):
                            i = 2 + 4 * gl + tl
                            cols = slice(tl * 128, (tl + 1) * 128)
                            e.matmul(pot[:, cols], lhsT=VTM[:, i, :], rhs=MS[0][:, cols], start=True, stop=False)
                            e.matmul(pot[:, cols], lhsT=VTM[:, i, :], rhs=MS[1][:, cols], start=False, stop=False)
                            for half in range(2):
                                c = 2 * i + half
                                cc_ = slice(tl * 128 + half * 64, tl * 128 + half * 64 + 64)
                                e.matmul(pot[:, cc_], lhsT=SPv[0][:, :, c], rhs=Qt[0][:, c * 64:(c + 1) * 64], start=False, stop=False)
                                e.matmul(pot[:, cc_], lhsT=SPv[1][:, :, 39 - c], rhs=Qt[1][:, c * 64:(c + 1) * 64], start=False, stop=True)
                    p.op("pe", g, reads=[("MS", 0), ("MS", 1), ("Qt", hp, 0, s0), ("Qt", hp, 1, s0)] + spk + [("VTM", hp, i0_) for i0_ in range(0, NT, 4)],
                         writes=[pk(otb)])

                def f2(otb=otb, pot=pot):
                    p.op("act", lambda e: e.activation(out=SQ[:], in_=pot, func=AF.Square), reads=[pk(otb)], writes=["SQ"])
                    p.op("pe", lambda e: e.matmul(bank(7), lhsT=onesb[:], rhs=SQ[:], start=True, stop=True), reads=["SQ", "consts"], writes=[pk(7)])
                    p.op("act", lambda e: e.activation(out=Rr[:], in_=bank(7), func=AF.Ln, scale=1.0 / 128, bias=EPS), reads=[pk(7)], writes=["Rr"])
                    p.op("act", lambda e: e.activation(out=Rr[:], in_=Rr[:], func=AF.Exp, scale=-0.5), reads=["Rr"], writes=["Rr"])

                def f3(gl=gl, s0=s0, otb=otb, pot=pot, obt=obt):
                    p.op("dve", lambda e: e.scalar_tensor_tensor(out=ON[:], in0=pot, scalar=onormP[:, br:br + 1], in1=Rr[:], op0=ALU.mult, op1=ALU.mult),
                         reads=[pk(otb), "Rr", "smallin"], writes=["ON"])
                    p.op("pool", lambda e: e.tensor_tensor(out=obt[:], in0=ON[:], in1=GT[:, gl * 512:(gl + 1) * 512], op=ALU.mult),
                         reads=["ON", ("GT", hp, s0)], writes=[("OBt", gl % 2)])
                    p.dma("sp", [(ob_d[hh, :, gl * 512:(gl + 1) * 512], obt[:])], reads=[("OBt", gl % 2)], writes=[("obd", hh, gl)])
                chains.append([f0, f1, f2, f3])
            nel = 4
            for step in range(nel + len(chains) - 1):
                for gl in range(len(chains)):
                    k = step - gl
                    if 0 <= k < nel:
                        chains[gl][k]()
                yield

        def co_run(ga, gb, stop_a, stop_b):
            act = [ga is not None, gb is not None]
            gens = [ga, gb]
            stops = [stop_a, stop_b]
            while act[0] or act[1]:
                for k in range(2):
                    if not act[k]:
                        continue
                    try:
                        v = next(gens[k])
                    except StopIteration:
                        act[k] = False
                        continue
                    if v == "SPLIT" and stops[k]:
                        act[k] = False

        _lo_s2 = ar.lo
        ar.lo = m_persist
        OB = ar.alloc("OB", [128, 8, TL], BF16)
        w3 = []
        for i in range(2):
            w3.append(dict(gh=ar.alloc("wgh%d" % i, [128, 8, 128], BF16), gg=ar.alloc("wgg%d" % i, [128, 8, 128], BF16),
                           bh=ar.alloc("wbh%d" % i, [128, 4, 128], BF16), bg=ar.alloc("wbg%d" % i, [128, 4, 128], BF16)))
            if i == 0:
                assert ar.lo <= m_persist + 5 * 2048 + 512 + 2048 + 2 * 9216 + 2 * 4608 + 4096
        lo_after_w3 = ar.lo
        ar.lo = _lo_s2

        def load_w3(nn, deps=()):
            if nn >= 8:
                return
            w = w3[nn % 2]
            p.dma("pool", [(w["gh"][:], wcol(4640 + nn * 128)), (w["gg"][:], wcol(5664 + nn * 128)),
                           (w["bh"][:], wbrh_d[:, nn * 128:(nn + 1) * 128].rearrange("(h p) n -> p h n", p=128)),
                           (w["bg"][:], wbrg_d[:, nn * 128:(nn + 1) * 128].rearrange("(h p) n -> p h n", p=128))], writes=[("w3", nn % 2)], deps=deps)

        def load_ob(h_, deps=()):
            for gl_ in range(4):
                p.dma("sp", [(OB[:, h_, gl_ * 512:(gl_ + 1) * 512], ob_d[h_, :, gl_ * 512:(gl_ + 1) * 512])],
                      reads=[("obd", h_, gl_)], writes=[("OBl", h_, gl_)], deps=deps)

        g2 = None
        for hh in range(nheads):
            g1 = head_p1(hh)
            co_run(g1, g2, True, False)
            g2 = head_p2(hh)
            co_run(g1, g2, False, True)
        snap = [("s_" + e_, p.cnt[e_]) for e_ in ENGS if p.cnt[e_] > 0] + \
               [("r%d" % i_, p.ring_total[i_]) for i_ in range(NRING) if p.ring_total[i_] > 0]
        for h_ in range(nheads - 1):
            load_ob(h_, deps=snap)
        load_w3(0, deps=snap)
        co_run(None, g2, False, False)
        if debug and debug.get("_stop") == "s2":
            p.finish()
            return nc
        p.fence()
        ar.lo = m_persist

        load_ob(nheads - 1)
        ar.lo = lo_after_w3
        grow = [ar.alloc_top("grow%d" % i, [128, D], F32) for i in range(2)]
        hi_grow = ar.hi
        MT = ar.alloc_top("MT", [128, 8, TL], BF16)
        wout = ar.alloc_top("wout", [128, 8, D], BF16)
        S12 = [ar.alloc("S12_%d" % i, [128, 512], F32) for i in range(2)]
        M12 = [ar.alloc("M12_%d" % i, [128, 512], F32) for i in range(2)]
        wmr = ar.alloc("wmr", [128, 8, 1024], BF16)
        brow = ar.alloc("brow", [128, D], F32)
        nrow = ar.alloc("nrow", [128, D], F32)
        woutk = [("wout", kc) for kc in range(8)]
        growk = [[("grow", gi_, 0), ("grow", gi_, 1)] for gi_ in range(2)]

        def load_wmr(j):
            p.dma("pool", [(wmr[:], wmod_d[:, j * 1024:(j + 1) * 1024].rearrange("(c p) n -> p c n", p=128))], writes=["wmr"])

        def gate_row(gi_, j, npost_d):
            p.dma("sp", [(brow[:], bmodR_d[:, j * 1024:(j + 1) * 1024].partition_broadcast(128)),
                         (nrow[:], npost_d.partition_broadcast(128))], writes=["brow"])
            for half in range(2):
                psr = bank(half)

                def g(e, psr=psr, half=half):
                    for kc in range(8):
                        e.matmul(psr, lhsT=csrep[:, kc, :], rhs=wmr[:, kc, half * 512:(half + 1) * 512], start=(kc == 0), stop=(kc == 7))
                p.op("pe", g, reads=["wmr"], writes=[pk(half)])
                p.op("dve", lambda e, psr=psr, half=half: e.tensor_tensor(out=grow[gi_][:, half * 512:(half + 1) * 512], in0=psr,
                                                                         in1=brow[:, half * 512:(half + 1) * 512], op=ALU.add),
                     reads=[pk(half), "brow"], writes=[("grow", gi_, half)])
            p.op("pool", lambda e: e.tensor_tensor(out=grow[gi_][:], in0=grow[gi_][:], in1=nrow[:], op=ALU.mult),
                 reads=[("grow", gi_, 0), ("grow", gi_, 1), "brow"], writes=[("grow", gi_, 0), ("grow", gi_, 1)])

        def side_work(nn):
            if nn == 0:
                for kc in range(8):
                    p.dma("pool", [(wout[:, kc, :], wout_d[kc * 128:(kc + 1) * 128, :])], writes=[("wout", kc)])
                mod_pp(3, wmr, "wmr")
                load_wmr(4)
            elif nn == 1:
                mod_pp(4, wmr, "wmr")
                load_wmr(2)
                p.op("dve", lambda e: e.scalar_tensor_tensor(out=vec["sc2x"][:], in0=modP[:, 4, :, 0], scalar=1.0, in1=npre2[:], op0=ALU.add, op1=ALU.mult),
                     reads=[("modP", 4), "smallin"], writes=["sc2x"])
                p.op("dve", lambda e: e.tensor_copy(out=vec["sh2x"][:], in_=modP[:, 3, :, 0]), reads=[("modP", 3)], writes=["sh2x"])
            elif nn == 2:
                gate_row(0, 2, npost1_d)
                load_wmr(5)
            elif nn == 3:
                gate_row(1, 5, npost2_d)

        load_w3(1)
        load_wmr(3)
        itn = 0
        for nn in range(8):
            w = w3[nn % 2]
            wk = ("w3", nn % 2)
            for gl in range(4):
                s = gl * 512
                for half in range(2):
                    bset = (itn % 3) * 2
                    itn += 1
                    pg_, pb_ = bank(bset), bank(bset + 1)
                    wgt, wbr = (w["gh"], w["bh"]) if half == 0 else (w["gg"], w["bg"])

                    def g(e, pg_=pg_, pb_=pb_, wgt=wgt, wbr=wbr, s=s, half=half):
                        for kc in range(8):
                            e.matmul(pg_, lhsT=wgt[:, kc, :], rhs=hT[:, kc, TC + s:TC + s + 512], start=(kc == 0), stop=(kc == 7))
                        for h in range(4):
                            e.matmul(pb_, lhsT=wbr[:, h, :], rhs=OB[:, half * 4 + h, s:s + 512], start=(h == 0), stop=(h == 3))
                    p.op("pe", g, reads=[wk] + [("OBl", half * 4 + h_, gl) for h_ in range(4)], writes=[pk(bset), pk(bset + 1)])
                    p.op("act", lambda e, pg_=pg_, half=half: e.activation(out=S12[half][:], in_=pg_, func=AF.Sigmoid),
                         reads=[pk(bset)], writes=[("S12", half)])
                    p.op("dve", lambda e, pb_=pb_, half=half: e.tensor_tensor(out=M12[half][:], in0=pb_, in1=S12[half][:], op=ALU.mult),
                         reads=[pk(bset + 1), ("S12", half)], writes=[("M12", half)])
                p.op("pool", lambda e, nn=nn, s=s: e.tensor_tensor(out=MT[:, nn, s:s + 512], in0=M12[0][:], in1=M12[1][:], op=ALU.add),
                     reads=[("M12", 0), ("M12", 1)], writes=[("MT", nn, gl)])
            load_w3(nn + 2)
            side_work(nn)
        dbg("MT", MT[:, 0, :], [], [128, TL])
        if debug and debug.get("_stop") == "3a":
            p.finish()
            return nc
        p.fence()
        ar.lo = m_small

        h2T = ar.alloc("h2T", [128, 8, TL], BF16)
        wdn = ar.alloc("wdn", [128, FC, D], BF16)
        m3 = ar.lo
        for fc in range(FC):
            p.dma("pool", [(wdn[:, fc, :], wfd_d[fc * 128:(fc + 1) * 128, :])], writes=[("wdn", fc)])
        NB3 = 3
        xb = [ar.alloc("xb3_%d" % i, [128, D], F32) for i in range(NB3)]
        T1 = [ar.alloc("T1_%d" % i, [128, D], F32) for i in range(NB3)]
        Z1 = [ar.alloc("Z1_%d" % i, [128, D], F32) for i in range(NB3)]
        XN = [ar.alloc("XN_%d" % i, [128, D], BF16) for i in range(NB3)]
        junk = ar.alloc("junk3", [128, D], BF16)
        ss1 = ar.alloc("ss1", [128, 16], F32)
        r1 = ar.alloc("r1", [128, 16], F32)
        ss2 = ar.alloc("ss2", [128, 16], F32)
        r2 = ar.alloc("r2", [128, 16], F32)
        def tile_a(i):
            yp = PSB[i % 3]
            ypk = [pk(2 * (i % 3)), pk(2 * (i % 3) + 1)]

            def g(e, i=i, yp=yp):
                last = None
                for half in range(2):
                    for kc in range(8):
                        last = e.matmul(yp[:, half * 512:(half + 1) * 512], lhsT=MT[:, kc, i * 128:(i + 1) * 128],
                                        rhs=wout[:, kc, half * 512:(half + 1) * 512], start=(kc == 0), stop=(kc == 7))
                return last
            p.op("pe", g, reads=woutk, writes=ypk)
            p.op("act", lambda e, i=i, yp=yp: e.activation(out=junk[:], in_=yp[:, :], func=AF.Square, accum_out=ss1[:, i:i + 1]),
                 reads=ypk, writes=["junk", ("ss1", i)])
            p.op("act", lambda e, i=i: e.activation(out=r1[:, i:i + 1], in_=ss1[:, i:i + 1], func=AF.Ln, scale=1.0 / D, bias=EPS),
                 reads=[("ss1", i)], writes=[("r1", i)])
            p.op("act", lambda e, i=i: e.activation(out=r1[:, i:i + 1], in_=r1[:, i:i + 1], func=AF.Exp, scale=-0.5),
                 reads=[("r1", i)], writes=[("r1", i)])
            p.op("dve", lambda e, i=i, yp=yp: e.scalar_tensor_tensor(out=T1[i % NB3][:], in0=yp[:, :], scalar=r1[:, i:i + 1], in1=grow[0][:],
                                                                   op0=ALU.mult, op1=ALU.mult),
                 reads=ypk + [("r1", i)] + growk[0], writes=[("T1", i % NB3)])

        def tile_a2(i):
            p.op("pool", lambda e, i=i: e.tensor_tensor(out=Z1[i % NB3][:], in0=T1[i % NB3][:], in1=xb[i % NB3][:], op=ALU.add),
                 reads=[("T1", i % NB3), ("xb", i % NB3)], writes=[("Z1", i % NB3)])
            p.dma("pool", [(z1_d[i * 128:(i + 1) * 128, :], Z1[i % NB3][:])], reads=[("Z1", i % NB3)], writes=[("z1d", i)])
            norm_transpose(i, None, None, Z1[i % NB3], ("Z1", i % NB3), XN[i % NB3], ("XN", i % NB3), vec["sc2x"], vec["sh2x"], ["sc2x", "sh2x"],
                           h2T, "h2T", i * 128, ss2, r2, load=False, pair=3, phase="norm")

        def tile_b(i):
            norm_transpose(i, None, None, Z1[i % NB3], ("Z1", i % NB3), XN[i % NB3], ("XN", i % NB3), vec["sc2x"], vec["sh2x"], ["sc2x", "sh2x"],
                           h2T, "h2T", i * 128, ss2, r2, load=False, pair=3, phase="tr")

        for i_ in range(2):
            p.dma("sp", [(xb[i_][:], x_d[i_ * 128:(i_ + 1) * 128, :])], writes=[("xb", i_)])
        for i in range(18):
            if i < 16:
                tile_a(i)
            if 1 <= i < 17:
                tile_a2(i - 1)
            if i + 2 < 16:
                p.dma("sp", [(xb[(i + 2) % NB3][:], x_d[(i + 2) * 128:(i + 3) * 128, :])], writes=[("xb", (i + 2) % NB3)])
            if i >= 2:
                tile_b(i - 2)
        dbg("h2T", h2T[:, 0, :], [], [128, TL])
        if debug and debug.get("_stop") == "3b":
            p.finish()
            return nc
        p.fence()
        ar.lo = m3
        ar.hi = hi_grow

        HID = ar.alloc("HID", [128, FC, 1024], BF16)
        wfg = [ar.alloc("wfg%d" % i, [128, 8, 128], BF16) for i in range(3)]
        wfu = [ar.alloc("wfu%d" % i, [128, 8, 128], BF16) for i in range(3)]
        SG = [ar.alloc("SG%d" % i, [128, 512], F32) for i in range(2)]
        T2 = [ar.alloc("T2_%d" % i, [128, D], F32) for i in range(2)]
        zt = [ar.alloc("zt%d" % i, [128, D], F32) for i in range(3)]
        OT_ = [ar.alloc("OT%d" % i, [128, D], F32) for i in range(2)]
        junk = ar.alloc("junk4", [128, D], BF16)
        ss3 = ar.alloc("ss3", [128, 16], F32)
        r3 = ar.alloc("r3", [128, 16], F32)
        wdnk = [("wdn", fc) for fc in range(FC)]
        itn = 0
        for grp in range(2):
            for fc in range(FC):
                wi = fc % 3
                p.dma("pool", [(wfg[wi][:], wfg_d[:, fc * 128:(fc + 1) * 128].rearrange("(c p) n -> p c n", p=128)),
                               (wfu[wi][:], wfu_d[:, fc * 128:(fc + 1) * 128].rearrange("(c p) n -> p c n", p=128))], writes=[("wf", wi)])
                for sub in range(2):
                    t0 = grp * 1024 + sub * 512
                    bset = (itn % 3) * 2
                    itn += 1
                    pg_, pu_ = bank(bset), bank(bset + 1)

                    def g(e, pg_=pg_, pu_=pu_, wi=wi, t0=t0):
                        last = None
                        for kc in range(8):
                            e.matmul(pg_, lhsT=wfg[wi][:, kc, :], rhs=h2T[:, kc, t0:t0 + 512], start=(kc == 0), stop=(kc == 7))
                        for kc in range(8):
                            last = e.matmul(pu_, lhsT=wfu[wi][:, kc, :], rhs=h2T[:, kc, t0:t0 + 512], start=(kc == 0), stop=(kc == 7))
                        return last
                    p.op("pe", g, reads=[("wf", wi)], writes=[pk(bset), pk(bset + 1)])
                    p.op("act", lambda e, pg_=pg_, sub=sub: e.activation(out=SG[sub][:], in_=pg_, func=AF.Silu), reads=[pk(bset)], writes=[("SG", sub)])
                    p.op("dve", lambda e, pu_=pu_, sub=sub, fc=fc: e.tensor_tensor(out=HID[:, fc, sub * 512:(sub + 1) * 512], in0=pu_, in1=SG[sub][:], op=ALU.mult),
                         reads=[pk(bset + 1), ("SG", sub)], writes=[("HID", fc, sub)])
            for i_ in range(grp * 8, grp * 8 + 2):
                p.dma("sp", [(zt[i_ % 3][:], z1_d[i_ * 128:(i_ + 1) * 128, :])], writes=[("zt", i_ % 3)])
            for tt in range(8):
                i = grp * 8 + tt
                yp = PSB[i % 2]
                ypk = [pk(2 * (i % 2)), pk(2 * (i % 2) + 1)]

                def g(e, tt=tt, yp=yp):
                    last = None
                    for half in range(2):
                        for fc in range(FC):
                            last = e.matmul(yp[:, half * 512:(half + 1) * 512], lhsT=HID[:, fc, tt * 128:(tt + 1) * 128],
                                            rhs=wdn[:, fc, half * 512:(half + 1) * 512], start=(fc == 0), stop=(fc == FC - 1))
                    return last
                p.op("pe", g, reads=wdnk + [("HID", fc, tt // 4) for fc in range(FC)], writes=ypk)
                p.op("act", lambda e, i=i, yp=yp: e.activation(out=junk[:], in_=yp[:, :], func=AF.Square, accum_out=ss3[:, i:i + 1]),
                     reads=ypk, writes=["junk", ("ss3", i)])
                p.op("act", lambda e, i=i: e.activation(out=r3[:, i:i + 1], in_=ss3[:, i:i + 1], func=AF.Ln, scale=1.0 / D, bias=EPS),
                     reads=[("ss3", i)], writes=[("r3", i)])
                p.op("act", lambda e, i=i: e.activation(out=r3[:, i:i + 1], in_=r3[:, i:i + 1], func=AF.Exp, scale=-0.5),
                     reads=[("r3", i)], writes=[("r3", i)])
                p.op("dve", lambda e, i=i, yp=yp: e.scalar_tensor_tensor(out=T2[i % 2][:], in0=yp[:, :], scalar=r3[:, i:i + 1], in1=grow[1][:],
                                                                       op0=ALU.mult, op1=ALU.mult),
                     reads=ypk + [("r3", i)], writes=[("T2", i % 2)])
                if tt + 2 < 8:
                    p.dma("sp", [(zt[(i + 2) % 3][:], z1_d[(i + 2) * 128:(i + 3) * 128, :])], writes=[("zt", (i + 2) % 3)])
                p.op("pool", lambda e, i=i: e.tensor_tensor(out=OT_[i % 2][:], in0=T2[i % 2][:], in1=zt[i % 3][:], op=ALU.add),
                     reads=[("T2", i % 2), ("zt", i % 3)], writes=[("OTo", i % 2)])
                p.dma("pool", [(out_d[i * 128:(i + 1) * 128, :], OT_[i % 2][:])], reads=[("OTo", i % 2)], writes=[("outd", i)])
        p.finish()
    return nc


_NC_CACHE = {}


def _host_inputs(inputs):
    f32 = np.float32
    g = lambda k: np.asarray(inputs[k], dtype=f32)
    x, c, ctx, c_ctx = g("x"), g("c"), g("ctx"), g("c_ctx")
    col = lambda v: np.ascontiguousarray(v.reshape(-1, 128).T)
    shared = {
        "cctx": col(c_ctx),
        "w_mod": np.ascontiguousarray(g("w_mod")[0]),
        "bmodP": col(g("b_mod")[0]),
        "bmodR": np.ascontiguousarray(g("b_mod")[0].reshape(1, -1)),
        "npre1P": col(g("norm_pre1")[0]),
        "npre2P": col(g("norm_pre2")[0]),
        "npost1R": np.ascontiguousarray(g("norm_post1")[0].reshape(1, -1)),
        "npost2R": np.ascontiguousarray(g("norm_post2")[0].reshape(1, -1)),
        "w_in": np.ascontiguousarray(g("w_in")[0]),
        "lbP": np.ascontiguousarray(g("hg_lb").reshape(2, 2, 4, 128).transpose(3, 0, 1, 2).reshape(128, 16)),
        "onormP": np.ascontiguousarray(np.stack([g("hg_onorm")[0], g("gla_onorm")[0]], axis=1)),
        "bgkP": np.ascontiguousarray(g("gla_b_gk")[0].reshape(2, 4, 128).transpose(2, 0, 1).reshape(128, 8)),
        "wgk": np.ascontiguousarray(g("gla_w_gk")[0]),
        "w_br_hg": np.ascontiguousarray(g("w_br_hg")[0]),
        "w_br_gla": np.ascontiguousarray(g("w_br_gla")[0]),
        "w_out": np.ascontiguousarray(g("w_out")[0]),
        "w_ff_gate": np.ascontiguousarray(g("w_ff_gate")[0]),
        "w_ff_up": np.ascontiguousarray(g("w_ff_up")[0]),
        "w_ff_down": np.ascontiguousarray(g("w_ff_down")[0]),
    }
    s = np.arange(128)[:, None]
    t = np.arange(128)[None, :]
    same = (s // 64) == (t // 64)
    shared["ident"] = np.eye(128, dtype=f32)
    shared["ones"] = np.ones((128, 128), f32)
    shared["maskF4"] = np.tile((same & (s <= t)).astype(f32), (1, 4))
    shared["maskB4"] = np.tile((same & (s >= t)).astype(f32), (1, 4))
    cmr = np.ones((128, T), f32)
    cmr[:, ::64] = 0.0
    shared["cm"] = cmr
    maps = []
    for b in range(NCORES):
        m = dict(shared)
        m["x"] = np.ascontiguousarray(x[b])
        m["ctxx"] = np.ascontiguousarray(ctx[b])
        m["cx"] = col(c[b])
        maps.append(m)
    return maps


def kernel(**inputs):
    if "nc" not in _NC_CACHE:
        _NC_CACHE["nc"] = build_program()
    nc = _NC_CACHE["nc"]
    maps = _host_inputs(inputs)
    res = run_bass_kernel_spmd(nc, maps, core_ids=list(range(NCORES)))
    out = np.stack([np.asarray(res.results[b]["out"], dtype=np.float32) for b in range(NCORES)], axis=0)
    return out
```
